# Optimizing a Trainium2 kernel written in Bass

```python
import jax, jax.numpy as jnp
from jax import lax
import numpy as np

D_MODEL = 2048
BATCH = 16
SEQ = 2048
DEPTH = 4
DEC_BATCH = 1
DEC_SEQ = 8192
PAST_LEN = 128

HG_HEADS = 8
HG_KDIM = 128
HG_VDIM = 128
HG_KW = HG_HEADS * HG_KDIM
HG_WIDTH = HG_HEADS * HG_VDIM
HG_CHUNK = 64
HEAD_DIM = 128
ATT_HEADS = 8
ATT_KV_HEADS = 2
ATT_WIDTH = ATT_HEADS * HEAD_DIM
KV_WIDTH = ATT_KV_HEADS * HEAD_DIM
WINDOW = 128
ROPE_THETA = 10000.0
D_FF = ((-(-8 * D_MODEL // 3) + 255) // 256) * 256
DEEPNORM_ALPHA = (2 * DEPTH) ** 0.25
DEEPNORM_BETA = (8 * DEPTH) ** -0.25
LN_EPS = 1e-5
RMS_EPS = 1e-6
IN_SPLITS = (HG_KW, HG_KW, HG_KW, HG_WIDTH, HG_WIDTH, ATT_WIDTH, KV_WIDTH, KV_WIDTH, D_MODEL, D_MODEL)
IN_WIDTH = sum(IN_SPLITS)

kernel_name = 'hybrid_hgrn2_swa_deepnorm_encoder'


def _split_points():
    pts, acc = [], 0
    for w in IN_SPLITS[:-1]:
        acc += w
        pts.append(acc)
    return pts


def layer_norm(x, g, b):
    xf = x.astype(jnp.float32)
    mu = jnp.mean(xf, axis=-1, keepdims=True)
    var = jnp.mean(jnp.square(xf - mu), axis=-1, keepdims=True)
    y = (xf - mu) * lax.rsqrt(var + LN_EPS) * g.astype(jnp.float32) + b.astype(jnp.float32)
    return y.astype(x.dtype)


def rope_tables(L):
    inv = 1.0 / (ROPE_THETA ** (jnp.arange(0, HEAD_DIM, 2, dtype=jnp.float32) / HEAD_DIM))
    ang = jnp.arange(L, dtype=jnp.float32)[:, None] * inv[None, :]
    return jnp.cos(ang), jnp.sin(ang)


def apply_rope(x, cos, sin):
    x1, x2 = jnp.split(x.astype(jnp.float32), 2, axis=-1)
    c = cos[None, :, None, :]
    s = sin[None, :, None, :]
    return jnp.concatenate([x1 * c - x2 * s, x2 * c + x1 * s], axis=-1).astype(x.dtype)


def hgrn2_scan(q, k, v, log_f):
    B, L, H, dk = q.shape
    dv = v.shape[-1]
    C = HG_CHUNK
    N = L // C

    def chunks(t):
        return t.reshape(B, N, C, H, t.shape[-1]).transpose(1, 0, 3, 2, 4)

    causal = jnp.tril(jnp.ones((C, C), dtype=bool))[None, None, :, :, None]

    def step(S, inp):
        qc, kc, vc, gc = inp
        b = jnp.cumsum(gc, axis=2)
        b_end = b[:, :, -1:, :]
        o_inter = jnp.einsum('bhck,bhkv->bhcv', qc * jnp.exp(b), S)
        rel = jnp.where(causal, b[:, :, :, None, :] - b[:, :, None, :, :], -jnp.inf)
        scores = jnp.einsum('bhtk,bhsk,bhtsk->bhts', qc, kc, jnp.exp(rel))
        o_intra = jnp.einsum('bhts,bhsv->bhtv', scores, vc)
        S_new = jnp.exp(b_end[:, :, 0, :])[..., None] * S + jnp.einsum(
            'bhck,bhcv->bhkv', kc * jnp.exp(b_end - b), vc)
        return S_new, o_inter + o_intra

    S0 = jnp.zeros((B, H, dk, dv), jnp.float32)
    _, o = lax.scan(step, S0, (chunks(q), chunks(k), chunks(v), chunks(log_f)))
    return o.transpose(1, 0, 3, 2, 4).reshape(B, L, H, dv)


def hgrn2_branch(hq, hf_fwd, hf_bwd, hi, hg, lower_bound, norm_g):
    B, L, _ = hq.shape
    f32 = jnp.float32
    q = jax.nn.silu(hq.astype(f32)).reshape(B, L, HG_HEADS, HG_KDIM)
    v = hi.astype(f32).reshape(B, L, HG_HEADS, HG_VDIM)
    lb_fwd, lb_bwd = jnp.split(lower_bound, 2)

    def gates(z, lb):
        f = lb.reshape(HG_HEADS, HG_KDIM) + (1.0 - lb.reshape(HG_HEADS, HG_KDIM)) * jax.nn.sigmoid(
            z.astype(f32).reshape(B, L, HG_HEADS, HG_KDIM))
        return 1.0 - f, jnp.log(f)

    k_f, g_f = gates(hf_fwd, lb_fwd)
    k_b, g_b = gates(hf_bwd, lb_bwd)
    o_fwd = hgrn2_scan(q, k_f, v, g_f)
    flip = lambda t: jnp.flip(t, axis=1)
    o_bwd = flip(hgrn2_scan(flip(q), flip(k_b), flip(v), flip(g_b)))
    o = o_fwd + o_bwd
    o = o * lax.rsqrt(jnp.mean(jnp.square(o), axis=-1, keepdims=True) + RMS_EPS) * norm_g.astype(f32)
    o = o.reshape(B, L, HG_WIDTH) * jax.nn.silu(hg.astype(f32))
    return o.astype(hq.dtype)


def window_attention_branch(aq, ak, av, sink, cos, sin):
    B, L, _ = aq.shape
    W = WINDOW
    N = L // W
    G = ATT_HEADS // ATT_KV_HEADS
    q = apply_rope(aq.reshape(B, L, ATT_HEADS, HEAD_DIM), cos, sin)
    k = apply_rope(ak.reshape(B, L, ATT_KV_HEADS, HEAD_DIM), cos, sin)
    v = av.reshape(B, L, ATT_KV_HEADS, HEAD_DIM)
    qb = q.reshape(B, N, W, ATT_KV_HEADS, G, HEAD_DIM)

    def neighbours(t):
        tp = jnp.pad(t, ((0, 0), (W, W), (0, 0), (0, 0))).reshape(B, N + 2, W, ATT_KV_HEADS, HEAD_DIM)
        return jnp.concatenate([tp[:, :-2], tp[:, 1:-1], tp[:, 2:]], axis=2)

    kb = neighbours(k)
    vb = neighbours(v)
    s = jnp.einsum('bnqhgd,bnkhd->bnhgqk', qb, kb).astype(jnp.float32) * (HEAD_DIM ** -0.5)
    i = jnp.arange(W)[:, None]
    j = jnp.arange(3 * W)[None, :]
    n = jnp.arange(N)[:, None, None]
    kpos = (n - 1) * W + j[None]
    valid = (jnp.abs(j - W - i)[None] <= WINDOW) & (kpos >= 0) & (kpos < L)
    s = jnp.where(valid[None, :, None, None], s, -jnp.inf)
    sink_logit = jnp.broadcast_to(
        sink.astype(jnp.float32).reshape(1, 1, ATT_KV_HEADS, G, 1, 1), s.shape[:-1] + (1,))
    p = jax.nn.softmax(jnp.concatenate([s, sink_logit], axis=-1), axis=-1)[..., :-1]
    o = jnp.einsum('bnhgqk,bnkhd->bnqhgd', p.astype(vb.dtype), vb)
    return o.reshape(B, L, ATT_WIDTH)


def mixer(h, w_in, lower_bound, hg_norm_g, sink, w_branch_a, w_branch_b, w_out, cos, sin):
    proj = h @ w_in
    hq, hf_f, hf_b, hi, hg, aq, ak, av, gate_a, gate_b = jnp.split(proj, _split_points(), axis=-1)
    a = hgrn2_branch(hq, hf_f, hf_b, hi, hg, lower_bound, hg_norm_g)
    b = window_attention_branch(aq, ak, av, sink, cos, sin)
    merged = jax.nn.sigmoid(gate_a) * (a @ w_branch_a) + jax.nn.sigmoid(gate_b) * (b @ w_branch_b)
    return merged @ w_out


def swiglu(h, w_ffn_in, w_ffn_out):
    gate, up = jnp.split(h @ w_ffn_in, 2, axis=-1)
    return (jax.nn.silu(gate) * up) @ w_ffn_out


def encoder_trunk(x, ln_in_g, ln_in_b, w_in, lb_logits, hg_norm_g, attn_sink, w_branch_a, w_branch_b,
                  w_out, ln1_g, ln1_b, w_ffn_in, w_ffn_out, ln2_g, ln2_b):
    L = x.shape[1]
    cos, sin = rope_tables(L)
    p = jax.nn.softmax(lb_logits.astype(jnp.float32), axis=0)
    lower_bounds = jnp.cumsum(p, axis=0) - p[0:1]
    h = layer_norm(x, ln_in_g, ln_in_b)
    for l in range(DEPTH):
        mix = mixer(h, w_in[l], lower_bounds[l], hg_norm_g[l], attn_sink[l],
                    w_branch_a[l], w_branch_b[l], w_out[l], cos, sin)
        h = layer_norm(DEEPNORM_ALPHA * h + mix, ln1_g[l], ln1_b[l])
        h = layer_norm(DEEPNORM_ALPHA * h + swiglu(h, w_ffn_in[l], w_ffn_out[l]), ln2_g[l], ln2_b[l])
    return h


def setup_inputs(seed: int = 0) -> dict:
    key = jax.random.key(seed)
    ks = jax.random.split(key, 18)
    nrm = lambda k, shape, scale: jax.random.normal(k, shape, jnp.float32) * scale
    return {
        'x_prompt': nrm(ks[0], (BATCH, SEQ, D_MODEL), 1.0),
        'x_sample': nrm(ks[1], (DEC_BATCH, DEC_SEQ, D_MODEL), 1.0),
        'ln_in_g': 1.0 + nrm(ks[2], (D_MODEL,), 0.02),
        'ln_in_b': nrm(ks[3], (D_MODEL,), 0.02),
        'w_in': nrm(ks[4], (DEPTH, D_MODEL, IN_WIDTH), D_MODEL ** -0.5),
        'lb_logits': nrm(ks[5], (DEPTH, 2 * HG_KW), 0.1),
        'hg_norm_g': 1.0 + nrm(ks[6], (DEPTH, HG_VDIM), 0.02),
        'attn_sink': nrm(ks[7], (DEPTH, ATT_HEADS), 0.5),
        'w_branch_a': nrm(ks[8], (DEPTH, HG_WIDTH, D_MODEL), HG_WIDTH ** -0.5),
        'w_branch_b': nrm(ks[9], (DEPTH, ATT_WIDTH, D_MODEL), ATT_WIDTH ** -0.5),
        'w_out': nrm(ks[10], (DEPTH, D_MODEL, D_MODEL), DEEPNORM_BETA * D_MODEL ** -0.5),
        'ln1_g': 1.0 + nrm(ks[11], (DEPTH, D_MODEL), 0.02),
        'ln1_b': nrm(ks[12], (DEPTH, D_MODEL), 0.02),
        'w_ffn_in': nrm(ks[13], (DEPTH, D_MODEL, 2 * D_FF), D_MODEL ** -0.5),
        'w_ffn_out': nrm(ks[14], (DEPTH, D_FF, D_MODEL), DEEPNORM_BETA * D_FF ** -0.5),
        'ln2_g': 1.0 + nrm(ks[15], (DEPTH, D_MODEL), 0.02),
        'ln2_b': nrm(ks[16], (DEPTH, D_MODEL), 0.02),
    }


def reference(x_prompt, x_sample, ln_in_g, ln_in_b, w_in, lb_logits, hg_norm_g, attn_sink, w_branch_a,
              w_branch_b, w_out, ln1_g, ln1_b, w_ffn_in, w_ffn_out, ln2_g, ln2_b):
    y_prompt = encoder_trunk(x_prompt, ln_in_g, ln_in_b, w_in, lb_logits, hg_norm_g, attn_sink, w_branch_a,
                             w_branch_b, w_out, ln1_g, ln1_b, w_ffn_in, w_ffn_out, ln2_g, ln2_b)
    y_sample = encoder_trunk(x_sample, ln_in_g, ln_in_b, w_in, lb_logits, hg_norm_g, attn_sink, w_branch_a,
                             w_branch_b, w_out, ln1_g, ln1_b, w_ffn_in, w_ffn_out, ln2_g, ln2_b)
    return (y_prompt, y_sample)
```

```python
import contextlib
import numpy as np
import ml_dtypes
import concourse.bass as bass
import concourse.mybir as mybir
from concourse.bass_utils import run_bass_kernel_spmd

F32 = mybir.dt.float32
BF16 = mybir.dt.bfloat16
AF = mybir.ActivationFunctionType
ALU = mybir.AluOpType

P = 128
D = 2048
KC = 16
UT = 2048
TT = 512
CH = 64
NCHK = UT // CH
DFF = 5632
FC = DFF // P
IN_W = 10752
OFF_Q, OFF_FF, OFF_FB, OFF_I, OFF_G = 0, 1024, 2048, 3072, 4096
OFF_AQ, OFF_AK, OFF_AV, OFF_GA, OFF_GB = 5120, 6144, 6400, 6656, 8704
DEPTH = 4
ALPHA = float((2 * DEPTH) ** 0.25)
LN_EPS = 1e-5
RMS_EPS = 1e-6
QSCALE = float(128 ** -0.5)
NEG = -30000.0
LMAX = 8192

COMPUTE = ('pe', 'act', 'dve', 'pool')
_UID = [0]


class Sched:
    def __init__(self, nc, es, n_dma_sems=12):
        self.nc = nc
        self.sem = {}
        for e in COMPUTE:
            self.sem[e] = es.enter_context(nc.semaphore("s_" + e))
        self.queues = ('sp', 'pool')
        self.dma_sems = {}
        for q in self.queues:
            self.dma_sems[q] = []
            for i in range(n_dma_sems):
                k = "d_%s_%d" % (q, i)
                self.sem[k] = es.enter_context(nc.semaphore(k))
                self.dma_sems[q].append(k)
        self.val = {k: 0 for k in self.sem}
        self.dma_rr = {q: 0 for q in self.queues}
        self.streams = {e: [] for e in ('pe', 'act', 'dve', 'pool', 'sp')}
        self.known = {e: {} for e in self.streams}
        self.res = {}
        self.n_inst = 0

    def _deps(self, reads, writes):
        deps = {}

        def add(tok):
            if tok is None:
                return
            k, v = tok
            if deps.get(k, 0) < v:
                deps[k] = v
        for r in reads:
            st = self.res.get(r)
            if st:
                add(st['w'])
        for w in writes:
            st = self.res.get(w)
            if st:
                add(st['w'])
                for k, v in st['r'].items():
                    add((k, v))
        return deps

    def _commit(self, tok, reads, writes):
        k, v = tok
        for r in reads:
            st = self.res.setdefault(r, {'w': None, 'r': {}})
            if st['r'].get(k, 0) < v:
                st['r'][k] = v
        for w in writes:
            self.res[w] = {'w': tok, 'r': {}}

    def _waits(self, stream, deps, skip_self=None):
        out = []
        kn = self.known[stream]
        for k, v in deps.items():
            if k == skip_self:
                continue
            if kn.get(k, 0) < v:
                kn[k] = v
                out.append((k, v))
        return out

    def op(self, eng, fn, reads=(), writes=()):
        deps = self._deps(reads, writes)
        waits = self._waits(eng, deps, skip_self='pe' if eng == 'pe' else None)
        self.val[eng] += 1
        tok = (eng, self.val[eng])
        self.streams[eng].append((waits, fn, tok, 1))
        self._commit(tok, reads, writes)
        return tok

    def dma(self, q, fn, reads=(), writes=()):
        deps = self._deps(reads, writes)
        sems = self.dma_sems[q]
        k = sems[self.dma_rr[q] % len(sems)]
        self.dma_rr[q] += 1
        if self.val[k] > 0:
            deps[k] = max(deps.get(k, 0), self.val[k])
        waits = self._waits(q, deps)
        self.val[k] += 16
        tok = (k, self.val[k])
        self.streams[q].append((waits, fn, tok, 16))
        self._commit(tok, reads, writes)
        return tok

    def flush(self, name=None):
        deps = {}
        for q in self.queues:
            for k in self.dma_sems[q]:
                if self.val[k] > 0:
                    deps[k] = self.val[k]
        waits = self._waits('sp', dict(deps))
        if waits:
            self.streams['sp'].append((waits, None, None, 0))
        streams = self.streams
        sem = self.sem
        cnt = [0]

        def replay(engine, lst):
            for waits, fn, tok, inc in lst:
                for k, v in waits:
                    engine.wait_ge(sem[k], v)
                    cnt[0] += 1
                if fn is not None:
                    ins = fn(engine)
                    ins.then_inc(sem[tok[0]], inc)
                    cnt[0] += 1

        _UID[0] += 1
        with self.nc.Block("%s_%d" % (name or "blk", _UID[0])) as block:
            @block.tensor
            def _(e):
                replay(e, streams['pe'])

            @block.scalar
            def _(e):
                replay(e, streams['act'])

            @block.vector
            def _(e):
                replay(e, streams['dve'])

            @block.gpsimd
            def _(e):
                replay(e, streams['pool'])

            @block.sync
            def _(e):
                replay(e, streams['sp'])
        self.n_inst += cnt[0]
        self.streams = {e: [] for e in streams}
        self.res = {}
        for e in self.known:
            for k in self.val:
                self.known[e][k] = self.val[k]


class Ring:
    def __init__(self, es, nc, name, shape, dt, n):
        _UID[0] += 1
        self.tiles = [es.enter_context(nc.sbuf_tensor("%s%d_%d" % (name, i, _UID[0]), shape, dt)) for i in range(n)]
        self.names = ["%s%d" % (name, i) for i in range(n)]
        self.i = 0

    def next(self):
        t, n = self.tiles[self.i % len(self.tiles)], self.names[self.i % len(self.tiles)]
        self.i += 1
        return t, n


class Builder:
    def __init__(self, seq_units, depth, debug=False, stop_after=None):
        self.debug = debug
        self.stop_after = stop_after
        self.seq_units = list(seq_units)
        self.depth = depth
        self.T = sum(seq_units) * UT
        self.lmax = max(seq_units) * UT
        nc = bass.Bass("TRN2", target_bir_lowering=False)
        self.nc = nc
        T, L = self.T, depth

        def din(name, shape, dt=F32):
            return nc.dram_tensor(name, shape, dt, kind="ExternalInput").ap()

        def dscr(name, shape, dt=F32):
            return nc.dram_tensor(name, shape, dt, kind="ExternalOutput" if debug else "Internal").ap()
        self.x = din("x", [T, D])
        self.ln_in_g = din("ln_in_g", [D])
        self.ln_in_b = din("ln_in_b", [D])
        self.w_in = din("w_in", [L, D, IN_W])
        self.lb_logits = din("lb_logits", [L, 2048])
        self.hg_norm_g = din("hg_norm_g", [L, P])
        self.attn_sink = din("attn_sink", [L, 8])
        self.w_branch_a = din("w_branch_a", [L, 1024, D])
        self.w_branch_b = din("w_branch_b", [L, 1024, D])
        self.w_out = din("w_out", [L, D, D])
        self.ln1_g = din("ln1_g", [L, D])
        self.ln1_b = din("ln1_b", [L, D])
        self.w_ffn_in = din("w_ffn_in", [L, D, 2 * DFF])
        self.w_ffn_out = din("w_ffn_out", [L, DFF, D])
        self.ln2_g = din("ln2_g", [L, D])
        self.ln2_b = din("ln2_b", [L, D])
        self.c_cos = din("c_cos", [P, LMAX])
        self.c_sin = din("c_sin", [P, LMAX])
        self.c_ident = din("c_ident", [P, P], BF16)
        self.c_hmf = din("c_hmf", [CH, CH])
        self.c_hmb = din("c_hmb", [CH, CH])
        self.c_amp = din("c_amp", [P, P], BF16)
        self.c_amn = din("c_amn", [P, P], BF16)
        self.y = nc.dram_tensor("y", [T, D], F32, kind="ExternalOutput").ap()
        self.H = dscr("H", [T, D])
        self.HT = [dscr("HT%d" % i, [P, KC, T], BF16) for i in range(2)]
        self.OB = dscr("OB", [8, P, self.lmax])
        self.AB = dscr("AB", [P, KC, UT], BF16)
        self.WB_in = dscr("WB_in", [L, IN_W // P, P, KC, P], BF16)
        self.WB_fi = dscr("WB_fi", [L, 2 * FC, P, KC, P], BF16)
        self.WB_a = dscr("WB_a", [L, KC, P, 8, P], BF16)
        self.WB_b = dscr("WB_b", [L, KC, P, 8, P], BF16)
        self.WB_o = dscr("WB_o", [L, D, D], BF16)
        self.WB_fo = dscr("WB_fo", [L, DFF, D], BF16)

    def tile(self, es, name, shape, dt=F32):
        _UID[0] += 1
        return es.enter_context(self.nc.sbuf_tensor("%s_%d" % (name, _UID[0]), shape, dt))

    def psum(self, es, name, shape, dt=F32):
        _UID[0] += 1
        return es.enter_context(self.nc.psum_tensor("%s_%d" % (name, _UID[0]), shape, dt))

    def build(self):
        nc = self.nc
        with contextlib.ExitStack() as g:
            self.S = Sched(nc, g)
            S = self.S
            L = self.depth
            self.ident = self.tile(g, "ident", [P, P], BF16)
            self.ones_bf = self.tile(g, "ones_bf", [P, P], BF16)
            self.ones_f = self.tile(g, "ones_f", [P, P], F32)
            self.hmf = self.tile(g, "hmf", [CH, CH], F32)
            self.hmb = self.tile(g, "hmb", [CH, CH], F32)
            self.amp = self.tile(g, "amp", [P, P], BF16)
            self.amn = self.tile(g, "amn", [P, P], BF16)
            self.eps_ln = self.tile(g, "eps_ln", [P, 1], F32)
            self.eps_rms = self.tile(g, "eps_rms", [P, 1], F32)
            self.LB = self.tile(g, "LB", [P, L, 16], F32)
            self.OML = self.tile(g, "OML", [P, L, 16], F32)
            self.LBM1 = self.tile(g, "LBM1", [P, L, 16], F32)
            self.NG = self.tile(g, "NG", [P, L], F32)
            self.ES = self.tile(g, "ES", [P, L * 8], F32)
            self.U = [self.tile(g, "U%d" % d, [P, 8, P], F32) for d in range(2)]
            self.Est = [self.tile(g, "Est%d" % d, [P, 8], F32) for d in range(2)]
            self.phase_setup()
            self.phase_preconvert()
            self.phase_ln_in()
            t0 = 0
            for l in range(L):
                hin, hout = self.HT[l % 2], self.HT[(l + 1) % 2]
                t0 = 0
                for nu in self.seq_units:
                    for u in reversed(range(nu)):
                        self.phase_sweep(l, hin, t0, nu, u, 1)
                    for u in range(nu):
                        self.phase_sweep(l, hin, t0, nu, u, 0)
                        self.phase_attn(l, hin, t0, nu, u)
                        self.phase_tail(l, hin, hout, t0, u, last=(l == L - 1))
                    t0 += nu * UT
        return nc

    def phase_setup(self):
        S, L = self.S, self.depth
        with contextlib.ExitStack() as es:
            lg = self.tile(es, "lg", [P, L, 16])
            ex = self.tile(es, "ex", [P, L, 16])
            mx = self.tile(es, "mx", [P, 16])
            sm = self.tile(es, "sm", [P, 16])
            S.dma('sp', lambda e: e.dma_start(out=self.ident[:], in_=self.c_ident), writes=['ident'])
            S.dma('sp', lambda e: e.dma_start(out=self.hmf[:], in_=self.c_hmf), writes=['hmf'])
            S.dma('sp', lambda e: e.dma_start(out=self.hmb[:], in_=self.c_hmb), writes=['hmb'])
            S.dma('sp', lambda e: e.dma_start(out=self.amp[:], in_=self.c_amp), writes=['amp'])
            S.dma('sp', lambda e: e.dma_start(out=self.amn[:], in_=self.c_amn), writes=['amn'])
            S.dma('sp', lambda e: e.dma_start(out=lg[:], in_=self.lb_logits.rearrange("l (c p) -> p l c", p=P),
                                              allow_slow_non_contiguous=True), writes=['lg'])
            S.dma('sp', lambda e: e.dma_start(out=self.NG[:], in_=self.hg_norm_g.rearrange("l p -> p l"),
                                              allow_slow_non_contiguous=True), writes=['NG'])
            S.dma('sp', lambda e: e.dma_start(out=self.ES[:], in_=self.attn_sink.rearrange("l h -> (l h)").partition_broadcast(P)),
                  writes=['ES'])
            S.op('pool', lambda e: e.memset(self.ones_bf[:], 1.0), writes=['ones_bf'])
            S.op('pool', lambda e: e.memset(self.ones_f[:], 1.0), writes=['ones_f'])
            S.op('pool', lambda e: e.memset(self.eps_ln[:], LN_EPS), writes=['eps_ln'])
            S.op('pool', lambda e: e.memset(self.eps_rms[:], RMS_EPS), writes=['eps_rms'])
            S.op('act', lambda e: e.activation(out=self.ES[:], in_=self.ES[:], func=AF.Exp), reads=['ES'], writes=['ES'])
            S.op('dve', lambda e: e.tensor_copy(out=mx[:], in_=lg[:, 0, :]), reads=['lg'], writes=['mx'])
            for l in range(1, L):
                S.op('dve', lambda e, l=l: e.tensor_tensor(out=mx[:], in0=mx[:], in1=lg[:, l, :], op=ALU.max),
                     reads=['lg', 'mx'], writes=['mx'])
            for l in range(L):
                S.op('dve', lambda e, l=l: e.tensor_tensor(out=ex[:, l, :], in0=lg[:, l, :], in1=mx[:], op=ALU.subtract),
                     reads=['lg', 'mx'], writes=['ex'])
            S.op('act', lambda e: e.activation(out=ex[:], in_=ex[:], func=AF.Exp), reads=['ex'], writes=['ex'])
            S.op('dve', lambda e: e.tensor_copy(out=sm[:], in_=ex[:, 0, :]), reads=['ex'], writes=['sm'])
            for l in range(1, L):
                S.op('dve', lambda e, l=l: e.tensor_tensor(out=sm[:], in0=sm[:], in1=ex[:, l, :], op=ALU.add),
                     reads=['ex', 'sm'], writes=['sm'])
            S.op('dve', lambda e: e.reciprocal(out=sm[:], in_=sm[:]), reads=['sm'], writes=['sm'])
            S.op('pool', lambda e: e.memset(self.LB[:, 0, :], 0.0), writes=['LB'])
            for l in range(1, L):
                S.op('dve', lambda e, l=l: e.tensor_tensor(out=ex[:, l, :], in0=ex[:, l, :], in1=sm[:], op=ALU.mult),
                     reads=['ex', 'sm'], writes=['ex'])
                S.op('dve', lambda e, l=l: e.tensor_tensor(out=self.LB[:, l, :], in0=self.LB[:, l - 1, :], in1=ex[:, l, :], op=ALU.add),
                     reads=['ex', 'LB'], writes=['LB'])
            S.op('dve', lambda e: e.tensor_scalar(out=self.OML[:], in0=self.LB[:], scalar1=-1.0, scalar2=1.0, op0=ALU.mult, op1=ALU.add),
                 reads=['LB'], writes=['OML'])
            S.op('dve', lambda e: e.tensor_scalar(out=self.LBM1[:], in0=self.LB[:], scalar1=-1.0, scalar2=None, op0=ALU.add),
                 reads=['LB'], writes=['LBM1'])
            S.flush("setup")

    def ln_block(self, xap, xres, G, Bt, gres, sc, scres, eng_b='dve'):
        S = self.S
        st = sc[:, 0:24]
        ag = sc[:, 24:26]
        sd = sc[:, 26:27]
        rs = sc[:, 27:28]
        nm = sc[:, 28:29]
        for c in range(4):
            S.op('dve', lambda e, c=c: e.bn_stats(out=sc[:, c * 6:(c + 1) * 6], in_=xap[:, c * 512:(c + 1) * 512]),
                 reads=xres, writes=[scres + 's%d' % c])
        S.op('dve', lambda e: e.bn_aggr(out=ag, in_=st), reads=[scres + 's%d' % c for c in range(4)], writes=[scres + 'ag'])
        S.op('act', lambda e: e.activation(out=sd, in_=sc[:, 25:26], func=AF.Sqrt, bias=self.eps_ln[:, 0:1], scale=1.0),
             reads=[scres + 'ag'], writes=[scres + 'sd'])
        S.op('dve', lambda e: e.reciprocal(out=rs, in_=sd), reads=[scres + 'sd'], writes=[scres + 'rs'])
        S.op('dve', lambda e: e.scalar_tensor_tensor(out=nm, in0=sc[:, 24:25], scalar=-1.0, in1=rs, op0=ALU.mult, op1=ALU.mult),
             reads=[scres + 'ag', scres + 'rs'], writes=[scres + 'nm'])
        S.op('act', lambda e: e.activation(out=xap, in_=xap, func=AF.Identity, scale=rs, bias=nm),
             reads=xres + [scres + 'rs', scres + 'nm'], writes=xres)
        S.op('dve', lambda e: e.tensor_tensor(out=xap, in0=xap, in1=G[:], op=ALU.mult), reads=xres + [gres], writes=xres)
        S.op(eng_b, lambda e: e.tensor_tensor(out=xap, in0=xap, in1=Bt[:], op=ALU.add), reads=xres + [gres], writes=xres)

    def transpose_block(self, xap, xres, xb, xbres, pT, pTres, hto, blk, htres):
        S = self.S
        S.op('act', lambda e: e.activation(out=xb[:], in_=xap, func=AF.Copy), reads=xres, writes=[xbres])
        for hb in range(2):
            def tr(e, hb=hb):
                for c in range(8):
                    cc = hb * 8 + c
                    ins = e.transpose(out=pT[:, hb, c * P:(c + 1) * P], in_=xb[:, cc * P:(cc + 1) * P], identity=self.ident[:])
                return ins
            S.op('pe', tr, reads=[xbres, 'ident'], writes=[pTres + str(hb)])
            eng = 'dve' if hb == 0 else 'act'
            if eng == 'dve':
                S.op('dve', lambda e, hb=hb: e.tensor_copy(out=hto[:, hb * 8:(hb + 1) * 8, blk * P:(blk + 1) * P],
                                                           in_=pT[:, hb, :].rearrange("p (c t) -> p c t", t=P)),
                     reads=[pTres + str(hb)], writes=[htres + 'b%d_%d' % (blk, hb)])
            else:
                S.op('act', lambda e, hb=hb: e.activation(out=hto[:, hb * 8:(hb + 1) * 8, blk * P:(blk + 1) * P],
                                                          in_=pT[:, hb, :].rearrange("p (c t) -> p c t", t=P), func=AF.Copy),
                     reads=[pTres + str(hb)], writes=[htres + 'b%d_%d' % (blk, hb)])

    def load_gb(self, G, Bt, gsrc, bsrc, gres):
        S = self.S
        S.dma('sp', lambda e: e.dma_start(out=G[:], in_=gsrc.partition_broadcast(P)), writes=[gres])
        S.dma('sp', lambda e: e.dma_start(out=Bt[:], in_=bsrc.partition_broadcast(P)), writes=[gres])

    def phase_ln_in(self):
        S = self.S
        with contextlib.ExitStack() as es:
            G = self.tile(es, "G", [P, D])
            Bt = self.tile(es, "Bt", [P, D])
            X = [self.tile(es, "X%d" % i, [P, 4, D]) for i in range(2)]
            XB = [self.tile(es, "XB%d" % i, [P, D], BF16) for i in range(2)]
            HTO = [self.tile(es, "HTO%d" % i, [P, KC, TT], BF16) for i in range(2)]
            SC = [self.tile(es, "SC%d" % i, [P, 32]) for i in range(2)]
            pT = self.psum(es, "pT", [P, 2, 1024], BF16)
            self.load_gb(G, Bt, self.ln_in_g, self.ln_in_b, 'GB')
            nt = self.T // TT
            for i in range(nt):
                x, xr = X[i % 2], "X%d" % (i % 2)
                hto, hr = HTO[i % 2], "HTO%d" % (i % 2)
                S.dma('sp', lambda e, i=i, x=x: e.dma_start(out=x[:], in_=self.x[i * TT:(i + 1) * TT, :].rearrange("(b p) d -> p b d", p=P)),
                      writes=[xr + 'b%d' % b for b in range(4)])
                for b in range(4):
                    k = (i * 4 + b) % 2
                    self.ln_block(x[:, b, :], [xr + 'b%d' % b], G, Bt, 'GB', SC[k], 'SC%d' % k)
                    self.transpose_block(x[:, b, :], [xr + 'b%d' % b], XB[k], 'XB%d' % k, pT, 'pT', hto, b, hr)
                S.dma('sp', lambda e, i=i, x=x: e.dma_start(out=self.H[i * TT:(i + 1) * TT, :].rearrange("(b p) d -> p b d", p=P), in_=x[:]),
                      reads=[xr + 'b%d' % b for b in range(4)])
                S.dma('sp', lambda e, i=i, hto=hto: e.dma_start(out=self.HT[0][:, :, i * TT:(i + 1) * TT], in_=hto[:]),
                      reads=[hr + 'b%d_%d' % (b, hb) for b in range(4) for hb in range(2)])
            S.flush("ln_in")

    def phase_preconvert(self):
        S = self.S
        for l in range(self.depth):
            for cb in range(IN_W // P):
                S.dma('pool', lambda e, l=l, cb=cb: e.dma_start(out=self.WB_in[l, cb], in_=self.w_in[l][:, cb * P:(cb + 1) * P].rearrange("(c p) n -> p c n", p=P)))
            for cb in range(2 * FC):
                S.dma('pool', lambda e, l=l, cb=cb: e.dma_start(out=self.WB_fi[l, cb], in_=self.w_ffn_in[l][:, cb * P:(cb + 1) * P].rearrange("(c p) n -> p c n", p=P)))
            for cb in range(KC):
                S.dma('pool', lambda e, l=l, cb=cb: e.dma_start(out=self.WB_a[l, cb], in_=self.w_branch_a[l][:, cb * P:(cb + 1) * P].rearrange("(c p) n -> p c n", p=P)))
                S.dma('pool', lambda e, l=l, cb=cb: e.dma_start(out=self.WB_b[l, cb], in_=self.w_branch_b[l][:, cb * P:(cb + 1) * P].rearrange("(c p) n -> p c n", p=P)))
            for k in range(KC):
                S.dma('pool', lambda e, l=l, k=k: e.dma_start(out=self.WB_o[l][k * P:(k + 1) * P, :], in_=self.w_out[l][k * P:(k + 1) * P, :]))
            for k in range(FC):
                S.dma('pool', lambda e, l=l, k=k: e.dma_start(out=self.WB_fo[l][k * P:(k + 1) * P, :], in_=self.w_ffn_out[l][k * P:(k + 1) * P, :]))
        S.flush("preconv")

    def wblk(self, dst_ap, src_ap, wres, q='pool'):
        self.S.dma(q, lambda e: e.dma_start(out=dst_ap, in_=src_ap), writes=[wres])

    def phase_sweep(self, l, hin, t0, nu, u, d):
        S = self.S
        tu = t0 + u * UT
        first_unit = (u == 0) if d == 0 else (u == nu - 1)
        nproj = 4 if d == 0 else 3
        with contextlib.ExitStack() as es:
            HTs = self.tile(es, "HTs", [P, KC, UT], BF16)
            WR = Ring(es, self.nc, "W", [P, nproj, KC, P], BF16, 2)
            QS = self.tile(es, "QS", [P, UT])
            SG = self.tile(es, "SG", [P, UT])
            GG = self.tile(es, "GG", [P, UT])
            BB = self.tile(es, "BB", [P, UT])
            EE = self.tile(es, "EE", [P, UT])
            SMK = self.tile(es, "SMK", [P, UT])
            QT = self.tile(es, "QT", [P, UT], BF16)
            KT = self.tile(es, "KT", [P, UT], BF16)
            VT = self.tile(es, "VT", [P, 16, P], BF16)
            KTK = self.tile(es, "KTK", [P, 16, P], BF16)
            OT = self.tile(es, "OT", [P, UT])
            SML = self.tile(es, "SML", [P, 4, NCHK])
            SMS = [self.tile(es, "SMS%d" % i, [P, CH], BF16) for i in range(2)]
            STL = [self.tile(es, "STL%d" % i, [P, P], BF16) for i in range(2)]
            if d == 0:
                HG = self.tile(es, "HG", [P, UT])
                AT = self.tile(es, "AT", [P, UT], BF16)
            ps = self.psum(es, "ps", [P, 6, 512], F32)
            pT = self.psum(es, "pTs", [P, 2, 1024], BF16)
            S.op('pool', lambda e: e.memset(SMK[:], 1.0), writes=['SMK'])
            S.op('pool', lambda e: e.memset(SMK[:].rearrange("p (c t) -> p c t", t=CH)[:, :, 0:1], 0.0), writes=['SMK'])
            S.dma('sp', lambda e: e.dma_start(out=HTs[:], in_=hin[:, :, tu:tu + UT]), writes=['HTs'])
            if first_unit:
                S.op('pool', lambda e: e.memset(self.U[d][:], 0.0), writes=['U'])
                S.op('pool', lambda e: e.memset(self.Est[d][:], 0.0), writes=['Est'])
            offs = [OFF_Q, OFF_FF if d == 0 else OFF_FB, OFF_I] + ([OFF_G] if d == 0 else [])
            wl = self.w_in[l]

            def load_w(j):
                wt, wn = WR.next()
                for pi, off in enumerate(offs):
                    self.wblk(wt[:, pi, :, :], self.WB_in[l, off // P + j], wn + '_%d' % pi)
                return wt, wn
            nxt = load_w(0)
            mask = self.hmf if d == 0 else self.hmb
            mres = 'hmf' if d == 0 else 'hmb'
            lc = (d * 8)
            for j in range(8):
                wt, wn = nxt
                if j + 1 < 8:
                    nxt = load_w(j + 1)
                lbc = lc + j
                for tt in range(4):
                    tsl = slice(tt * TT, (tt + 1) * TT)
                    for pi, dst in ((0, 'q'), (1, 'f')) + (((3, 'g'),) if d == 0 else ()):
                        bank = (tt * 3 + (pi if pi < 2 else 2)) % 4
                        def mm(e, pi=pi, bank=bank, tsl=tsl, wt=wt):
                            for k in range(KC):
                                ins = e.matmul(ps[:, bank, :], wt[:, pi, k, :], HTs[:, k, tsl], start=(k == 0), stop=(k == KC - 1))
                            return ins
                        S.op('pe', mm, reads=[wn + '_%d' % pi, 'HTs'], writes=['ps%d' % bank])
                        if dst == 'q':
                            S.op('act', lambda e, bank=bank, tsl=tsl: e.activation(out=QS[:, tsl], in_=ps[:, bank, :], func=AF.Silu),
                                 reads=['ps%d' % bank], writes=['QS%d' % tt])
                        elif dst == 'f':
                            S.op('act', lambda e, bank=bank, tsl=tsl: e.activation(out=SG[:, tsl], in_=ps[:, bank, :], func=AF.Sigmoid),
                                 reads=['ps%d' % bank], writes=['SG%d' % tt])
                        else:
                            S.op('act', lambda e, bank=bank, tsl=tsl: e.activation(out=HG[:, tsl], in_=ps[:, bank, :], func=AF.Silu),
                                 reads=['ps%d' % bank], writes=['HG%d' % tt])
                for g4 in range(4):
                    bank = 4 + (g4 % 2)
                    def mmv(e, g4=g4, bank=bank, wt=wt):
                        for bl in range(4):
                            blk = g4 * 4 + bl
                            for k in range(KC):
                                ins = e.matmul(ps[:, bank, bl * P:(bl + 1) * P], HTs[:, k, blk * P:(blk + 1) * P], wt[:, 2, k, :],
                                               start=(k == 0), stop=(k == KC - 1))
                        return ins
                    S.op('pe', mmv, reads=[wn + '_2', 'HTs'], writes=['ps%d' % bank])
                    S.op('dve', lambda e, g4=g4, bank=bank: e.tensor_copy(out=VT[:, g4 * 4:(g4 + 1) * 4, :],
                                                                            in_=ps[:, bank, :].rearrange("p (b v) -> p b v", v=P)),
                         reads=['ps%d' % bank], writes=['VT%d' % g4])
                allq = ['QS%d' % i for i in range(4)]
                allsg = ['SG%d' % i for i in range(4)]
                S.op('act', lambda e, lbc=lbc: e.activation(out=GG[:], in_=SG[:], func=AF.Ln, scale=self.OML[:, l, lbc:lbc + 1],
                                                            bias=self.LB[:, l, lbc:lbc + 1]),
                     reads=allsg + ['OML', 'LB'], writes=['GG'])
                S.op('dve', lambda e, lbc=lbc: e.tensor_scalar(out=SG[:], in0=SG[:], scalar1=-1.0, scalar2=self.LBM1[:, l, lbc:lbc + 1],
                                                               op0=ALU.add, op1=ALU.mult),
                     reads=allsg + ['LBM1', 'GG'], writes=allsg)
                S.op('dve', lambda e: e.tensor_tensor_scan(out=BB[:], data0=SMK[:], data1=GG[:], initial=0.0, op0=ALU.mult, op1=ALU.add),
                     reads=['SMK', 'GG'], writes=['BB'])
                b3 = BB[:].rearrange("p (c t) -> p c t", t=CH)
                if d == 1:
                    S.op('dve', lambda e: e.tensor_tensor(out=EE[:].rearrange("p (c t) -> p c t", t=CH), in0=b3[:, :, CH - 1:CH].broadcast_to([P, NCHK, CH]),
                                                          in1=b3, op=ALU.subtract), reads=['BB'], writes=['EE'])
                    S.op('dve', lambda e: e.tensor_tensor(out=BB[:], in0=EE[:], in1=GG[:], op=ALU.add), reads=['EE', 'GG'], writes=['BB'])
                    iend, imid = 0, CH // 2
                else:
                    iend, imid = CH - 1, CH // 2 - 1
                ev, mv, esh, cex = SML[:, 0, :], SML[:, 1, :], SML[:, 2, :], SML[:, 3, :]
                S.op('dve', lambda e: e.tensor_copy(out=mv, in_=b3[:, :, imid]), reads=['BB'], writes=['mv'])
                S.op('dve', lambda e: e.tensor_tensor(out=ev, in0=b3[:, :, iend], in1=b3[:, :, imid], op=ALU.subtract), reads=['BB'], writes=['ev'])
                if d == 0:
                    S.op('dve', lambda e: e.tensor_copy(out=SML[:, 2, 1:NCHK], in_=SML[:, 0, 0:NCHK - 1]), reads=['ev'], writes=['esh'])
                    S.op('dve', lambda e, j=j: e.tensor_copy(out=SML[:, 2, 0:1], in_=self.Est[d][:, j:j + 1]), reads=['Est'], writes=['esh0'])
                    S.op('dve', lambda e, j=j: e.tensor_copy(out=self.Est[d][:, j:j + 1], in_=SML[:, 0, NCHK - 1:NCHK]), reads=['ev', 'esh0'], writes=['Est'])
                else:
                    S.op('dve', lambda e: e.tensor_copy(out=SML[:, 2, 0:NCHK - 1], in_=SML[:, 0, 1:NCHK]), reads=['ev'], writes=['esh'])
                    S.op('dve', lambda e, j=j: e.tensor_copy(out=SML[:, 2, NCHK - 1:NCHK], in_=self.Est[d][:, j:j + 1]), reads=['Est'], writes=['esh0'])
                    S.op('dve', lambda e, j=j: e.tensor_copy(out=self.Est[d][:, j:j + 1], in_=SML[:, 0, 0:1]), reads=['ev', 'esh0'], writes=['Est'])
                S.op('dve', lambda e: e.tensor_tensor(out=cex, in0=esh, in1=mv, op=ALU.add), reads=['esh', 'esh0', 'mv'], writes=['cex'])
                S.op('act', lambda e: e.activation(out=cex, in_=cex, func=AF.Exp), reads=['cex'], writes=['cex'])
                S.op('dve', lambda e: e.tensor_tensor(out=b3, in0=b3, in1=SML[:, 1, :].unsqueeze(2).broadcast_to([P, NCHK, CH]), op=ALU.subtract),
                     reads=['BB', 'mv'], writes=['BB'])
                S.op('act', lambda e: e.activation(out=EE[:], in_=BB[:], func=AF.Exp), reads=['BB'], writes=['EE'])
                S.op('dve', lambda e: e.tensor_tensor(out=QT[:], in0=QS[:], in1=EE[:], op=ALU.mult), reads=allq + ['EE'], writes=['QT'])
                S.op('act', lambda e: e.activation(out=EE[:], in_=BB[:], func=AF.Exp, scale=-1.0), reads=['BB', 'QT'], writes=['EE'])
                S.op('dve', lambda e: e.tensor_tensor(out=KT[:], in0=SG[:], in1=EE[:], op=ALU.mult), reads=allsg + ['EE'], writes=['KT'])
                for hb in range(2):
                    def trk(e, hb=hb):
                        for c in range(8):
                            blk = hb * 8 + c
                            ins = e.transpose(out=pT[:, hb, c * P:(c + 1) * P], in_=KT[:, blk * P:(blk + 1) * P], identity=self.ident[:])
                        return ins
                    S.op('pe', trk, reads=['KT', 'ident'], writes=['pTs%d' % hb])
                    S.op('act', lambda e, hb=hb: e.activation(out=KTK[:, hb * 8:(hb + 1) * 8, :], in_=pT[:, hb, :].rearrange("p (c t) -> p c t", t=P),
                                                              func=AF.Copy), reads=['pTs%d' % hb], writes=['KTK%d' % hb])
                order = list(range(NCHK)) if d == 0 else list(reversed(range(NCHK)))
                Uj = self.U[d][:, j, :]
                for n_i, i in enumerate(order):
                    csl = slice(i * CH, (i + 1) * CH)
                    blk, half = i // 2, i % 2
                    pr = slice(half * CH, (half + 1) * CH)
                    sb = n_i % 2
                    grp = i // 8
                    ob = 4 + (grp % 2)
                    oc = slice((i % 8) * CH, (i % 8 + 1) * CH)
                    S.op('pe', lambda e, sb=sb, csl=csl: e.matmul(ps[0:CH, sb, 0:CH], KT[:, csl], QT[:, csl], start=True, stop=True),
                         reads=['KT', 'QT'], writes=['ps%d' % sb])
                    S.op('pe', lambda e, sb=sb, blk=blk, pr=pr: e.matmul(ps[:, 2 + sb, 0:P], KTK[pr, blk, :], VT[pr, blk, :], start=True, stop=True),
                         reads=['KTK%d' % (blk // 8), 'VT%d' % (blk // 4)], writes=['ps%d' % (2 + sb)])
                    sms, smn = SMS[sb], 'SMS%d' % sb
                    stl, stn = STL[sb], 'STL%d' % sb
                    S.op('dve', lambda e, sb=sb, pr=pr, sms=sms: e.tensor_tensor(out=sms[pr, :], in0=ps[0:CH, sb, 0:CH], in1=mask[:], op=ALU.mult),
                         reads=['ps%d' % sb, mres], writes=[smn])
                    S.op('act', lambda e, i=i, stl=stl, Uj=Uj: e.activation(out=stl[:], in_=Uj, func=AF.Copy, scale=SML[:, 3, i:i + 1]),
                         reads=['U', 'cex'], writes=[stn])
                    S.op('dve', lambda e, i=i, sb=sb, Uj=Uj: e.scalar_tensor_tensor(out=Uj, in0=Uj, scalar=SML[:, 3, i:i + 1], in1=ps[:, 2 + sb, 0:P],
                                                                             op0=ALU.mult, op1=ALU.add),
                         reads=['U', 'cex', 'ps%d' % (2 + sb), stn], writes=['U'])

                    def mmo(e, ob=ob, oc=oc, csl=csl, blk=blk, pr=pr, stl=stl, sms=sms):
                        e.matmul(ps[:, ob, oc], stl[:], QT[:, csl], start=True, stop=False)
                        return e.matmul(ps[:, ob, oc], VT[pr, blk, :], sms[pr, :], start=False, stop=True)
                    S.op('pe', mmo, reads=[stn, smn, 'QT', 'VT%d' % (blk // 4)], writes=['ps%d' % ob])
                    if n_i % 8 == 7:
                        S.op('act', lambda e, ob=ob, grp=grp: e.activation(out=OT[:, grp * 512:(grp + 1) * 512], in_=ps[:, ob, :], func=AF.Copy),
                             reads=['ps%d' % ob], writes=['OT%d' % grp])
                if d == 1:
                    S.dma('sp', lambda e, j=j: e.dma_start(out=self.OB[j, :, u * UT:(u + 1) * UT], in_=OT[:]),
                          reads=['OT%d' % g_ for g_ in range(4)])
                else:
                    allo = ['OT%d' % g_ for g_ in range(4)]
                    S.dma('sp', lambda e, j=j: e.dma_start(out=EE[:], in_=self.OB[j, :, u * UT:(u + 1) * UT]), writes=['EE'])
                    S.op('dve', lambda e: e.tensor_tensor(out=OT[:], in0=OT[:], in1=EE[:], op=ALU.add), reads=allo + ['EE'], writes=allo)
                    S.op('act', lambda e: e.activation(out=GG[:], in_=OT[:], func=AF.Square), reads=allo, writes=['GG'])
                    for tt in range(4):
                        tsl = slice(tt * TT, (tt + 1) * TT)
                        bank = tt % 2
                        S.op('pe', lambda e, bank=bank, tsl=tsl: e.matmul(ps[:, bank, :], self.ones_f[:], GG[:, tsl], start=True, stop=True),
                             reads=['ones_f', 'GG'], writes=['ps%d' % bank])
                        S.op('act', lambda e, bank=bank, tsl=tsl: e.activation(out=EE[:, tsl], in_=ps[:, bank, :], func=AF.Sqrt, scale=1.0 / P,
                                                                               bias=self.eps_rms[:, 0:1]),
                             reads=['ps%d' % bank, 'eps_rms'], writes=['EE'])
                    S.op('dve', lambda e: e.reciprocal(out=EE[:], in_=EE[:]), reads=['EE'], writes=['EE'])
                    S.op('dve', lambda e: e.scalar_tensor_tensor(out=OT[:], in0=OT[:], scalar=self.NG[:, l:l + 1], in1=EE[:], op0=ALU.mult, op1=ALU.mult),
                         reads=allo + ['EE', 'NG'], writes=allo)
                    S.op('dve', lambda e: e.tensor_tensor(out=AT[:], in0=OT[:], in1=HG[:], op=ALU.mult),
                         reads=allo + ['HG%d' % i for i in range(4)], writes=['AT'])
                    S.dma('sp', lambda e, j=j: e.dma_start(out=self.AB[:, j, :], in_=AT[:]), reads=['AT'])
            S.flush("sweep")

    def phase_attn(self, l, hin, t0, nu, u):
        S = self.S
        tu = t0 + u * UT
        has_prev = u > 0
        has_next = u < nu - 1
        NK = UT + 2 * P
        pos0 = u * UT - P
        with contextlib.ExitStack() as es:
            HTs = self.tile(es, "HTa", [P, KC, NK], BF16)
            COS = self.tile(es, "COS", [P, NK])
            SIN = self.tile(es, "SIN", [P, NK])
            WQ = Ring(es, self.nc, "WQ", [P, 4, KC, P], BF16, 1)
            WKV = self.tile(es, "WKV", [P, 4, KC, P], BF16)
            KR = self.tile(es, "KR", [P, 2, NK], BF16)
            VTt = self.tile(es, "VTt", [P, 18, 2 * P], BF16)
            QR = self.tile(es, "QR", [P, 4, UT], BF16)
            T1 = [self.tile(es, "T1%d" % i, [P, TT]) for i in range(2)]
            T2 = [self.tile(es, "T2%d" % i, [P, TT]) for i in range(2)]
            PT = [self.tile(es, "PT%d" % i, [P, 3, 512], BF16) for i in range(2)]
            RD = self.tile(es, "RD", [P, 512])
            BT = self.tile(es, "BT", [P, 4, UT], BF16)
            ps = self.psum(es, "psa", [P, 8, 512], F32)
            lo = 0 if has_prev else P
            hi = NK if has_next else NK - P
            S.dma('sp', lambda e: e.dma_start(out=HTs[:, :, lo:hi], in_=hin[:, :, tu - P + lo:tu - P + hi]), writes=['HTa'])
            S.dma('sp', lambda e: e.dma_start(out=COS[:, lo:hi], in_=self.c_cos[:, pos0 + lo:pos0 + hi]), writes=['COS'])
            S.dma('sp', lambda e: e.dma_start(out=SIN[:, lo:hi], in_=self.c_sin[:, pos0 + lo:pos0 + hi]), writes=['SIN'])
            wl = self.w_in[l]
            for i4 in range(4):
                self.wblk(WKV[:, i4, :, :], self.WB_in[l, OFF_AK // P + i4], 'WKV%d' % i4)
            rope_i = [0]

            def rope(bank, ncol, cs, out_ap, outres):
                k = rope_i[0] % 2
                rope_i[0] += 1
                t1, t2 = T1[k], T2[k]
                S.op('dve', lambda e: e.tensor_tensor(out=t1[:, 0:ncol], in0=ps[:, bank, 0:ncol], in1=COS[:, cs], op=ALU.mult),
                     reads=['ps%d' % bank, 'COS'], writes=['T1%d' % k])
                S.op('dve', lambda e: e.tensor_tensor(out=t2[0:64, 0:ncol], in0=ps[64:128, bank, 0:ncol], in1=SIN[64:128, cs], op=ALU.mult),
                     reads=['ps%d' % bank, 'SIN'], writes=['T2a%d' % k])
                S.op('dve', lambda e: e.tensor_tensor(out=t2[64:128, 0:ncol], in0=ps[0:64, bank, 0:ncol], in1=SIN[0:64, cs], op=ALU.mult),
                     reads=['ps%d' % bank, 'SIN'], writes=['T2b%d' % k])
                S.op('dve', lambda e: e.tensor_tensor(out=out_ap, in0=t1[:, 0:ncol], in1=t2[:, 0:ncol], op=ALU.add),
                     reads=['T1%d' % k, 'T2a%d' % k, 'T2b%d' % k], writes=[outres])
            segs = []
            if has_prev:
                segs.append((0, P))
            for tt in range(4):
                segs.append((P + tt * TT, TT))
            if has_next:
                segs.append((P + UT, P))
            bi = 0
            for kvh in range(2):
                for (c0, nc_) in segs:
                    bank = bi % 2
                    bi += 1
                    def mmk(e, kvh=kvh, c0=c0, nc_=nc_, bank=bank):
                        for k in range(KC):
                            ins = e.matmul(ps[:, bank, 0:nc_], WKV[:, kvh, k, :], HTs[:, k, c0:c0 + nc_], start=(k == 0), stop=(k == KC - 1))
                        return ins
                    S.op('pe', mmk, reads=['WKV%d' % kvh, 'HTa'], writes=['ps%d' % bank])
                    rope(bank, nc_, slice(c0, c0 + nc_), KR[:, kvh, c0:c0 + nc_], 'KR')
            blks = list(range(0 if has_prev else 1, 18 if has_next else 17))
            for gi in range(0, len(blks), 2):
                grp = blks[gi:gi + 2]
                bank = 2 + (gi // 2) % 2
                def mmv(e, grp=grp, bank=bank):
                    for bl, blk in enumerate(grp):
                        for k in range(KC):
                            ins = e.matmul(ps[:, bank, bl * 2 * P:(bl + 1) * 2 * P], HTs[:, k, blk * P:(blk + 1) * P], WKV[:, 2:4, k, :],
                                           start=(k == 0), stop=(k == KC - 1))
                    return ins
                S.op('pe', mmv, reads=['WKV2', 'WKV3', 'HTa'], writes=['ps%d' % bank])
                S.op('act', lambda e, grp=grp, bank=bank: e.activation(out=VTt[:, grp[0]:grp[0] + len(grp), :],
                                                                       in_=ps[:, bank, 0:len(grp) * 2 * P].rearrange("p (b v) -> p b v", v=2 * P), func=AF.Copy),
                     reads=['ps%d' % bank], writes=['VTt'])
            nxt = WQ.next()
            for i4 in range(4):
                self.wblk(nxt[0][:, i4, :, :], self.WB_in[l, OFF_AQ // P + i4], nxt[1] + '_%d' % i4)
            for kvh in range(2):
                wq, wqn = nxt
                for g in range(4):
                    for tt in range(4):
                        bank = bi % 2
                        bi += 1
                        def mmq(e, g=g, tt=tt, bank=bank, wq=wq):
                            for k in range(KC):
                                ins = e.matmul(ps[:, bank, :], wq[:, g, k, :], HTs[:, k, P + tt * TT:P + (tt + 1) * TT],
                                               start=(k == 0), stop=(k == KC - 1))
                            return ins
                        S.op('pe', mmq, reads=[wqn + '_%d' % g, 'HTa'], writes=['ps%d' % bank])
                        rope(bank, TT, slice(P + tt * TT, P + (tt + 1) * TT), QR[:, g, tt * TT:(tt + 1) * TT], 'QR')
                if kvh == 0:
                    nxt = WQ.next()
                    for i4 in range(4):
                        self.wblk(nxt[0][:, i4, :, :], self.WB_in[l, OFF_AQ // P + 4 + i4], nxt[1] + '_%d' % i4)
                for n in range(16):
                    kbs = []
                    if n > 0 or has_prev:
                        kbs.append((n, self.amp, 'amp'))
                    kbs.append((n + 1, None, None))
                    if n < 15 or has_next:
                        kbs.append((n + 2, self.amn, 'amn'))
                    pb = (n % 2) * 3
                    pt, ptn = PT[n % 2], 'PT%d' % (n % 2)
                    qap = QR[:, :, n * P:(n + 1) * P]
                    for ki, (kb, am, amres) in enumerate(kbs):
                        def mms(e, ki=ki, kb=kb, am=am, pb=pb, kvh=kvh, qap=qap):
                            ins = e.matmul(ps[:, pb + ki, :], KR[:, kvh, kb * P:(kb + 1) * P], qap, start=True, stop=(am is None))
                            if am is not None:
                                ins = e.matmul(ps[:, pb + ki, :], self.ident[:], am[:].unsqueeze(1).broadcast_to([P, 4, P]), start=False, stop=True)
                            return ins
                        S.op('pe', mms, reads=['KR', 'QR', 'ident'] + ([amres] if am is not None else []), writes=['ps%d' % (pb + ki)])
                        S.op('act', lambda e, ki=ki, pb=pb, pt=pt: e.activation(out=pt[:, ki, :], in_=ps[:, pb + ki, :], func=AF.Exp, scale=QSCALE),
                             reads=['ps%d' % (pb + ki)], writes=[ptn + '_%d' % ki])
                    nk = len(kbs)

                    def mmd(e, nk=nk, pt=pt):
                        for ki in range(nk):
                            ins = e.matmul(ps[:, 6, :], self.ones_bf[:], pt[:, ki, :], start=(ki == 0), stop=(ki == nk - 1))
                        return ins
                    S.op('pe', mmd, reads=['ones_bf'] + [ptn + '_%d' % ki for ki in range(nk)], writes=['ps6'])

                    def mmpv(e, kbs=kbs, pt=pt, kvh=kvh):
                        for ki, (kb, _, _) in enumerate(kbs):
                            ins = e.matmul(ps[:, 7, :], VTt[:, kb, kvh * P:(kvh + 1) * P], pt[:, ki, :], start=(ki == 0), stop=(ki == len(kbs) - 1))
                        return ins
                    S.op('pe', mmpv, reads=['VTt'] + [ptn + '_%d' % ki for ki in range(nk)], writes=['ps7'])
                    es_ap = self.ES[:, l * 8 + kvh * 4:l * 8 + kvh * 4 + 4].unsqueeze(2).broadcast_to([P, 4, P])
                    S.op('dve', lambda e, es_ap=es_ap: e.tensor_tensor(out=RD[:].rearrange("p (g q) -> p g q", q=P), in0=ps[:, 6, :].rearrange("p (g q) -> p g q", q=P),
                                                                       in1=es_ap, op=ALU.add), reads=['ps6', 'ES'], writes=['RD'])
                    S.op('dve', lambda e: e.reciprocal(out=RD[:], in_=RD[:]), reads=['RD'], writes=['RD'])
                    S.op('dve', lambda e, n=n: e.tensor_tensor(out=BT[:, :, n * P:(n + 1) * P], in0=ps[:, 7, :].rearrange("p (g q) -> p g q", q=P),
                                                               in1=RD[:].rearrange("p (g q) -> p g q", q=P), op=ALU.mult),
                         reads=['ps7', 'RD'], writes=['BT'])
                S.dma('sp', lambda e, kvh=kvh: e.dma_start(out=self.AB[:, 8 + kvh * 4:8 + kvh * 4 + 4, :], in_=BT[:]), reads=['BT'])
            S.flush("attn")

    def phase_tail(self, l, hin, hout, t0, u, last):
        S = self.S
        tu = t0 + u * UT
        with contextlib.ExitStack() as es:
            HTt = self.tile(es, "HTt", [P, KC, TT], BF16)
            ABt = self.tile(es, "ABt", [P, KC, TT], BF16)
            MT = self.tile(es, "MT", [P, KC, TT], BF16)
            X = self.tile(es, "Xt", [P, 4, D])
            RES = Ring(es, self.nc, "RES", [P, 1024], F32, 2)
            ACT_T = self.tile(es, "ACTT", [P, FC, TT], BF16)
            G = self.tile(es, "Gt", [P, D])
            Bt = self.tile(es, "Btt", [P, D])
            WG = Ring(es, self.nc, "WG", [P, 2, KC, P], BF16, 2)
            WAB = Ring(es, self.nc, "WAB", [P, 2, 8, P], BF16, 2)
            WK = Ring(es, self.nc, "WK", [P, 1024], BF16, 4)
            SGA = [self.tile(es, "SGA%d" % i, [P, TT]) for i in range(2)]
            SGB = [self.tile(es, "SGB%d" % i, [P, TT]) for i in range(2)]
            XB = [self.tile(es, "XBt%d" % i, [P, D], BF16) for i in range(2)]
            SC = [self.tile(es, "SCt%d" % i, [P, 32]) for i in range(2)]
            ps = self.psum(es, "pst", [P, 8, 512], F32)
            wl = self.w_in[l]
            for tt in range(4):
                tk = tu + tt * TT
                S.dma('sp', lambda e, tk=tk: e.dma_start(out=HTt[:], in_=hin[:, :, tk:tk + TT]), writes=['HTt'])
                S.dma('sp', lambda e, tt=tt: e.dma_start(out=ABt[:], in_=self.AB[:, :, tt * TT:(tt + 1) * TT]), writes=['ABt'])
                self.load_gb(G, Bt, self.ln1_g[l], self.ln1_b[l], 'GBt')

                def load_merge(c):
                    wg, wgn = WG.next()
                    self.wblk(wg[:, 0, :, :], self.WB_in[l, OFF_GA // P + c], wgn + '_0')
                    self.wblk(wg[:, 1, :, :], self.WB_in[l, OFF_GB // P + c], wgn + '_1')
                    wab, wabn = WAB.next()
                    self.wblk(wab[:, 0, :, :], self.WB_a[l, c], wabn + '_0')
                    self.wblk(wab[:, 1, :, :], self.WB_b[l, c], wabn + '_1')
                    return wg, wgn, wab, wabn
                nxt = load_merge(0)
                for c in range(KC):
                    wg, wgn, wab, wabn = nxt
                    if c + 1 < KC:
                        nxt = load_merge(c + 1)
                    pb = (c % 2) * 4
                    for gi in range(2):
                        def mmg(e, gi=gi, pb=pb, wg=wg):
                            for k in range(KC):
                                ins = e.matmul(ps[:, pb + gi, :], wg[:, gi, k, :], HTt[:, k, :], start=(k == 0), stop=(k == KC - 1))
                            return ins
                        S.op('pe', mmg, reads=[wgn + '_%d' % gi, 'HTt'], writes=['ps%d' % (pb + gi)])

                        def mmb(e, gi=gi, pb=pb, wab=wab):
                            for k in range(8):
                                ins = e.matmul(ps[:, pb + 2 + gi, :], wab[:, gi, k, :], ABt[:, gi * 8 + k, :], start=(k == 0), stop=(k == 7))
                            return ins
                        S.op('pe', mmb, reads=[wabn + '_%d' % gi, 'ABt'], writes=['ps%d' % (pb + 2 + gi)])
                    k2 = c % 2
                    sga, sgb = SGA[k2], SGB[k2]
                    S.op('act', lambda e, pb=pb, sga=sga: e.activation(out=sga[:], in_=ps[:, pb, :], func=AF.Sigmoid),
                         reads=['ps%d' % pb], writes=['SGA%d' % k2])
                    S.op('act', lambda e, pb=pb, sgb=sgb: e.activation(out=sgb[:], in_=ps[:, pb + 1, :], func=AF.Sigmoid),
                         reads=['ps%d' % (pb + 1)], writes=['SGB%d' % k2])
                    S.op('dve', lambda e, pb=pb, sga=sga: e.tensor_tensor(out=sga[:], in0=sga[:], in1=ps[:, pb + 2, :], op=ALU.mult),
                         reads=['SGA%d' % k2, 'ps%d' % (pb + 2)], writes=['SGA%d' % k2])
                    S.op('dve', lambda e, pb=pb, sgb=sgb: e.tensor_tensor(out=sgb[:], in0=sgb[:], in1=ps[:, pb + 3, :], op=ALU.mult),
                         reads=['SGB%d' % k2, 'ps%d' % (pb + 3)], writes=['SGB%d' % k2])
                    S.op('dve', lambda e, c=c, sga=sga, sgb=sgb: e.tensor_tensor(out=MT[:, c, :], in0=sga[:], in1=sgb[:], op=ALU.add),
                         reads=['SGA%d' % k2, 'SGB%d' % k2], writes=['MT%d' % c])
                allmt = ['MT%d' % c for c in range(KC)]

                def tm_proj(wsrc, nk, lhs_tile, lhs_res, res_from_dram):
                    for half in range(2):
                        hs = slice(half * 1024, (half + 1) * 1024)

                        def load_k(k):
                            wk, wkn = WK.next()
                            S.dma('pool', lambda e, k=k, wk=wk, hs=hs: e.dma_start(out=wk[:], in_=wsrc[k * P:(k + 1) * P, hs]), writes=[wkn])
                            return wk, wkn
                        q = [load_k(0), load_k(1), load_k(2)]
                        for k in range(nk):
                            wk, wkn = q.pop(0)
                            if k + 3 < nk:
                                q.append(load_k(k + 3))

                            def mmt(e, k=k, wk=wk):
                                for blk in range(4):
                                    for nb in range(2):
                                        ins = e.matmul(ps[:, blk * 2 + nb, :], lhs_tile[:, k, blk * P:(blk + 1) * P], wk[:, nb * 512:(nb + 1) * 512],
                                                       start=(k == 0), stop=(k == nk - 1))
                                return ins
                            S.op('pe', mmt, reads=[wkn] + lhs_res, writes=['ps%d' % b for b in range(8)])
                        for blk in range(4):
                            if res_from_dram:
                                rt, rn = RES.next()
                                S.dma('sp', lambda e, blk=blk, rt=rt, hs=hs, tk=tk: e.dma_start(out=rt[:], in_=self.H[tk + blk * P:tk + (blk + 1) * P, hs]), writes=[rn])
                                S.op('dve', lambda e, blk=blk, rt=rt, hs=hs: e.scalar_tensor_tensor(out=X[:, blk, hs], in0=rt[:], scalar=ALPHA,
                                                                                            in1=ps[:, blk * 2:blk * 2 + 2, :].rearrange("p a b -> p (a b)"),
                                                                                            op0=ALU.mult, op1=ALU.add),
                                     reads=[rn, 'ps%d' % (blk * 2), 'ps%d' % (blk * 2 + 1)], writes=['X%d_%d' % (blk, half)])
                            else:
                                S.op('dve', lambda e, blk=blk, hs=hs: e.scalar_tensor_tensor(out=X[:, blk, hs], in0=X[:, blk, hs], scalar=ALPHA,
                                                                                     in1=ps[:, blk * 2:blk * 2 + 2, :].rearrange("p a b -> p (a b)"),
                                                                                     op0=ALU.mult, op1=ALU.add),
                                     reads=['X%d_%d' % (blk, half), 'ps%d' % (blk * 2), 'ps%d' % (blk * 2 + 1)], writes=['X%d_%d' % (blk, half)])
                tm_proj(self.WB_o[l], KC, MT, allmt, True)
                for blk in range(4):
                    k2 = blk % 2
                    xr = ['X%d_0' % blk, 'X%d_1' % blk]
                    self.ln_block(X[:, blk, :], xr, G, Bt, 'GBt', SC[k2], 'SCt%d' % k2)
                    self.tail_transpose(X[:, blk, :], xr, XB[k2], 'XBt%d' % k2, ps, MT, blk, 'H1T')
                h1t = ['H1Tb%d_%d' % (b, hb) for b in range(4) for hb in range(2)]
                self.load_gb(G, Bt, self.ln2_g[l], self.ln2_b[l], 'GBt')
                wf = self.w_ffn_in[l]

                def load_ffn(c):
                    wg, wgn = WG.next()
                    self.wblk(wg[:, 0, :, :], self.WB_fi[l, c], wgn + '_0')
                    self.wblk(wg[:, 1, :, :], self.WB_fi[l, FC + c], wgn + '_1')
                    return wg, wgn
                nxt = load_ffn(0)
                for c in range(FC):
                    wg, wgn = nxt
                    if c + 1 < FC:
                        nxt = load_ffn(c + 1)
                    pb = (c % 4) * 2
                    for gi in range(2):
                        def mmf(e, gi=gi, pb=pb, wg=wg):
                            for k in range(KC):
                                ins = e.matmul(ps[:, pb + gi, :], wg[:, gi, k, :], MT[:, k, :], start=(k == 0), stop=(k == KC - 1))
                            return ins
                        S.op('pe', mmf, reads=[wgn + '_%d' % gi] + h1t, writes=['ps%d' % (pb + gi)])
                    k2 = c % 2
                    sga = SGA[k2]
                    S.op('act', lambda e, pb=pb, sga=sga: e.activation(out=sga[:], in_=ps[:, pb, :], func=AF.Silu),
                         reads=['ps%d' % pb], writes=['SGA%d' % k2])
                    S.op('dve', lambda e, pb=pb, sga=sga, c=c: e.tensor_tensor(out=ACT_T[:, c, :], in0=sga[:], in1=ps[:, pb + 1, :], op=ALU.mult),
                         reads=['SGA%d' % k2, 'ps%d' % (pb + 1)], writes=['ACT%d' % c])
                tm_proj(self.WB_fo[l], FC, ACT_T, ['ACT%d' % c for c in range(FC)], False)
                for blk in range(4):
                    k2 = blk % 2
                    xr = ['X%d_0' % blk, 'X%d_1' % blk]
                    self.ln_block(X[:, blk, :], xr, G, Bt, 'GBt', SC[k2], 'SCt%d' % k2)
                    if not last:
                        self.tail_transpose(X[:, blk, :], xr, XB[k2], 'XBt%d' % k2, ps, HTt, blk, 'H2T')
                dst = self.y if last else self.H
                S.dma('sp', lambda e, tk=tk: e.dma_start(out=dst[tk:tk + TT, :].rearrange("(b p) d -> p b d", p=P), in_=X[:]),
                      reads=['X%d_%d' % (b, h_) for b in range(4) for h_ in range(2)])
                if not last:
                    S.dma('sp', lambda e, tk=tk: e.dma_start(out=hout[:, :, tk:tk + TT], in_=HTt[:]),
                          reads=['H2Tb%d_%d' % (b, hb) for b in range(4) for hb in range(2)])
            S.flush("tail")

    def tail_transpose(self, xap, xres, xb, xbres, ps, hto, blk, htres):
        S = self.S
        S.op('act', lambda e: e.activation(out=xb[:], in_=xap, func=AF.Copy), reads=xres, writes=[xbres])
        for hb in range(2):
            pv = ps[:, hb, :].bitcast(BF16)

            def tr(e, hb=hb, pv=pv):
                for c in range(8):
                    cc = hb * 8 + c
                    ins = e.transpose(out=pv[:, c * P:(c + 1) * P], in_=xb[:, cc * P:(cc + 1) * P], identity=self.ident[:])
                return ins
            S.op('pe', tr, reads=[xbres, 'ident'], writes=['ps%d' % hb])
            if hb == 0:
                S.op('dve', lambda e, hb=hb, pv=pv: e.tensor_copy(out=hto[:, hb * 8:(hb + 1) * 8, blk * P:(blk + 1) * P],
                                                                   in_=pv.rearrange("p (c t) -> p c t", t=P)),
                     reads=['ps%d' % hb], writes=[htres + 'b%d_%d' % (blk, hb)])
            else:
                S.op('act', lambda e, hb=hb, pv=pv: e.activation(out=hto[:, hb * 8:(hb + 1) * 8, blk * P:(blk + 1) * P],
                                                                  in_=pv.rearrange("p (c t) -> p c t", t=P), func=AF.Copy),
                     reads=['ps%d' % hb], writes=[htres + 'b%d_%d' % (blk, hb)])


def const_tables():
    inv = 1.0 / (10000.0 ** (np.arange(0, 128, 2, dtype=np.float32) / 128.0))
    ang = np.arange(LMAX, dtype=np.float32)[None, :] * inv[:, None].astype(np.float32)
    ang = ang.astype(np.float32)
    cos, sin = np.cos(ang).astype(np.float32), np.sin(ang).astype(np.float32)
    c_cos = np.concatenate([cos, cos], 0)
    c_sin = np.concatenate([sin, -sin], 0)
    s = np.arange(CH)[:, None]
    t = np.arange(CH)[None, :]
    hmf = (s <= t).astype(np.float32)
    hmb = (s >= t).astype(np.float32)
    j = np.arange(P)[:, None]
    i = np.arange(P)[None, :]
    amp = np.where(j >= i, 0.0, NEG).astype(ml_dtypes.bfloat16)
    amn = np.where(j <= i, 0.0, NEG).astype(ml_dtypes.bfloat16)
    return dict(c_cos=np.ascontiguousarray(c_cos), c_sin=np.ascontiguousarray(c_sin), c_ident=np.eye(P).astype(ml_dtypes.bfloat16),
                c_hmf=hmf, c_hmb=hmb, c_amp=amp, c_amn=amn)


_WNAMES = ["ln_in_g", "ln_in_b", "w_in", "lb_logits", "hg_norm_g", "attn_sink", "w_branch_a", "w_branch_b", "w_out",
           "ln1_g", "ln1_b", "w_ffn_in", "w_ffn_out", "ln2_g", "ln2_b"]


def run_cores(xs, weights, seq_units, depth):
    b = Builder(seq_units, depth)
    nc = b.build()
    consts = const_tables()
    in_maps = []
    for x in xs:
        m = {"x": np.ascontiguousarray(x, dtype=np.float32)}
        for k in _WNAMES:
            m[k] = np.ascontiguousarray(np.asarray(weights[k], dtype=np.float32))
        m.update(consts)
        in_maps.append(m)
    res = run_bass_kernel_spmd(nc, in_maps, core_ids=list(range(len(xs))))
    return [r["y"] for r in res.results]


def kernel(x_prompt, x_sample, **weights):
    x_prompt = np.asarray(x_prompt, dtype=np.float32)
    x_sample = np.asarray(x_sample, dtype=np.float32)
    weights = {k: np.asarray(v) for k, v in weights.items()}
    n = 8
    B, Ls, _ = x_prompt.shape
    per = B // n
    xs = []
    for c in range(n):
        xs.append(np.concatenate([x_prompt[c * per:(c + 1) * per].reshape(per * Ls, D), x_sample[0]], axis=0))
    nsu = x_sample.shape[1] // UT
    ys = run_cores(xs, weights, [1] * per + [nsu], DEPTH)
    y_prompt = np.empty_like(x_prompt)
    y_sample = np.empty_like(x_sample)
    seg = x_sample.shape[1] // n
    for c in range(n):
        y_prompt[c * per:(c + 1) * per] = ys[c][:per * Ls].reshape(per, Ls, D)
        y_sample[0, c * seg:(c + 1) * seg] = ys[c][per * Ls + c * seg:per * Ls + (c + 1) * seg]
    return (y_prompt, y_sample)
```

```python
import contextlib
import numpy as np
import ml_dtypes
import concourse.bass as bass
import concourse.mybir as mybir
from concourse.bass_utils import run_bass_kernel_spmd

F32 = mybir.dt.float32
BF16 = mybir.dt.bfloat16
AF = mybir.ActivationFunctionType
ALU = mybir.AluOpType

P = 128
D = 2048
KC = 16
UT = 2048
TT = 512
CH = 64
NCHK = UT // CH
DFF = 5632
FC = DFF // P
IN_W = 10752
OFF_Q, OFF_FF, OFF_FB, OFF_I, OFF_G = 0, 1024, 2048, 3072, 4096
OFF_AQ, OFF_AK, OFF_AV, OFF_GA, OFF_GB = 5120, 6144, 6400, 6656, 8704
DEPTH = 4
ALPHA = float((2 * DEPTH) ** 0.25)
LN_EPS = 1e-5
RMS_EPS = 1e-6
QSCALE = float(128 ** -0.5)
NEG = -30000.0
LMAX = 8192

COMPUTE = ('pe', 'act', 'dve', 'pool')
_UID = [0]


class Sched:
    def __init__(self, nc, es, n_dma_sems=12):
        self.nc = nc
        self.sem = {}
        for e in COMPUTE:
            self.sem[e] = es.enter_context(nc.semaphore("s_" + e))
        self.queues = ('sp', 'pool')
        self.dma_sems = {}
        for q in self.queues:
            self.dma_sems[q] = []
            for i in range(n_dma_sems):
                k = "d_%s_%d" % (q, i)
                self.sem[k] = es.enter_context(nc.semaphore(k))
                self.dma_sems[q].append(k)
        self.val = {k: 0 for k in self.sem}
        self.dma_rr = {q: 0 for q in self.queues}
        self.streams = {e: [] for e in ('pe', 'act', 'dve', 'pool', 'sp')}
        self.known = {e: {} for e in self.streams}
        self.res = {}
        self.n_inst = 0

    def _deps(self, reads, writes):
        deps = {}

        def add(tok):
            if tok is None:
                return
            k, v = tok
            if deps.get(k, 0) < v:
                deps[k] = v
        for r in reads:
            st = self.res.get(r)
            if st:
                add(st['w'])
        for w in writes:
            st = self.res.get(w)
            if st:
                add(st['w'])
                for k, v in st['r'].items():
                    add((k, v))
        return deps

    def _commit(self, tok, reads, writes):
        k, v = tok
        for r in reads:
            st = self.res.setdefault(r, {'w': None, 'r': {}})
            if st['r'].get(k, 0) < v:
                st['r'][k] = v
        for w in writes:
            self.res[w] = {'w': tok, 'r': {}}

    def _waits(self, stream, deps, skip_self=None):
        out = []
        kn = self.known[stream]
        for k, v in deps.items():
            if k == skip_self:
                continue
            if kn.get(k, 0) < v:
                kn[k] = v
                out.append((k, v))
        return out

    def op(self, eng, fn, reads=(), writes=()):
        deps = self._deps(reads, writes)
        waits = self._waits(eng, deps, skip_self='pe' if eng == 'pe' else None)
        self.val[eng] += 1
        tok = (eng, self.val[eng])
        self.streams[eng].append((waits, fn, tok, 1))
        self._commit(tok, reads, writes)
        return tok

    def dma(self, q, fn, reads=(), writes=()):
        deps = self._deps(reads, writes)
        sems = self.dma_sems[q]
        k = sems[self.dma_rr[q] % len(sems)]
        self.dma_rr[q] += 1
        if self.val[k] > 0:
            deps[k] = max(deps.get(k, 0), self.val[k])
        waits = self._waits(q, deps)
        self.val[k] += 16
        tok = (k, self.val[k])
        self.streams[q].append((waits, fn, tok, 16))
        self._commit(tok, reads, writes)
        return tok

    def flush(self, name=None):
        deps = {}
        for q in self.queues:
            for k in self.dma_sems[q]:
                if self.val[k] > 0:
                    deps[k] = self.val[k]
        waits = self._waits('sp', dict(deps))
        if waits:
            self.streams['sp'].append((waits, None, None, 0))
        streams = self.streams
        sem = self.sem
        cnt = [0]

        def replay(engine, lst):
            for waits, fn, tok, inc in lst:
                for k, v in waits:
                    engine.wait_ge(sem[k], v)
                    cnt[0] += 1
                if fn is not None:
                    ins = fn(engine)
                    ins.then_inc(sem[tok[0]], inc)
                    cnt[0] += 1

        _UID[0] += 1
        with self.nc.Block("%s_%d" % (name or "blk", _UID[0])) as block:
            @block.tensor
            def _(e):
                replay(e, streams['pe'])

            @block.scalar
            def _(e):
                replay(e, streams['act'])

            @block.vector
            def _(e):
                replay(e, streams['dve'])

            @block.gpsimd
            def _(e):
                replay(e, streams['pool'])

            @block.sync
            def _(e):
                replay(e, streams['sp'])
        self.n_inst += cnt[0]
        self.streams = {e: [] for e in streams}
        self.res = {}
        for e in self.known:
            for k in self.val:
                self.known[e][k] = self.val[k]


class Ring:
    def __init__(self, es, nc, name, shape, dt, n):
        _UID[0] += 1
        self.tiles = [es.enter_context(nc.sbuf_tensor("%s%d_%d" % (name, i, _UID[0]), shape, dt)) for i in range(n)]
        self.names = ["%s%d" % (name, i) for i in range(n)]
        self.i = 0

    def next(self):
        t, n = self.tiles[self.i % len(self.tiles)], self.names[self.i % len(self.tiles)]
        self.i += 1
        return t, n


class Builder:
    def __init__(self, seq_units, depth, debug=False, stop_after=None):
        self.debug = debug
        self.stop_after = stop_after
        self.seq_units = list(seq_units)
        self.depth = depth
        self.T = sum(seq_units) * UT
        self.lmax = max(seq_units) * UT
        nc = bass.Bass("TRN2", target_bir_lowering=False)
        self.nc = nc
        T, L = self.T, depth

        def din(name, shape, dt=F32):
            return nc.dram_tensor(name, shape, dt, kind="ExternalInput").ap()

        def dscr(name, shape, dt=F32):
            return nc.dram_tensor(name, shape, dt, kind="ExternalOutput" if debug else "Internal").ap()
        self.x = din("x", [T, D])
        self.ln_in_g = din("ln_in_g", [D])
        self.ln_in_b = din("ln_in_b", [D])
        self.w_in = din("w_in", [L, D, IN_W])
        self.lb_logits = din("lb_logits", [L, 2048])
        self.hg_norm_g = din("hg_norm_g", [L, P])
        self.attn_sink = din("attn_sink", [L, 8])
        self.w_branch_a = din("w_branch_a", [L, 1024, D])
        self.w_branch_b = din("w_branch_b", [L, 1024, D])
        self.w_out = din("w_out", [L, D, D])
        self.ln1_g = din("ln1_g", [L, D])
        self.ln1_b = din("ln1_b", [L, D])
        self.w_ffn_in = din("w_ffn_in", [L, D, 2 * DFF])
        self.w_ffn_out = din("w_ffn_out", [L, DFF, D])
        self.ln2_g = din("ln2_g", [L, D])
        self.ln2_b = din("ln2_b", [L, D])
        self.c_cos = din("c_cos", [P, LMAX])
        self.c_sin = din("c_sin", [P, LMAX])
        self.c_ident = din("c_ident", [P, P], BF16)
        self.c_hmf = din("c_hmf", [CH, CH])
        self.c_hmb = din("c_hmb", [CH, CH])
        self.c_amp = din("c_amp", [P, P], BF16)
        self.c_amn = din("c_amn", [P, P], BF16)
        self.y = nc.dram_tensor("y", [T, D], F32, kind="ExternalOutput").ap()
        self.H = dscr("H", [T, D])
        self.HT = [dscr("HT%d" % i, [P, KC, T], BF16) for i in range(2)]
        self.OB = dscr("OB", [8, P, self.lmax])
        self.AB = dscr("AB", [P, KC, UT], BF16)
        self.WB_in = dscr("WB_in", [L, IN_W // P, P, KC, P], BF16)
        self.WB_fi = dscr("WB_fi", [L, 2 * FC, P, KC, P], BF16)
        self.WB_a = dscr("WB_a", [L, KC, P, 8, P], BF16)
        self.WB_b = dscr("WB_b", [L, KC, P, 8, P], BF16)
        self.WB_o = dscr("WB_o", [L, D, D], BF16)
        self.WB_fo = dscr("WB_fo", [L, DFF, D], BF16)

    def tile(self, es, name, shape, dt=F32):
        _UID[0] += 1
        return es.enter_context(self.nc.sbuf_tensor("%s_%d" % (name, _UID[0]), shape, dt))

    def psum(self, es, name, shape, dt=F32):
        _UID[0] += 1
        return es.enter_context(self.nc.psum_tensor("%s_%d" % (name, _UID[0]), shape, dt))

    def build(self):
        nc = self.nc
        with contextlib.ExitStack() as g:
            self.S = Sched(nc, g)
            S = self.S
            L = self.depth
            self.ident = self.tile(g, "ident", [P, P], BF16)
            self.ones_bf = self.tile(g, "ones_bf", [P, P], BF16)
            self.ones_f = self.tile(g, "ones_f", [P, P], F32)
            self.hmf = self.tile(g, "hmf", [CH, CH], F32)
            self.hmb = self.tile(g, "hmb", [CH, CH], F32)
            self.amp = self.tile(g, "amp", [P, P], BF16)
            self.amn = self.tile(g, "amn", [P, P], BF16)
            self.eps_ln = self.tile(g, "eps_ln", [P, 1], F32)
            self.eps_rms = self.tile(g, "eps_rms", [P, 1], F32)
            self.LB = self.tile(g, "LB", [P, L, 16], F32)
            self.OML = self.tile(g, "OML", [P, L, 16], F32)
            self.LBM1 = self.tile(g, "LBM1", [P, L, 16], F32)
            self.NG = self.tile(g, "NG", [P, L], F32)
            self.ES = self.tile(g, "ES", [P, L * 8], F32)
            self.U = [self.tile(g, "U%d" % d, [P, 8, P], F32) for d in range(2)]
            self.Est = [self.tile(g, "Est%d" % d, [P, 8], F32) for d in range(2)]
            self.phase_setup()
            self.phase_preconvert()
            self.phase_ln_in()
            t0 = 0
            for l in range(L):
                hin, hout = self.HT[l % 2], self.HT[(l + 1) % 2]
                t0 = 0
                for nu in self.seq_units:
                    for u in reversed(range(nu)):
                        self.phase_sweep(l, hin, t0, nu, u, 1)
                    for u in range(nu):
                        self.phase_sweep(l, hin, t0, nu, u, 0)
                        self.phase_attn(l, hin, t0, nu, u)
                        self.phase_tail(l, hin, hout, t0, u, last=(l == L - 1))
                    t0 += nu * UT
        return nc

    def phase_setup(self):
        S, L = self.S, self.depth
        with contextlib.ExitStack() as es:
            lg = self.tile(es, "lg", [P, L, 16])
            ex = self.tile(es, "ex", [P, L, 16])
            mx = self.tile(es, "mx", [P, 16])
            sm = self.tile(es, "sm", [P, 16])
            S.dma('sp', lambda e: e.dma_start(out=self.ident[:], in_=self.c_ident), writes=['ident'])
            S.dma('sp', lambda e: e.dma_start(out=self.hmf[:], in_=self.c_hmf), writes=['hmf'])
            S.dma('sp', lambda e: e.dma_start(out=self.hmb[:], in_=self.c_hmb), writes=['hmb'])
            S.dma('sp', lambda e: e.dma_start(out=self.amp[:], in_=self.c_amp), writes=['amp'])
            S.dma('sp', lambda e: e.dma_start(out=self.amn[:], in_=self.c_amn), writes=['amn'])
            S.dma('sp', lambda e: e.dma_start(out=lg[:], in_=self.lb_logits.rearrange("l (c p) -> p l c", p=P),
                                              allow_slow_non_contiguous=True), writes=['lg'])
            S.dma('sp', lambda e: e.dma_start(out=self.NG[:], in_=self.hg_norm_g.rearrange("l p -> p l"),
                                              allow_slow_non_contiguous=True), writes=['NG'])
            S.dma('sp', lambda e: e.dma_start(out=self.ES[:], in_=self.attn_sink.rearrange("l h -> (l h)").partition_broadcast(P)),
                  writes=['ES'])
            S.op('pool', lambda e: e.memset(self.ones_bf[:], 1.0), writes=['ones_bf'])
            S.op('pool', lambda e: e.memset(self.ones_f[:], 1.0), writes=['ones_f'])
            S.op('pool', lambda e: e.memset(self.eps_ln[:], LN_EPS), writes=['eps_ln'])
            S.op('pool', lambda e: e.memset(self.eps_rms[:], RMS_EPS), writes=['eps_rms'])
            S.op('act', lambda e: e.activation(out=self.ES[:], in_=self.ES[:], func=AF.Exp), reads=['ES'], writes=['ES'])
            S.op('dve', lambda e: e.tensor_copy(out=mx[:], in_=lg[:, 0, :]), reads=['lg'], writes=['mx'])
            for l in range(1, L):
                S.op('dve', lambda e, l=l: e.tensor_tensor(out=mx[:], in0=mx[:], in1=lg[:, l, :], op=ALU.max),
                     reads=['lg', 'mx'], writes=['mx'])
            for l in range(L):
                S.op('dve', lambda e, l=l: e.tensor_tensor(out=ex[:, l, :], in0=lg[:, l, :], in1=mx[:], op=ALU.subtract),
                     reads=['lg', 'mx'], writes=['ex'])
            S.op('act', lambda e: e.activation(out=ex[:], in_=ex[:], func=AF.Exp), reads=['ex'], writes=['ex'])
            S.op('dve', lambda e: e.tensor_copy(out=sm[:], in_=ex[:, 0, :]), reads=['ex'], writes=['sm'])
            for l in range(1, L):
                S.op('dve', lambda e, l=l: e.tensor_tensor(out=sm[:], in0=sm[:], in1=ex[:, l, :], op=ALU.add),
                     reads=['ex', 'sm'], writes=['sm'])
            S.op('dve', lambda e: e.reciprocal(out=sm[:], in_=sm[:]), reads=['sm'], writes=['sm'])
            S.op('pool', lambda e: e.memset(self.LB[:, 0, :], 0.0), writes=['LB'])
            for l in range(1, L):
                S.op('dve', lambda e, l=l: e.tensor_tensor(out=ex[:, l, :], in0=ex[:, l, :], in1=sm[:], op=ALU.mult),
                     reads=['ex', 'sm'], writes=['ex'])
                S.op('dve', lambda e, l=l: e.tensor_tensor(out=self.LB[:, l, :], in0=self.LB[:, l - 1, :], in1=ex[:, l, :], op=ALU.add),
                     reads=['ex', 'LB'], writes=['LB'])
            S.op('dve', lambda e: e.tensor_scalar(out=self.OML[:], in0=self.LB[:], scalar1=-1.0, scalar2=1.0, op0=ALU.mult, op1=ALU.add),
                 reads=['LB'], writes=['OML'])
            S.op('dve', lambda e: e.tensor_scalar(out=self.LBM1[:], in0=self.LB[:], scalar1=-1.0, scalar2=None, op0=ALU.add),
                 reads=['LB'], writes=['LBM1'])
            S.flush("setup")

    def ln_block(self, xap, xres, G, Bt, gres, sc, scres, eng_b='dve'):
        S = self.S
        st = sc[:, 0:24]
        ag = sc[:, 24:26]
        sd = sc[:, 26:27]
        rs = sc[:, 27:28]
        nm = sc[:, 28:29]
        for c in range(4):
            S.op('dve', lambda e, c=c: e.bn_stats(out=sc[:, c * 6:(c + 1) * 6], in_=xap[:, c * 512:(c + 1) * 512]),
                 reads=xres, writes=[scres + 's%d' % c])
        S.op('dve', lambda e: e.bn_aggr(out=ag, in_=st), reads=[scres + 's%d' % c for c in range(4)], writes=[scres + 'ag'])
        S.op('act', lambda e: e.activation(out=sd, in_=sc[:, 25:26], func=AF.Sqrt, bias=self.eps_ln[:, 0:1], scale=1.0),
             reads=[scres + 'ag'], writes=[scres + 'sd'])
        S.op('dve', lambda e: e.reciprocal(out=rs, in_=sd), reads=[scres + 'sd'], writes=[scres + 'rs'])
        S.op('dve', lambda e: e.scalar_tensor_tensor(out=nm, in0=sc[:, 24:25], scalar=-1.0, in1=rs, op0=ALU.mult, op1=ALU.mult),
             reads=[scres + 'ag', scres + 'rs'], writes=[scres + 'nm'])
        S.op('act', lambda e: e.activation(out=xap, in_=xap, func=AF.Identity, scale=rs, bias=nm),
             reads=xres + [scres + 'rs', scres + 'nm'], writes=xres)
        S.op('dve', lambda e: e.tensor_tensor(out=xap, in0=xap, in1=G[:], op=ALU.mult), reads=xres + [gres], writes=xres)
        S.op(eng_b, lambda e: e.tensor_tensor(out=xap, in0=xap, in1=Bt[:], op=ALU.add), reads=xres + [gres], writes=xres)

    def transpose_block(self, xap, xres, xb, xbres, pT, pTres, hto, blk, htres):
        S = self.S
        S.op('act', lambda e: e.activation(out=xb[:], in_=xap, func=AF.Copy), reads=xres, writes=[xbres])
        for hb in range(2):
            def tr(e, hb=hb):
                for c in range(8):
                    cc = hb * 8 + c
                    ins = e.transpose(out=pT[:, hb, c * P:(c + 1) * P], in_=xb[:, cc * P:(cc + 1) * P], identity=self.ident[:])
                return ins
            S.op('pe', tr, reads=[xbres, 'ident'], writes=[pTres + str(hb)])
            eng = 'dve' if hb == 0 else 'act'
            if eng == 'dve':
                S.op('dve', lambda e, hb=hb: e.tensor_copy(out=hto[:, hb * 8:(hb + 1) * 8, blk * P:(blk + 1) * P],
                                                           in_=pT[:, hb, :].rearrange("p (c t) -> p c t", t=P)),
                     reads=[pTres + str(hb)], writes=[htres + 'b%d_%d' % (blk, hb)])
            else:
                S.op('act', lambda e, hb=hb: e.activation(out=hto[:, hb * 8:(hb + 1) * 8, blk * P:(blk + 1) * P],
                                                          in_=pT[:, hb, :].rearrange("p (c t) -> p c t", t=P), func=AF.Copy),
                     reads=[pTres + str(hb)], writes=[htres + 'b%d_%d' % (blk, hb)])

    def load_gb(self, G, Bt, gsrc, bsrc, gres):
        S = self.S
        S.dma('sp', lambda e: e.dma_start(out=G[:], in_=gsrc.partition_broadcast(P)), writes=[gres])
        S.dma('sp', lambda e: e.dma_start(out=Bt[:], in_=bsrc.partition_broadcast(P)), writes=[gres])

    def phase_ln_in(self):
        S = self.S
        with contextlib.ExitStack() as es:
            G = self.tile(es, "G", [P, D])
            Bt = self.tile(es, "Bt", [P, D])
            X = [self.tile(es, "X%d" % i, [P, 4, D]) for i in range(2)]
            XB = [self.tile(es, "XB%d" % i, [P, D], BF16) for i in range(2)]
            HTO = [self.tile(es, "HTO%d" % i, [P, KC, TT], BF16) for i in range(2)]
            SC = [self.tile(es, "SC%d" % i, [P, 32]) for i in range(2)]
            pT = self.psum(es, "pT", [P, 2, 1024], BF16)
            self.load_gb(G, Bt, self.ln_in_g, self.ln_in_b, 'GB')
            nt = self.T // TT
            for i in range(nt):
                x, xr = X[i % 2], "X%d" % (i % 2)
                hto, hr = HTO[i % 2], "HTO%d" % (i % 2)
                S.dma('sp', lambda e, i=i, x=x: e.dma_start(out=x[:], in_=self.x[i * TT:(i + 1) * TT, :].rearrange("(b p) d -> p b d", p=P)),
                      writes=[xr + 'b%d' % b for b in range(4)])
                for b in range(4):
                    k = (i * 4 + b) % 2
                    self.ln_block(x[:, b, :], [xr + 'b%d' % b], G, Bt, 'GB', SC[k], 'SC%d' % k)
                    self.transpose_block(x[:, b, :], [xr + 'b%d' % b], XB[k], 'XB%d' % k, pT, 'pT', hto, b, hr)
                S.dma('sp', lambda e, i=i, x=x: e.dma_start(out=self.H[i * TT:(i + 1) * TT, :].rearrange("(b p) d -> p b d", p=P), in_=x[:]),
                      reads=[xr + 'b%d' % b for b in range(4)])
                S.dma('sp', lambda e, i=i, hto=hto: e.dma_start(out=self.HT[0][:, :, i * TT:(i + 1) * TT], in_=hto[:]),
                      reads=[hr + 'b%d_%d' % (b, hb) for b in range(4) for hb in range(2)])
            S.flush("ln_in")

    def phase_preconvert(self):
        S = self.S
        for l in range(self.depth):
            for cb in range(IN_W // P):
                S.dma('pool', lambda e, l=l, cb=cb: e.dma_start(out=self.WB_in[l, cb], in_=self.w_in[l][:, cb * P:(cb + 1) * P].rearrange("(c p) n -> p c n", p=P)))
            for cb in range(2 * FC):
                S.dma('pool', lambda e, l=l, cb=cb: e.dma_start(out=self.WB_fi[l, cb], in_=self.w_ffn_in[l][:, cb * P:(cb + 1) * P].rearrange("(c p) n -> p c n", p=P)))
            for cb in range(KC):
                S.dma('pool', lambda e, l=l, cb=cb: e.dma_start(out=self.WB_a[l, cb], in_=self.w_branch_a[l][:, cb * P:(cb + 1) * P].rearrange("(c p) n -> p c n", p=P)))
                S.dma('pool', lambda e, l=l, cb=cb: e.dma_start(out=self.WB_b[l, cb], in_=self.w_branch_b[l][:, cb * P:(cb + 1) * P].rearrange("(c p) n -> p c n", p=P)))
            for k in range(KC):
                S.dma('pool', lambda e, l=l, k=k: e.dma_start(out=self.WB_o[l][k * P:(k + 1) * P, :], in_=self.w_out[l][k * P:(k + 1) * P, :]))
            for k in range(FC):
                S.dma('pool', lambda e, l=l, k=k: e.dma_start(out=self.WB_fo[l][k * P:(k + 1) * P, :], in_=self.w_ffn_out[l][k * P:(k + 1) * P, :]))
        S.flush("preconv")

    def wblk(self, dst_ap, src_ap, wres, q='pool'):
        self.S.dma(q, lambda e: e.dma_start(out=dst_ap, in_=src_ap), writes=[wres])

    def phase_sweep(self, l, hin, t0, nu, u, d):
        S = self.S
        tu = t0 + u * UT
        first_unit = (u == 0) if d == 0 else (u == nu - 1)
        nproj = 4 if d == 0 else 3
        with contextlib.ExitStack() as es:
            HTs = self.tile(es, "HTs", [P, KC, UT], BF16)
            WR = Ring(es, self.nc, "W", [P, nproj, KC, P], BF16, 2)
            QS = self.tile(es, "QS", [P, UT])
            SG = self.tile(es, "SG", [P, UT])
            GG = self.tile(es, "GG", [P, UT])
            BB = self.tile(es, "BB", [P, UT])
            EE = self.tile(es, "EE", [P, UT])
            SMK = self.tile(es, "SMK", [P, UT])
            QT = self.tile(es, "QT", [P, UT], BF16)
            KT = self.tile(es, "KT", [P, UT], BF16)
            VT = self.tile(es, "VT", [P, 16, P], BF16)
            KTK = self.tile(es, "KTK", [P, 16, P], BF16)
            OT = self.tile(es, "OT", [P, UT])
            SML = self.tile(es, "SML", [P, 4, NCHK])
            SMS = [self.tile(es, "SMS%d" % i, [P, CH], BF16) for i in range(2)]
            STL = [self.tile(es, "STL%d" % i, [P, P], BF16) for i in range(2)]
            U1 = self.tile(es, "U1", [P, P])
            if d == 0:
                HG = self.tile(es, "HG", [P, UT])
                AT = self.tile(es, "AT", [P, UT], BF16)
            ps = self.psum(es, "ps", [P, 6, 512], F32)
            pT = self.psum(es, "pTs", [P, 2, 1024], BF16)
            S.op('pool', lambda e: e.memset(SMK[:], 1.0), writes=['SMK'])
            S.op('pool', lambda e: e.memset(SMK[:].rearrange("p (c t) -> p c t", t=CH)[:, :, 0:1], 0.0), writes=['SMK'])
            S.dma('sp', lambda e: e.dma_start(out=HTs[:], in_=hin[:, :, tu:tu + UT]), writes=['HTs'])
            if first_unit:
                S.op('pool', lambda e: e.memset(self.U[d][:], 0.0), writes=['U'])
                S.op('pool', lambda e: e.memset(self.Est[d][:], 0.0), writes=['Est'])
            offs = [OFF_Q, OFF_FF if d == 0 else OFF_FB, OFF_I] + ([OFF_G] if d == 0 else [])
            wl = self.w_in[l]

            def load_w(j):
                wt, wn = WR.next()
                for pi, off in enumerate(offs):
                    self.wblk(wt[:, pi, :, :], self.WB_in[l, off // P + j], wn + '_%d' % pi)
                return wt, wn
            nxt = load_w(0)
            mask = self.hmf if d == 0 else self.hmb
            mres = 'hmf' if d == 0 else 'hmb'
            lc = (d * 8)
            for j in range(8):
                wt, wn = nxt
                if j + 1 < 8:
                    nxt = load_w(j + 1)
                lbc = lc + j
                def emit_proj(which):
                  for tt in range(4):
                      tsl = slice(tt * TT, (tt + 1) * TT)
                      for pi, dst in which:
                          bank = (tt * 3 + (pi if pi < 2 else 2)) % 4
                          def mm(e, pi=pi, bank=bank, tsl=tsl, wt=wt):
                              for k in range(KC):
                                  ins = e.matmul(ps[:, bank, :], wt[:, pi, k, :], HTs[:, k, tsl], start=(k == 0), stop=(k == KC - 1))
                              return ins
                          S.op('pe', mm, reads=[wn + '_%d' % pi, 'HTs'], writes=['ps%d' % bank])
                          if dst == 'q':
                              S.op('act', lambda e, bank=bank, tsl=tsl: e.activation(out=QS[:, tsl], in_=ps[:, bank, :], func=AF.Silu),
                                   reads=['ps%d' % bank], writes=['QS%d' % tt])
                          elif dst == 'f':
                              S.op('act', lambda e, bank=bank, tsl=tsl: e.activation(out=SG[:, tsl], in_=ps[:, bank, :], func=AF.Sigmoid),
                                   reads=['ps%d' % bank], writes=['SG%d' % tt])
                          else:
                              S.op('act', lambda e, bank=bank, tsl=tsl: e.activation(out=HG[:, tsl], in_=ps[:, bank, :], func=AF.Silu),
                                   reads=['ps%d' % bank], writes=['HG%d' % tt])
                emit_proj(((1, 'f'),))
                allq = ['QS%d' % i for i in range(4)]
                allsg = ['SG%d' % i for i in range(4)]
                S.op('act', lambda e, lbc=lbc: e.activation(out=GG[:], in_=SG[:], func=AF.Ln, scale=self.OML[:, l, lbc:lbc + 1],
                                                            bias=self.LB[:, l, lbc:lbc + 1]),
                     reads=allsg + ['OML', 'LB'], writes=['GG'])
                S.op('dve', lambda e, lbc=lbc: e.tensor_scalar(out=SG[:], in0=SG[:], scalar1=-1.0, scalar2=self.LBM1[:, l, lbc:lbc + 1],
                                                               op0=ALU.add, op1=ALU.mult),
                     reads=allsg + ['LBM1', 'GG'], writes=allsg)
                S.op('dve', lambda e: e.tensor_tensor_scan(out=BB[:], data0=SMK[:], data1=GG[:], initial=0.0, op0=ALU.mult, op1=ALU.add),
                     reads=['SMK', 'GG'], writes=['BB'])
                b3 = BB[:].rearrange("p (c t) -> p c t", t=CH)
                if d == 1:
                    S.op('dve', lambda e: e.tensor_tensor(out=EE[:].rearrange("p (c t) -> p c t", t=CH), in0=b3[:, :, CH - 1:CH].broadcast_to([P, NCHK, CH]),
                                                          in1=b3, op=ALU.subtract), reads=['BB'], writes=['EE'])
                    S.op('dve', lambda e: e.tensor_tensor(out=BB[:], in0=EE[:], in1=GG[:], op=ALU.add), reads=['EE', 'GG'], writes=['BB'])
                    iend, imid = 0, CH // 2
                else:
                    iend, imid = CH - 1, CH // 2 - 1
                ev, mv, esh, cex = SML[:, 0, :], SML[:, 1, :], SML[:, 2, :], SML[:, 3, :]
                S.op('dve', lambda e: e.tensor_copy(out=mv, in_=b3[:, :, imid]), reads=['BB'], writes=['mv'])
                S.op('dve', lambda e: e.tensor_tensor(out=ev, in0=b3[:, :, iend], in1=b3[:, :, imid], op=ALU.subtract), reads=['BB'], writes=['ev'])
                if d == 0:
                    S.op('dve', lambda e: e.tensor_copy(out=SML[:, 2, 1:NCHK], in_=SML[:, 0, 0:NCHK - 1]), reads=['ev'], writes=['esh'])
                    S.op('dve', lambda e, j=j: e.tensor_copy(out=SML[:, 2, 0:1], in_=self.Est[d][:, j:j + 1]), reads=['Est'], writes=['esh0'])
                    S.op('dve', lambda e, j=j: e.tensor_copy(out=self.Est[d][:, j:j + 1], in_=SML[:, 0, NCHK - 1:NCHK]), reads=['ev', 'esh0'], writes=['Est'])
                else:
                    S.op('dve', lambda e: e.tensor_copy(out=SML[:, 2, 0:NCHK - 1], in_=SML[:, 0, 1:NCHK]), reads=['ev'], writes=['esh'])
                    S.op('dve', lambda e, j=j: e.tensor_copy(out=SML[:, 2, NCHK - 1:NCHK], in_=self.Est[d][:, j:j + 1]), reads=['Est'], writes=['esh0'])
                    S.op('dve', lambda e, j=j: e.tensor_copy(out=self.Est[d][:, j:j + 1], in_=SML[:, 0, 0:1]), reads=['ev', 'esh0'], writes=['Est'])
                S.op('dve', lambda e: e.tensor_tensor(out=cex, in0=esh, in1=mv, op=ALU.add), reads=['esh', 'esh0', 'mv'], writes=['cex'])
                S.op('act', lambda e: e.activation(out=cex, in_=cex, func=AF.Exp), reads=['cex'], writes=['cex'])
                S.op('dve', lambda e: e.tensor_tensor(out=b3, in0=b3, in1=SML[:, 1, :].unsqueeze(2).broadcast_to([P, NCHK, CH]), op=ALU.subtract),
                     reads=['BB', 'mv'], writes=['BB'])
                S.op('act', lambda e: e.activation(out=EE[:], in_=BB[:], func=AF.Exp), reads=['BB'], writes=['EE'])
                emit_proj(((0, 'q'),) + (((3, 'g'),) if d == 0 else ()))
                for g4 in range(4):
                    bank = 4 + (g4 % 2)
                    def mmv(e, g4=g4, bank=bank, wt=wt):
                        for bl in range(4):
                            blk = g4 * 4 + bl
                            for k in range(KC):
                                ins = e.matmul(ps[:, bank, bl * P:(bl + 1) * P], HTs[:, k, blk * P:(blk + 1) * P], wt[:, 2, k, :],
                                               start=(k == 0), stop=(k == KC - 1))
                        return ins
                    S.op('pe', mmv, reads=[wn + '_2', 'HTs'], writes=['ps%d' % bank])
                    S.op('dve', lambda e, g4=g4, bank=bank: e.tensor_copy(out=VT[:, g4 * 4:(g4 + 1) * 4, :],
                                                                            in_=ps[:, bank, :].rearrange("p (b v) -> p b v", v=P)),
                         reads=['ps%d' % bank], writes=['VT%d' % g4])
                S.op('dve', lambda e: e.tensor_tensor(out=QT[:], in0=QS[:], in1=EE[:], op=ALU.mult), reads=allq + ['EE'], writes=['QT'])
                S.op('act', lambda e: e.activation(out=EE[:], in_=BB[:], func=AF.Exp, scale=-1.0), reads=['BB', 'QT'], writes=['EE'])
                S.op('dve', lambda e: e.tensor_tensor(out=KT[:], in0=SG[:], in1=EE[:], op=ALU.mult), reads=allsg + ['EE'], writes=['KT'])
                for hb in range(2):
                    def trk(e, hb=hb):
                        for c in range(8):
                            blk = hb * 8 + c
                            ins = e.transpose(out=pT[:, hb, c * P:(c + 1) * P], in_=KT[:, blk * P:(blk + 1) * P], identity=self.ident[:])
                        return ins
                    S.op('pe', trk, reads=['KT', 'ident'], writes=['pTs%d' % hb])
                    S.op('act', lambda e, hb=hb: e.activation(out=KTK[:, hb * 8:(hb + 1) * 8, :], in_=pT[:, hb, :].rearrange("p (c t) -> p c t", t=P),
                                                              func=AF.Copy), reads=['pTs%d' % hb], writes=['KTK%d' % hb])
                order = list(range(NCHK)) if d == 0 else list(reversed(range(NCHK)))
                Uj = self.U[d][:, j, :]
                def chunk_idx(n_i):
                    i = order[n_i]
                    return i, slice(i * CH, (i + 1) * CH), i // 2, slice((i % 2) * CH, (i % 2 + 1) * CH), n_i % 2

                def emit_front(n_i):
                    i, csl, blk, pr, sb = chunk_idx(n_i)
                    S.op('pe', lambda e, sb=sb, csl=csl: e.matmul(ps[0:CH, sb, 0:CH], KT[:, csl], QT[:, csl], start=True, stop=True),
                         reads=['KT', 'QT'], writes=['ps%d' % sb])
                    S.op('pe', lambda e, sb=sb, blk=blk, pr=pr: e.matmul(ps[:, 2 + sb, 0:P], KTK[pr, blk, :], VT[pr, blk, :], start=True, stop=True),
                         reads=['KTK%d' % (blk // 8), 'VT%d' % (blk // 4)], writes=['ps%d' % (2 + sb)])
                emit_front(0)
                for n_i in range(NCHK):
                    i, csl, blk, pr, sb = chunk_idx(n_i)
                    grp = i // 8
                    ob = 4 + (grp % 2)
                    oc = slice((i % 8) * CH, (i % 8 + 1) * CH)
                    sms, smn = SMS[sb], 'SMS%d' % sb
                    stl, stn = STL[sb], 'STL%d' % sb
                    S.op('dve', lambda e, sb=sb, pr=pr, sms=sms: e.tensor_tensor(out=sms[pr, :], in0=ps[0:CH, sb, 0:CH], in1=mask[:], op=ALU.mult),
                         reads=['ps%d' % sb, mres], writes=[smn])
                    Ur, Urn = (Uj, 'U') if n_i % 2 == 0 else (U1[:], 'U1')
                    Uw, Uwn = (U1[:], 'U1') if n_i % 2 == 0 else (Uj, 'U')
                    S.op('act', lambda e, i=i, stl=stl, Ur=Ur: e.activation(out=stl[:], in_=Ur, func=AF.Copy, scale=SML[:, 3, i:i + 1]),
                         reads=[Urn, 'cex'], writes=[stn])
                    S.op('dve', lambda e, i=i, sb=sb, Ur=Ur, Uw=Uw: e.scalar_tensor_tensor(out=Uw, in0=Ur, scalar=SML[:, 3, i:i + 1], in1=ps[:, 2 + sb, 0:P],
                                                                                       op0=ALU.mult, op1=ALU.add),
                         reads=[Urn, 'cex', 'ps%d' % (2 + sb)], writes=[Uwn])
                    if n_i + 1 < NCHK:
                        emit_front(n_i + 1)

                    def mmo(e, ob=ob, oc=oc, csl=csl, blk=blk, pr=pr, stl=stl, sms=sms):
                        e.matmul(ps[:, ob, oc], stl[:], QT[:, csl], start=True, stop=False)
                        return e.matmul(ps[:, ob, oc], VT[pr, blk, :], sms[pr, :], start=False, stop=True)
                    S.op('pe', mmo, reads=[stn, smn, 'QT', 'VT%d' % (blk // 4)], writes=['ps%d' % ob])
                    if n_i % 8 == 7:
                        S.op('act', lambda e, ob=ob, grp=grp: e.activation(out=OT[:, grp * 512:(grp + 1) * 512], in_=ps[:, ob, :], func=AF.Copy),
                             reads=['ps%d' % ob], writes=['OT%d' % grp])
                if d == 1:
                    S.dma('sp', lambda e, j=j: e.dma_start(out=self.OB[j, :, u * UT:(u + 1) * UT], in_=OT[:]),
                          reads=['OT%d' % g_ for g_ in range(4)])
                else:
                    allo = ['OT%d' % g_ for g_ in range(4)]
                    S.dma('sp', lambda e, j=j: e.dma_start(out=EE[:], in_=self.OB[j, :, u * UT:(u + 1) * UT]), writes=['EE'])
                    S.op('dve', lambda e: e.tensor_tensor(out=OT[:], in0=OT[:], in1=EE[:], op=ALU.add), reads=allo + ['EE'], writes=allo)
                    S.op('act', lambda e: e.activation(out=GG[:], in_=OT[:], func=AF.Square), reads=allo, writes=['GG'])
                    for tt in range(4):
                        tsl = slice(tt * TT, (tt + 1) * TT)
                        bank = tt % 2
                        S.op('pe', lambda e, bank=bank, tsl=tsl: e.matmul(ps[:, bank, :], self.ones_f[:], GG[:, tsl], start=True, stop=True),
                             reads=['ones_f', 'GG'], writes=['ps%d' % bank])
                        S.op('act', lambda e, bank=bank, tsl=tsl: e.activation(out=EE[:, tsl], in_=ps[:, bank, :], func=AF.Sqrt, scale=1.0 / P,
                                                                               bias=self.eps_rms[:, 0:1]),
                             reads=['ps%d' % bank, 'eps_rms'], writes=['EE'])
                    S.op('dve', lambda e: e.reciprocal(out=EE[:], in_=EE[:]), reads=['EE'], writes=['EE'])
                    S.op('dve', lambda e: e.scalar_tensor_tensor(out=OT[:], in0=OT[:], scalar=self.NG[:, l:l + 1], in1=EE[:], op0=ALU.mult, op1=ALU.mult),
                         reads=allo + ['EE', 'NG'], writes=allo)
                    S.op('dve', lambda e: e.tensor_tensor(out=AT[:], in0=OT[:], in1=HG[:], op=ALU.mult),
                         reads=allo + ['HG%d' % i for i in range(4)], writes=['AT'])
                    S.dma('sp', lambda e, j=j: e.dma_start(out=self.AB[:, j, :], in_=AT[:]), reads=['AT'])
            S.flush("sweep")

    def phase_attn(self, l, hin, t0, nu, u):
        S = self.S
        tu = t0 + u * UT
        has_prev = u > 0
        has_next = u < nu - 1
        NK = UT + 2 * P
        pos0 = u * UT - P
        with contextlib.ExitStack() as es:
            HTs = self.tile(es, "HTa", [P, KC, NK], BF16)
            COS = self.tile(es, "COS", [P, NK])
            SIN = self.tile(es, "SIN", [P, NK])
            WQ = Ring(es, self.nc, "WQ", [P, 4, KC, P], BF16, 1)
            WKV = self.tile(es, "WKV", [P, 4, KC, P], BF16)
            KR = self.tile(es, "KR", [P, 2, NK], BF16)
            VTt = self.tile(es, "VTt", [P, 18, 2 * P], BF16)
            QR = self.tile(es, "QR", [P, 4, UT], BF16)
            T1 = [self.tile(es, "T1%d" % i, [P, TT]) for i in range(2)]
            T2 = [self.tile(es, "T2%d" % i, [P, TT]) for i in range(2)]
            PT = [self.tile(es, "PT%d" % i, [P, 3, 512], BF16) for i in range(2)]
            RD = self.tile(es, "RD", [P, 512])
            BT = self.tile(es, "BT", [P, 4, UT], BF16)
            ps = self.psum(es, "psa", [P, 8, 512], F32)
            lo = 0 if has_prev else P
            hi = NK if has_next else NK - P
            S.dma('sp', lambda e: e.dma_start(out=HTs[:, :, lo:hi], in_=hin[:, :, tu - P + lo:tu - P + hi]), writes=['HTa'])
            S.dma('sp', lambda e: e.dma_start(out=COS[:, lo:hi], in_=self.c_cos[:, pos0 + lo:pos0 + hi]), writes=['COS'])
            S.dma('sp', lambda e: e.dma_start(out=SIN[:, lo:hi], in_=self.c_sin[:, pos0 + lo:pos0 + hi]), writes=['SIN'])
            wl = self.w_in[l]
            for i4 in range(4):
                self.wblk(WKV[:, i4, :, :], self.WB_in[l, OFF_AK // P + i4], 'WKV%d' % i4)
            rope_i = [0]

            def rope(bank, ncol, cs, out_ap, outres):
                k = rope_i[0] % 2
                rope_i[0] += 1
                t1, t2 = T1[k], T2[k]
                S.op('dve', lambda e: e.tensor_tensor(out=t1[:, 0:ncol], in0=ps[:, bank, 0:ncol], in1=COS[:, cs], op=ALU.mult),
                     reads=['ps%d' % bank, 'COS'], writes=['T1%d' % k])
                S.op('dve', lambda e: e.tensor_tensor(out=t2[0:64, 0:ncol], in0=ps[64:128, bank, 0:ncol], in1=SIN[64:128, cs], op=ALU.mult),
                     reads=['ps%d' % bank, 'SIN'], writes=['T2a%d' % k])
                S.op('dve', lambda e: e.tensor_tensor(out=t2[64:128, 0:ncol], in0=ps[0:64, bank, 0:ncol], in1=SIN[0:64, cs], op=ALU.mult),
                     reads=['ps%d' % bank, 'SIN'], writes=['T2b%d' % k])
                S.op('dve', lambda e: e.tensor_tensor(out=out_ap, in0=t1[:, 0:ncol], in1=t2[:, 0:ncol], op=ALU.add),
                     reads=['T1%d' % k, 'T2a%d' % k, 'T2b%d' % k], writes=[outres])
            segs = []
            if has_prev:
                segs.append((0, P))
            for tt in range(4):
                segs.append((P + tt * TT, TT))
            if has_next:
                segs.append((P + UT, P))
            bi = 0
            for kvh in range(2):
                for (c0, nc_) in segs:
                    bank = bi % 2
                    bi += 1
                    def mmk(e, kvh=kvh, c0=c0, nc_=nc_, bank=bank):
                        for k in range(KC):
                            ins = e.matmul(ps[:, bank, 0:nc_], WKV[:, kvh, k, :], HTs[:, k, c0:c0 + nc_], start=(k == 0), stop=(k == KC - 1))
                        return ins
                    S.op('pe', mmk, reads=['WKV%d' % kvh, 'HTa'], writes=['ps%d' % bank])
                    rope(bank, nc_, slice(c0, c0 + nc_), KR[:, kvh, c0:c0 + nc_], 'KR')
            blks = list(range(0 if has_prev else 1, 18 if has_next else 17))
            for gi in range(0, len(blks), 2):
                grp = blks[gi:gi + 2]
                bank = 2 + (gi // 2) % 2
                def mmv(e, grp=grp, bank=bank):
                    for bl, blk in enumerate(grp):
                        for k in range(KC):
                            ins = e.matmul(ps[:, bank, bl * 2 * P:(bl + 1) * 2 * P], HTs[:, k, blk * P:(blk + 1) * P], WKV[:, 2:4, k, :],
                                           start=(k == 0), stop=(k == KC - 1))
                    return ins
                S.op('pe', mmv, reads=['WKV2', 'WKV3', 'HTa'], writes=['ps%d' % bank])
                S.op('act', lambda e, grp=grp, bank=bank: e.activation(out=VTt[:, grp[0]:grp[0] + len(grp), :],
                                                                       in_=ps[:, bank, 0:len(grp) * 2 * P].rearrange("p (b v) -> p b v", v=2 * P), func=AF.Copy),
                     reads=['ps%d' % bank], writes=['VTt'])
            nxt = WQ.next()
            for i4 in range(4):
                self.wblk(nxt[0][:, i4, :, :], self.WB_in[l, OFF_AQ // P + i4], nxt[1] + '_%d' % i4)
            for kvh in range(2):
                wq, wqn = nxt
                for g in range(4):
                    for tt in range(4):
                        bank = bi % 2
                        bi += 1
                        def mmq(e, g=g, tt=tt, bank=bank, wq=wq):
                            for k in range(KC):
                                ins = e.matmul(ps[:, bank, :], wq[:, g, k, :], HTs[:, k, P + tt * TT:P + (tt + 1) * TT],
                                               start=(k == 0), stop=(k == KC - 1))
                            return ins
                        S.op('pe', mmq, reads=[wqn + '_%d' % g, 'HTa'], writes=['ps%d' % bank])
                        rope(bank, TT, slice(P + tt * TT, P + (tt + 1) * TT), QR[:, g, tt * TT:(tt + 1) * TT], 'QR')
                if kvh == 0:
                    nxt = WQ.next()
                    for i4 in range(4):
                        self.wblk(nxt[0][:, i4, :, :], self.WB_in[l, OFF_AQ // P + 4 + i4], nxt[1] + '_%d' % i4)
                for n in range(16):
                    kbs = []
                    if n > 0 or has_prev:
                        kbs.append((n, self.amp, 'amp'))
                    kbs.append((n + 1, None, None))
                    if n < 15 or has_next:
                        kbs.append((n + 2, self.amn, 'amn'))
                    pb = (n % 2) * 3
                    pt, ptn = PT[n % 2], 'PT%d' % (n % 2)
                    qap = QR[:, :, n * P:(n + 1) * P]
                    for ki, (kb, am, amres) in enumerate(kbs):
                        def mms(e, ki=ki, kb=kb, am=am, pb=pb, kvh=kvh, qap=qap):
                            ins = e.matmul(ps[:, pb + ki, :], KR[:, kvh, kb * P:(kb + 1) * P], qap, start=True, stop=(am is None))
                            if am is not None:
                                ins = e.matmul(ps[:, pb + ki, :], self.ident[:], am[:].unsqueeze(1).broadcast_to([P, 4, P]), start=False, stop=True)
                            return ins
                        S.op('pe', mms, reads=['KR', 'QR', 'ident'] + ([amres] if am is not None else []), writes=['ps%d' % (pb + ki)])
                        S.op('act', lambda e, ki=ki, pb=pb, pt=pt: e.activation(out=pt[:, ki, :], in_=ps[:, pb + ki, :], func=AF.Exp, scale=QSCALE),
                             reads=['ps%d' % (pb + ki)], writes=[ptn + '_%d' % ki])
                    nk = len(kbs)

                    def mmd(e, nk=nk, pt=pt):
                        for ki in range(nk):
                            ins = e.matmul(ps[:, 6, :], self.ones_bf[:], pt[:, ki, :], start=(ki == 0), stop=(ki == nk - 1))
                        return ins
                    S.op('pe', mmd, reads=['ones_bf'] + [ptn + '_%d' % ki for ki in range(nk)], writes=['ps6'])

                    def mmpv(e, kbs=kbs, pt=pt, kvh=kvh):
                        for ki, (kb, _, _) in enumerate(kbs):
                            ins = e.matmul(ps[:, 7, :], VTt[:, kb, kvh * P:(kvh + 1) * P], pt[:, ki, :], start=(ki == 0), stop=(ki == len(kbs) - 1))
                        return ins
                    S.op('pe', mmpv, reads=['VTt'] + [ptn + '_%d' % ki for ki in range(nk)], writes=['ps7'])
                    es_ap = self.ES[:, l * 8 + kvh * 4:l * 8 + kvh * 4 + 4].unsqueeze(2).broadcast_to([P, 4, P])
                    S.op('dve', lambda e, es_ap=es_ap: e.tensor_tensor(out=RD[:].rearrange("p (g q) -> p g q", q=P), in0=ps[:, 6, :].rearrange("p (g q) -> p g q", q=P),
                                                                       in1=es_ap, op=ALU.add), reads=['ps6', 'ES'], writes=['RD'])
                    S.op('dve', lambda e: e.reciprocal(out=RD[:], in_=RD[:]), reads=['RD'], writes=['RD'])
                    S.op('dve', lambda e, n=n: e.tensor_tensor(out=BT[:, :, n * P:(n + 1) * P], in0=ps[:, 7, :].rearrange("p (g q) -> p g q", q=P),
                                                               in1=RD[:].rearrange("p (g q) -> p g q", q=P), op=ALU.mult),
                         reads=['ps7', 'RD'], writes=['BT'])
                S.dma('sp', lambda e, kvh=kvh: e.dma_start(out=self.AB[:, 8 + kvh * 4:8 + kvh * 4 + 4, :], in_=BT[:]), reads=['BT'])
            S.flush("attn")

    def phase_tail(self, l, hin, hout, t0, u, last):
        S = self.S
        tu = t0 + u * UT
        with contextlib.ExitStack() as es:
            HTt = self.tile(es, "HTt", [P, KC, TT], BF16)
            ABt = self.tile(es, "ABt", [P, KC, TT], BF16)
            MT = self.tile(es, "MT", [P, KC, TT], BF16)
            X = self.tile(es, "Xt", [P, 4, D])
            RES = Ring(es, self.nc, "RES", [P, 1024], F32, 2)
            ACT_T = self.tile(es, "ACTT", [P, FC, TT], BF16)
            G = self.tile(es, "Gt", [P, D])
            Bt = self.tile(es, "Btt", [P, D])
            WG = Ring(es, self.nc, "WG", [P, 2, KC, P], BF16, 2)
            WAB = Ring(es, self.nc, "WAB", [P, 2, 8, P], BF16, 2)
            WK = Ring(es, self.nc, "WK", [P, 1024], BF16, 4)
            SGA = [self.tile(es, "SGA%d" % i, [P, TT]) for i in range(2)]
            SGB = [self.tile(es, "SGB%d" % i, [P, TT]) for i in range(2)]
            XB = [self.tile(es, "XBt%d" % i, [P, D], BF16) for i in range(2)]
            SC = [self.tile(es, "SCt%d" % i, [P, 32]) for i in range(2)]
            ps = self.psum(es, "pst", [P, 8, 512], F32)
            wl = self.w_in[l]
            for tt in range(4):
                tk = tu + tt * TT
                S.dma('sp', lambda e, tk=tk: e.dma_start(out=HTt[:], in_=hin[:, :, tk:tk + TT]), writes=['HTt'])
                S.dma('sp', lambda e, tt=tt: e.dma_start(out=ABt[:], in_=self.AB[:, :, tt * TT:(tt + 1) * TT]), writes=['ABt'])
                self.load_gb(G, Bt, self.ln1_g[l], self.ln1_b[l], 'GBt')

                def load_merge(c):
                    wg, wgn = WG.next()
                    self.wblk(wg[:, 0, :, :], self.WB_in[l, OFF_GA // P + c], wgn + '_0')
                    self.wblk(wg[:, 1, :, :], self.WB_in[l, OFF_GB // P + c], wgn + '_1')
                    wab, wabn = WAB.next()
                    self.wblk(wab[:, 0, :, :], self.WB_a[l, c], wabn + '_0')
                    self.wblk(wab[:, 1, :, :], self.WB_b[l, c], wabn + '_1')
                    return wg, wgn, wab, wabn
                nxt = load_merge(0)
                for c in range(KC):
                    wg, wgn, wab, wabn = nxt
                    if c + 1 < KC:
                        nxt = load_merge(c + 1)
                    pb = (c % 2) * 4
                    for gi in range(2):
                        def mmg(e, gi=gi, pb=pb, wg=wg):
                            for k in range(KC):
                                ins = e.matmul(ps[:, pb + gi, :], wg[:, gi, k, :], HTt[:, k, :], start=(k == 0), stop=(k == KC - 1))
                            return ins
                        S.op('pe', mmg, reads=[wgn + '_%d' % gi, 'HTt'], writes=['ps%d' % (pb + gi)])

                        def mmb(e, gi=gi, pb=pb, wab=wab):
                            for k in range(8):
                                ins = e.matmul(ps[:, pb + 2 + gi, :], wab[:, gi, k, :], ABt[:, gi * 8 + k, :], start=(k == 0), stop=(k == 7))
                            return ins
                        S.op('pe', mmb, reads=[wabn + '_%d' % gi, 'ABt'], writes=['ps%d' % (pb + 2 + gi)])
                    k2 = c % 2
                    sga, sgb = SGA[k2], SGB[k2]
                    S.op('act', lambda e, pb=pb, sga=sga: e.activation(out=sga[:], in_=ps[:, pb, :], func=AF.Sigmoid),
                         reads=['ps%d' % pb], writes=['SGA%d' % k2])
                    S.op('act', lambda e, pb=pb, sgb=sgb: e.activation(out=sgb[:], in_=ps[:, pb + 1, :], func=AF.Sigmoid),
                         reads=['ps%d' % (pb + 1)], writes=['SGB%d' % k2])
                    S.op('dve', lambda e, pb=pb, sga=sga: e.tensor_tensor(out=sga[:], in0=sga[:], in1=ps[:, pb + 2, :], op=ALU.mult),
                         reads=['SGA%d' % k2, 'ps%d' % (pb + 2)], writes=['SGA%d' % k2])
                    S.op('dve', lambda e, pb=pb, sgb=sgb: e.tensor_tensor(out=sgb[:], in0=sgb[:], in1=ps[:, pb + 3, :], op=ALU.mult),
                         reads=['SGB%d' % k2, 'ps%d' % (pb + 3)], writes=['SGB%d' % k2])
                    S.op('dve', lambda e, c=c, sga=sga, sgb=sgb: e.tensor_tensor(out=MT[:, c, :], in0=sga[:], in1=sgb[:], op=ALU.add),
                         reads=['SGA%d' % k2, 'SGB%d' % k2], writes=['MT%d' % c])
                allmt = ['MT%d' % c for c in range(KC)]

                def tm_proj(wsrc, nk, lhs_tile, lhs_res, res_from_dram):
                    for half in range(2):
                        hs = slice(half * 1024, (half + 1) * 1024)

                        def load_k(k):
                            wk, wkn = WK.next()
                            S.dma('pool', lambda e, k=k, wk=wk, hs=hs: e.dma_start(out=wk[:], in_=wsrc[k * P:(k + 1) * P, hs]), writes=[wkn])
                            return wk, wkn
                        q = [load_k(0), load_k(1), load_k(2)]
                        for k in range(nk):
                            wk, wkn = q.pop(0)
                            if k + 3 < nk:
                                q.append(load_k(k + 3))

                            def mmt(e, k=k, wk=wk):
                                for blk in range(4):
                                    for nb in range(2):
                                        ins = e.matmul(ps[:, blk * 2 + nb, :], lhs_tile[:, k, blk * P:(blk + 1) * P], wk[:, nb * 512:(nb + 1) * 512],
                                                       start=(k == 0), stop=(k == nk - 1))
                                return ins
                            S.op('pe', mmt, reads=[wkn] + lhs_res, writes=['ps%d' % b for b in range(8)])
                        for blk in range(4):
                            if res_from_dram:
                                rt, rn = RES.next()
                                S.dma('sp', lambda e, blk=blk, rt=rt, hs=hs, tk=tk: e.dma_start(out=rt[:], in_=self.H[tk + blk * P:tk + (blk + 1) * P, hs]), writes=[rn])
                                S.op('dve', lambda e, blk=blk, rt=rt, hs=hs: e.scalar_tensor_tensor(out=X[:, blk, hs], in0=rt[:], scalar=ALPHA,
                                                                                            in1=ps[:, blk * 2:blk * 2 + 2, :].rearrange("p a b -> p (a b)"),
                                                                                            op0=ALU.mult, op1=ALU.add),
                                     reads=[rn, 'ps%d' % (blk * 2), 'ps%d' % (blk * 2 + 1)], writes=['X%d_%d' % (blk, half)])
                            else:
                                S.op('dve', lambda e, blk=blk, hs=hs: e.scalar_tensor_tensor(out=X[:, blk, hs], in0=X[:, blk, hs], scalar=ALPHA,
                                                                                     in1=ps[:, blk * 2:blk * 2 + 2, :].rearrange("p a b -> p (a b)"),
                                                                                     op0=ALU.mult, op1=ALU.add),
                                     reads=['X%d_%d' % (blk, half), 'ps%d' % (blk * 2), 'ps%d' % (blk * 2 + 1)], writes=['X%d_%d' % (blk, half)])
                tm_proj(self.WB_o[l], KC, MT, allmt, True)
                for blk in range(4):
                    k2 = blk % 2
                    xr = ['X%d_0' % blk, 'X%d_1' % blk]
                    self.ln_block(X[:, blk, :], xr, G, Bt, 'GBt', SC[k2], 'SCt%d' % k2)
                    self.tail_transpose(X[:, blk, :], xr, XB[k2], 'XBt%d' % k2, ps, MT, blk, 'H1T')
                h1t = ['H1Tb%d_%d' % (b, hb) for b in range(4) for hb in range(2)]
                self.load_gb(G, Bt, self.ln2_g[l], self.ln2_b[l], 'GBt')
                wf = self.w_ffn_in[l]

                def load_ffn(c):
                    wg, wgn = WG.next()
                    self.wblk(wg[:, 0, :, :], self.WB_fi[l, c], wgn + '_0')
                    self.wblk(wg[:, 1, :, :], self.WB_fi[l, FC + c], wgn + '_1')
                    return wg, wgn
                nxt = load_ffn(0)
                for c in range(FC):
                    wg, wgn = nxt
                    if c + 1 < FC:
                        nxt = load_ffn(c + 1)
                    pb = (c % 4) * 2
                    for gi in range(2):
                        def mmf(e, gi=gi, pb=pb, wg=wg):
                            for k in range(KC):
                                ins = e.matmul(ps[:, pb + gi, :], wg[:, gi, k, :], MT[:, k, :], start=(k == 0), stop=(k == KC - 1))
                            return ins
                        S.op('pe', mmf, reads=[wgn + '_%d' % gi] + h1t, writes=['ps%d' % (pb + gi)])
                    k2 = c % 2
                    sga = SGA[k2]
                    S.op('act', lambda e, pb=pb, sga=sga: e.activation(out=sga[:], in_=ps[:, pb, :], func=AF.Silu),
                         reads=['ps%d' % pb], writes=['SGA%d' % k2])
                    S.op('dve', lambda e, pb=pb, sga=sga, c=c: e.tensor_tensor(out=ACT_T[:, c, :], in0=sga[:], in1=ps[:, pb + 1, :], op=ALU.mult),
                         reads=['SGA%d' % k2, 'ps%d' % (pb + 1)], writes=['ACT%d' % c])
                tm_proj(self.WB_fo[l], FC, ACT_T, ['ACT%d' % c for c in range(FC)], False)
                for blk in range(4):
                    k2 = blk % 2
                    xr = ['X%d_0' % blk, 'X%d_1' % blk]
                    self.ln_block(X[:, blk, :], xr, G, Bt, 'GBt', SC[k2], 'SCt%d' % k2)
                    if not last:
                        self.tail_transpose(X[:, blk, :], xr, XB[k2], 'XBt%d' % k2, ps, HTt, blk, 'H2T')
                dst = self.y if last else self.H
                S.dma('sp', lambda e, tk=tk: e.dma_start(out=dst[tk:tk + TT, :].rearrange("(b p) d -> p b d", p=P), in_=X[:]),
                      reads=['X%d_%d' % (b, h_) for b in range(4) for h_ in range(2)])
                if not last:
                    S.dma('sp', lambda e, tk=tk: e.dma_start(out=hout[:, :, tk:tk + TT], in_=HTt[:]),
                          reads=['H2Tb%d_%d' % (b, hb) for b in range(4) for hb in range(2)])
            S.flush("tail")

    def tail_transpose(self, xap, xres, xb, xbres, ps, hto, blk, htres):
        S = self.S
        S.op('act', lambda e: e.activation(out=xb[:], in_=xap, func=AF.Copy), reads=xres, writes=[xbres])
        for hb in range(2):
            pv = ps[:, hb, :].bitcast(BF16)

            def tr(e, hb=hb, pv=pv):
                for c in range(8):
                    cc = hb * 8 + c
                    ins = e.transpose(out=pv[:, c * P:(c + 1) * P], in_=xb[:, cc * P:(cc + 1) * P], identity=self.ident[:])
                return ins
            S.op('pe', tr, reads=[xbres, 'ident'], writes=['ps%d' % hb])
            if hb == 0:
                S.op('dve', lambda e, hb=hb, pv=pv: e.tensor_copy(out=hto[:, hb * 8:(hb + 1) * 8, blk * P:(blk + 1) * P],
                                                                   in_=pv.rearrange("p (c t) -> p c t", t=P)),
                     reads=['ps%d' % hb], writes=[htres + 'b%d_%d' % (blk, hb)])
            else:
                S.op('act', lambda e, hb=hb, pv=pv: e.activation(out=hto[:, hb * 8:(hb + 1) * 8, blk * P:(blk + 1) * P],
                                                                  in_=pv.rearrange("p (c t) -> p c t", t=P), func=AF.Copy),
                     reads=['ps%d' % hb], writes=[htres + 'b%d_%d' % (blk, hb)])


def const_tables():
    inv = 1.0 / (10000.0 ** (np.arange(0, 128, 2, dtype=np.float32) / 128.0))
    ang = np.arange(LMAX, dtype=np.float32)[None, :] * inv[:, None].astype(np.float32)
    ang = ang.astype(np.float32)
    cos, sin = np.cos(ang).astype(np.float32), np.sin(ang).astype(np.float32)
    c_cos = np.concatenate([cos, cos], 0)
    c_sin = np.concatenate([sin, -sin], 0)
    s = np.arange(CH)[:, None]
    t = np.arange(CH)[None, :]
    hmf = (s <= t).astype(np.float32)
    hmb = (s >= t).astype(np.float32)
    j = np.arange(P)[:, None]
    i = np.arange(P)[None, :]
    amp = np.where(j >= i, 0.0, NEG).astype(ml_dtypes.bfloat16)
    amn = np.where(j <= i, 0.0, NEG).astype(ml_dtypes.bfloat16)
    return dict(c_cos=np.ascontiguousarray(c_cos), c_sin=np.ascontiguousarray(c_sin), c_ident=np.eye(P).astype(ml_dtypes.bfloat16),
                c_hmf=hmf, c_hmb=hmb, c_amp=amp, c_amn=amn)


_WNAMES = ["ln_in_g", "ln_in_b", "w_in", "lb_logits", "hg_norm_g", "attn_sink", "w_branch_a", "w_branch_b", "w_out",
           "ln1_g", "ln1_b", "w_ffn_in", "w_ffn_out", "ln2_g", "ln2_b"]


def run_cores(xs, weights, seq_units, depth):
    b = Builder(seq_units, depth)
    nc = b.build()
    consts = const_tables()
    in_maps = []
    for x in xs:
        m = {"x": np.ascontiguousarray(x, dtype=np.float32)}
        for k in _WNAMES:
            m[k] = np.ascontiguousarray(np.asarray(weights[k], dtype=np.float32))
        m.update(consts)
        in_maps.append(m)
    res = run_bass_kernel_spmd(nc, in_maps, core_ids=list(range(len(xs))))
    return [r["y"] for r in res.results]


def kernel(x_prompt, x_sample, **weights):
    x_prompt = np.asarray(x_prompt, dtype=np.float32)
    x_sample = np.asarray(x_sample, dtype=np.float32)
    weights = {k: np.asarray(v) for k, v in weights.items()}
    n = 8
    B, Ls, _ = x_prompt.shape
    per = B // n
    xs = []
    for c in range(n):
        xs.append(np.concatenate([x_prompt[c * per:(c + 1) * per].reshape(per * Ls, D), x_sample[0]], axis=0))
    nsu = x_sample.shape[1] // UT
    ys = run_cores(xs, weights, [1] * per + [nsu], DEPTH)
    y_prompt = np.empty_like(x_prompt)
    y_sample = np.empty_like(x_sample)
    seg = x_sample.shape[1] // n
    for c in range(n):
        y_prompt[c * per:(c + 1) * per] = ys[c][:per * Ls].reshape(per, Ls, D)
        y_sample[0, c * seg:(c + 1) * seg] = ys[c][per * Ls + c * seg:per * Ls + (c + 1) * seg]
    return (y_prompt, y_sample)
```

```python
import contextlib
import numpy as np
import ml_dtypes
import concourse.bass as bass
import concourse.mybir as mybir
from concourse.bass_utils import run_bass_kernel_spmd

F32 = mybir.dt.float32
BF16 = mybir.dt.bfloat16
AF = mybir.ActivationFunctionType
ALU = mybir.AluOpType

P = 128
D = 2048
KC = 16
UT = 2048
TT = 512
CH = 64
NCHK = UT // CH
DFF = 5632
FC = DFF // P
IN_W = 10752
OFF_Q, OFF_FF, OFF_FB, OFF_I, OFF_G = 0, 1024, 2048, 3072, 4096
OFF_AQ, OFF_AK, OFF_AV, OFF_GA, OFF_GB = 5120, 6144, 6400, 6656, 8704
DEPTH = 4
ALPHA = float((2 * DEPTH) ** 0.25)
LN_EPS = 1e-5
RMS_EPS = 1e-6
QSCALE = float(128 ** -0.5)
NEG = -30000.0
LMAX = 8192

COMPUTE = ('pe', 'act', 'dve', 'pool')
_UID = [0]


class Sched:
    def __init__(self, nc, es, n_dma_sems=12):
        self.nc = nc
        self.sem = {}
        for e in COMPUTE:
            self.sem[e] = es.enter_context(nc.semaphore("s_" + e))
        self.queues = ('sp', 'pool')
        self.dma_sems = {}
        for q in self.queues:
            self.dma_sems[q] = []
            for i in range(n_dma_sems):
                k = "d_%s_%d" % (q, i)
                self.sem[k] = es.enter_context(nc.semaphore(k))
                self.dma_sems[q].append(k)
        self.val = {k: 0 for k in self.sem}
        self.dma_rr = {q: 0 for q in self.queues}
        self.streams = {e: [] for e in ('pe', 'act', 'dve', 'pool', 'sp')}
        self.known = {e: {} for e in self.streams}
        self.res = {}
        self.n_inst = 0

    def _deps(self, reads, writes):
        deps = {}

        def add(tok):
            if tok is None:
                return
            k, v = tok
            if deps.get(k, 0) < v:
                deps[k] = v
        for r in reads:
            st = self.res.get(r)
            if st:
                add(st['w'])
        for w in writes:
            st = self.res.get(w)
            if st:
                add(st['w'])
                for k, v in st['r'].items():
                    add((k, v))
        return deps

    def _commit(self, tok, reads, writes):
        k, v = tok
        for r in reads:
            st = self.res.setdefault(r, {'w': None, 'r': {}})
            if st['r'].get(k, 0) < v:
                st['r'][k] = v
        for w in writes:
            self.res[w] = {'w': tok, 'r': {}}

    def _waits(self, stream, deps, skip_self=None):
        out = []
        kn = self.known[stream]
        for k, v in deps.items():
            if k == skip_self:
                continue
            if kn.get(k, 0) < v:
                kn[k] = v
                out.append((k, v))
        return out

    def op(self, eng, fn, reads=(), writes=()):
        deps = self._deps(reads, writes)
        waits = self._waits(eng, deps, skip_self='pe' if eng == 'pe' else None)
        self.val[eng] += 1
        tok = (eng, self.val[eng])
        self.streams[eng].append((waits, fn, tok, 1))
        self._commit(tok, reads, writes)
        return tok

    def dma(self, q, fn, reads=(), writes=()):
        deps = self._deps(reads, writes)
        sems = self.dma_sems[q]
        k = sems[self.dma_rr[q] % len(sems)]
        self.dma_rr[q] += 1
        if self.val[k] > 0:
            deps[k] = max(deps.get(k, 0), self.val[k])
        waits = self._waits(q, deps)
        self.val[k] += 16
        tok = (k, self.val[k])
        self.streams[q].append((waits, fn, tok, 16))
        self._commit(tok, reads, writes)
        return tok

    def flush(self, name=None):
        deps = {}
        for q in self.queues:
            for k in self.dma_sems[q]:
                if self.val[k] > 0:
                    deps[k] = self.val[k]
        waits = self._waits('sp', dict(deps))
        if waits:
            self.streams['sp'].append((waits, None, None, 0))
        streams = self.streams
        sem = self.sem
        cnt = [0]

        def replay(engine, lst):
            for waits, fn, tok, inc in lst:
                for k, v in waits:
                    engine.wait_ge(sem[k], v)
                    cnt[0] += 1
                if fn is not None:
                    ins = fn(engine)
                    ins.then_inc(sem[tok[0]], inc)
                    cnt[0] += 1

        _UID[0] += 1
        with self.nc.Block("%s_%d" % (name or "blk", _UID[0])) as block:
            @block.tensor
            def _(e):
                replay(e, streams['pe'])

            @block.scalar
            def _(e):
                replay(e, streams['act'])

            @block.vector
            def _(e):
                replay(e, streams['dve'])

            @block.gpsimd
            def _(e):
                replay(e, streams['pool'])

            @block.sync
            def _(e):
                replay(e, streams['sp'])
        self.n_inst += cnt[0]
        self.streams = {e: [] for e in streams}
        self.res = {}
        for e in self.known:
            for k in self.val:
                self.known[e][k] = self.val[k]


class Ring:
    def __init__(self, es, nc, name, shape, dt, n):
        _UID[0] += 1
        self.tiles = [es.enter_context(nc.sbuf_tensor("%s%d_%d" % (name, i, _UID[0]), shape, dt)) for i in range(n)]
        self.names = ["%s%d" % (name, i) for i in range(n)]
        self.i = 0

    def next(self):
        t, n = self.tiles[self.i % len(self.tiles)], self.names[self.i % len(self.tiles)]
        self.i += 1
        return t, n


class Builder:
    def __init__(self, seq_units, depth, debug=False, stop_after=None):
        self.debug = debug
        self.stop_after = stop_after
        self.seq_units = list(seq_units)
        self.depth = depth
        self.T = sum(seq_units) * UT
        self.lmax = max(seq_units) * UT
        nc = bass.Bass("TRN2", target_bir_lowering=False)
        self.nc = nc
        T, L = self.T, depth

        def din(name, shape, dt=F32):
            return nc.dram_tensor(name, shape, dt, kind="ExternalInput").ap()

        def dscr(name, shape, dt=F32):
            return nc.dram_tensor(name, shape, dt, kind="ExternalOutput" if debug else "Internal").ap()
        self.x = din("x", [T, D])
        self.ln_in_g = din("ln_in_g", [D])
        self.ln_in_b = din("ln_in_b", [D])
        self.w_in = din("w_in", [L, D, IN_W])
        self.lb_logits = din("lb_logits", [L, 2048])
        self.hg_norm_g = din("hg_norm_g", [L, P])
        self.attn_sink = din("attn_sink", [L, 8])
        self.w_branch_a = din("w_branch_a", [L, 1024, D])
        self.w_branch_b = din("w_branch_b", [L, 1024, D])
        self.w_out = din("w_out", [L, D, D])
        self.ln1_g = din("ln1_g", [L, D])
        self.ln1_b = din("ln1_b", [L, D])
        self.w_ffn_in = din("w_ffn_in", [L, D, 2 * DFF])
        self.w_ffn_out = din("w_ffn_out", [L, DFF, D])
        self.ln2_g = din("ln2_g", [L, D])
        self.ln2_b = din("ln2_b", [L, D])
        self.c_cos = din("c_cos", [P, LMAX])
        self.c_sin = din("c_sin", [P, LMAX])
        self.c_ident = din("c_ident", [P, P], BF16)
        self.c_hmf = din("c_hmf", [CH, CH])
        self.c_hmb = din("c_hmb", [CH, CH])
        self.c_amp = din("c_amp", [P, P], BF16)
        self.c_amn = din("c_amn", [P, P], BF16)
        self.y = nc.dram_tensor("y", [T, D], F32, kind="ExternalOutput").ap()
        self.H = dscr("H", [T, D])
        self.HT = [dscr("HT%d" % i, [P, KC, T], BF16) for i in range(2)]
        self.OB = dscr("OB", [8, P, self.lmax])
        self.AB = dscr("AB", [P, KC, UT], BF16)
        self.WB_in = dscr("WB_in", [L, IN_W // P, P, KC, P], BF16)
        self.WB_fi = dscr("WB_fi", [L, 2 * FC, P, KC, P], BF16)
        self.WB_a = dscr("WB_a", [L, KC, P, 8, P], BF16)
        self.WB_b = dscr("WB_b", [L, KC, P, 8, P], BF16)
        self.WB_o = dscr("WB_o", [L, D, D], BF16)
        self.WB_fo = dscr("WB_fo", [L, DFF, D], BF16)

    def tile(self, es, name, shape, dt=F32):
        _UID[0] += 1
        return es.enter_context(self.nc.sbuf_tensor("%s_%d" % (name, _UID[0]), shape, dt))

    def psum(self, es, name, shape, dt=F32):
        _UID[0] += 1
        return es.enter_context(self.nc.psum_tensor("%s_%d" % (name, _UID[0]), shape, dt))

    def build(self):
        nc = self.nc
        with contextlib.ExitStack() as g:
            self.S = Sched(nc, g)
            S = self.S
            L = self.depth
            self.ident = self.tile(g, "ident", [P, P], BF16)
            self.ones_bf = self.tile(g, "ones_bf", [P, P], BF16)
            self.ones_f = self.tile(g, "ones_f", [P, P], F32)
            self.hmf = self.tile(g, "hmf", [CH, CH], F32)
            self.hmb = self.tile(g, "hmb", [CH, CH], F32)
            self.amp = self.tile(g, "amp", [P, P], BF16)
            self.amn = self.tile(g, "amn", [P, P], BF16)
            self.eps_ln = self.tile(g, "eps_ln", [P, 1], F32)
            self.eps_rms = self.tile(g, "eps_rms", [P, 1], F32)
            self.LB = self.tile(g, "LB", [P, L, 16], F32)
            self.OML = self.tile(g, "OML", [P, L, 16], F32)
            self.LBM1 = self.tile(g, "LBM1", [P, L, 16], F32)
            self.NG = self.tile(g, "NG", [P, L], F32)
            self.ES = self.tile(g, "ES", [P, L * 8], F32)
            self.U = [self.tile(g, "U%d" % d, [P, 8, P], F32) for d in range(2)]
            self.Est = [self.tile(g, "Est%d" % d, [P, 8], F32) for d in range(2)]
            self.phase_setup()
            self.phase_preconvert()
            self.phase_ln_in()
            t0 = 0
            for l in range(L):
                hin, hout = self.HT[l % 2], self.HT[(l + 1) % 2]
                t0 = 0
                for nu in self.seq_units:
                    for u in reversed(range(nu)):
                        self.phase_sweep(l, hin, t0, nu, u, 1)
                    for u in range(nu):
                        self.phase_sweep(l, hin, t0, nu, u, 0)
                        self.phase_attn(l, hin, t0, nu, u)
                        self.phase_tail(l, hin, hout, t0, u, last=(l == L - 1))
                    t0 += nu * UT
        return nc

    def phase_setup(self):
        S, L = self.S, self.depth
        with contextlib.ExitStack() as es:
            lg = self.tile(es, "lg", [P, L, 16])
            ex = self.tile(es, "ex", [P, L, 16])
            mx = self.tile(es, "mx", [P, 16])
            sm = self.tile(es, "sm", [P, 16])
            S.dma('sp', lambda e: e.dma_start(out=self.ident[:], in_=self.c_ident), writes=['ident'])
            S.dma('sp', lambda e: e.dma_start(out=self.hmf[:], in_=self.c_hmf), writes=['hmf'])
            S.dma('sp', lambda e: e.dma_start(out=self.hmb[:], in_=self.c_hmb), writes=['hmb'])
            S.dma('sp', lambda e: e.dma_start(out=self.amp[:], in_=self.c_amp), writes=['amp'])
            S.dma('sp', lambda e: e.dma_start(out=self.amn[:], in_=self.c_amn), writes=['amn'])
            S.dma('sp', lambda e: e.dma_start(out=lg[:], in_=self.lb_logits.rearrange("l (c p) -> p l c", p=P),
                                              allow_slow_non_contiguous=True), writes=['lg'])
            S.dma('sp', lambda e: e.dma_start(out=self.NG[:], in_=self.hg_norm_g.rearrange("l p -> p l"),
                                              allow_slow_non_contiguous=True), writes=['NG'])
            S.dma('sp', lambda e: e.dma_start(out=self.ES[:], in_=self.attn_sink.rearrange("l h -> (l h)").partition_broadcast(P)),
                  writes=['ES'])
            S.op('pool', lambda e: e.memset(self.ones_bf[:], 1.0), writes=['ones_bf'])
            S.op('pool', lambda e: e.memset(self.ones_f[:], 1.0), writes=['ones_f'])
            S.op('pool', lambda e: e.memset(self.eps_ln[:], LN_EPS), writes=['eps_ln'])
            S.op('pool', lambda e: e.memset(self.eps_rms[:], RMS_EPS), writes=['eps_rms'])
            S.op('act', lambda e: e.activation(out=self.ES[:], in_=self.ES[:], func=AF.Exp), reads=['ES'], writes=['ES'])
            S.op('dve', lambda e: e.tensor_copy(out=mx[:], in_=lg[:, 0, :]), reads=['lg'], writes=['mx'])
            for l in range(1, L):
                S.op('dve', lambda e, l=l: e.tensor_tensor(out=mx[:], in0=mx[:], in1=lg[:, l, :], op=ALU.max),
                     reads=['lg', 'mx'], writes=['mx'])
            for l in range(L):
                S.op('dve', lambda e, l=l: e.tensor_tensor(out=ex[:, l, :], in0=lg[:, l, :], in1=mx[:], op=ALU.subtract),
                     reads=['lg', 'mx'], writes=['ex'])
            S.op('act', lambda e: e.activation(out=ex[:], in_=ex[:], func=AF.Exp), reads=['ex'], writes=['ex'])
            S.op('dve', lambda e: e.tensor_copy(out=sm[:], in_=ex[:, 0, :]), reads=['ex'], writes=['sm'])
            for l in range(1, L):
                S.op('dve', lambda e, l=l: e.tensor_tensor(out=sm[:], in0=sm[:], in1=ex[:, l, :], op=ALU.add),
                     reads=['ex', 'sm'], writes=['sm'])
            S.op('dve', lambda e: e.reciprocal(out=sm[:], in_=sm[:]), reads=['sm'], writes=['sm'])
            S.op('pool', lambda e: e.memset(self.LB[:, 0, :], 0.0), writes=['LB'])
            for l in range(1, L):
                S.op('dve', lambda e, l=l: e.tensor_tensor(out=ex[:, l, :], in0=ex[:, l, :], in1=sm[:], op=ALU.mult),
                     reads=['ex', 'sm'], writes=['ex'])
                S.op('dve', lambda e, l=l: e.tensor_tensor(out=self.LB[:, l, :], in0=self.LB[:, l - 1, :], in1=ex[:, l, :], op=ALU.add),
                     reads=['ex', 'LB'], writes=['LB'])
            S.op('dve', lambda e: e.tensor_scalar(out=self.OML[:], in0=self.LB[:], scalar1=-1.0, scalar2=1.0, op0=ALU.mult, op1=ALU.add),
                 reads=['LB'], writes=['OML'])
            S.op('dve', lambda e: e.tensor_scalar(out=self.LBM1[:], in0=self.LB[:], scalar1=-1.0, scalar2=None, op0=ALU.add),
                 reads=['LB'], writes=['LBM1'])
            S.flush("setup")

    def ln_block(self, xap, xres, G, Bt, gres, sc, scres, eng_b='dve'):
        S = self.S
        st = sc[:, 0:24]
        ag = sc[:, 24:26]
        sd = sc[:, 26:27]
        rs = sc[:, 27:28]
        nm = sc[:, 28:29]
        for c in range(4):
            S.op('dve', lambda e, c=c: e.bn_stats(out=sc[:, c * 6:(c + 1) * 6], in_=xap[:, c * 512:(c + 1) * 512]),
                 reads=xres, writes=[scres + 's%d' % c])
        S.op('dve', lambda e: e.bn_aggr(out=ag, in_=st), reads=[scres + 's%d' % c for c in range(4)], writes=[scres + 'ag'])
        S.op('act', lambda e: e.activation(out=sd, in_=sc[:, 25:26], func=AF.Sqrt, bias=self.eps_ln[:, 0:1], scale=1.0),
             reads=[scres + 'ag'], writes=[scres + 'sd'])
        S.op('dve', lambda e: e.reciprocal(out=rs, in_=sd), reads=[scres + 'sd'], writes=[scres + 'rs'])
        S.op('dve', lambda e: e.scalar_tensor_tensor(out=nm, in0=sc[:, 24:25], scalar=-1.0, in1=rs, op0=ALU.mult, op1=ALU.mult),
             reads=[scres + 'ag', scres + 'rs'], writes=[scres + 'nm'])
        S.op('act', lambda e: e.activation(out=xap, in_=xap, func=AF.Identity, scale=rs, bias=nm),
             reads=xres + [scres + 'rs', scres + 'nm'], writes=xres)
        S.op('dve', lambda e: e.tensor_tensor(out=xap, in0=xap, in1=G[:], op=ALU.mult), reads=xres + [gres], writes=xres)
        S.op(eng_b, lambda e: e.tensor_tensor(out=xap, in0=xap, in1=Bt[:], op=ALU.add), reads=xres + [gres], writes=xres)

    def transpose_block(self, xap, xres, xb, xbres, pT, pTres, hto, blk, htres):
        S = self.S
        S.op('act', lambda e: e.activation(out=xb[:], in_=xap, func=AF.Copy), reads=xres, writes=[xbres])
        for hb in range(2):
            def tr(e, hb=hb):
                for c in range(8):
                    cc = hb * 8 + c
                    ins = e.transpose(out=pT[:, hb, c * P:(c + 1) * P], in_=xb[:, cc * P:(cc + 1) * P], identity=self.ident[:])
                return ins
            S.op('pe', tr, reads=[xbres, 'ident'], writes=[pTres + str(hb)])
            eng = 'dve' if hb == 0 else 'act'
            if eng == 'dve':
                S.op('dve', lambda e, hb=hb: e.tensor_copy(out=hto[:, hb * 8:(hb + 1) * 8, blk * P:(blk + 1) * P],
                                                           in_=pT[:, hb, :].rearrange("p (c t) -> p c t", t=P)),
                     reads=[pTres + str(hb)], writes=[htres + 'b%d_%d' % (blk, hb)])
            else:
                S.op('act', lambda e, hb=hb: e.activation(out=hto[:, hb * 8:(hb + 1) * 8, blk * P:(blk + 1) * P],
                                                          in_=pT[:, hb, :].rearrange("p (c t) -> p c t", t=P), func=AF.Copy),
                     reads=[pTres + str(hb)], writes=[htres + 'b%d_%d' % (blk, hb)])

    def load_gb(self, G, Bt, gsrc, bsrc, gres):
        S = self.S
        S.dma('sp', lambda e: e.dma_start(out=G[:], in_=gsrc.partition_broadcast(P)), writes=[gres])
        S.dma('sp', lambda e: e.dma_start(out=Bt[:], in_=bsrc.partition_broadcast(P)), writes=[gres])

    def phase_ln_in(self):
        S = self.S
        with contextlib.ExitStack() as es:
            G = self.tile(es, "G", [P, D])
            Bt = self.tile(es, "Bt", [P, D])
            X = [self.tile(es, "X%d" % i, [P, 4, D]) for i in range(2)]
            XB = [self.tile(es, "XB%d" % i, [P, D], BF16) for i in range(2)]
            HTO = [self.tile(es, "HTO%d" % i, [P, KC, TT], BF16) for i in range(2)]
            SC = [self.tile(es, "SC%d" % i, [P, 32]) for i in range(2)]
            pT = self.psum(es, "pT", [P, 2, 1024], BF16)
            self.load_gb(G, Bt, self.ln_in_g, self.ln_in_b, 'GB')
            nt = self.T // TT
            for i in range(nt):
                x, xr = X[i % 2], "X%d" % (i % 2)
                hto, hr = HTO[i % 2], "HTO%d" % (i % 2)
                S.dma('sp', lambda e, i=i, x=x: e.dma_start(out=x[:], in_=self.x[i * TT:(i + 1) * TT, :].rearrange("(b p) d -> p b d", p=P)),
                      writes=[xr + 'b%d' % b for b in range(4)])
                for b in range(4):
                    k = (i * 4 + b) % 2
                    self.ln_block(x[:, b, :], [xr + 'b%d' % b], G, Bt, 'GB', SC[k], 'SC%d' % k)
                    self.transpose_block(x[:, b, :], [xr + 'b%d' % b], XB[k], 'XB%d' % k, pT, 'pT', hto, b, hr)
                S.dma('sp', lambda e, i=i, x=x: e.dma_start(out=self.H[i * TT:(i + 1) * TT, :].rearrange("(b p) d -> p b d", p=P), in_=x[:]),
                      reads=[xr + 'b%d' % b for b in range(4)])
                S.dma('sp', lambda e, i=i, hto=hto: e.dma_start(out=self.HT[0][:, :, i * TT:(i + 1) * TT], in_=hto[:]),
                      reads=[hr + 'b%d_%d' % (b, hb) for b in range(4) for hb in range(2)])
            S.flush("ln_in")

    def phase_preconvert(self):
        S = self.S
        for l in range(self.depth):
            for cb in range(IN_W // P):
                S.dma('pool', lambda e, l=l, cb=cb: e.dma_start(out=self.WB_in[l, cb], in_=self.w_in[l][:, cb * P:(cb + 1) * P].rearrange("(c p) n -> p c n", p=P)))
            for cb in range(2 * FC):
                S.dma('pool', lambda e, l=l, cb=cb: e.dma_start(out=self.WB_fi[l, cb], in_=self.w_ffn_in[l][:, cb * P:(cb + 1) * P].rearrange("(c p) n -> p c n", p=P)))
            for cb in range(KC):
                S.dma('pool', lambda e, l=l, cb=cb: e.dma_start(out=self.WB_a[l, cb], in_=self.w_branch_a[l][:, cb * P:(cb + 1) * P].rearrange("(c p) n -> p c n", p=P)))
                S.dma('pool', lambda e, l=l, cb=cb: e.dma_start(out=self.WB_b[l, cb], in_=self.w_branch_b[l][:, cb * P:(cb + 1) * P].rearrange("(c p) n -> p c n", p=P)))
            for k in range(KC):
                S.dma('pool', lambda e, l=l, k=k: e.dma_start(out=self.WB_o[l][k * P:(k + 1) * P, :], in_=self.w_out[l][k * P:(k + 1) * P, :]))
            for k in range(FC):
                S.dma('pool', lambda e, l=l, k=k: e.dma_start(out=self.WB_fo[l][k * P:(k + 1) * P, :], in_=self.w_ffn_out[l][k * P:(k + 1) * P, :]))

    def wblk(self, dst_ap, src_ap, wres, q='pool'):
        self.S.dma(q, lambda e: e.dma_start(out=dst_ap, in_=src_ap), writes=[wres])

    def phase_sweep(self, l, hin, t0, nu, u, d):
        S = self.S
        tu = t0 + u * UT
        first_unit = (u == 0) if d == 0 else (u == nu - 1)
        nproj = 4 if d == 0 else 3
        with contextlib.ExitStack() as es:
            HTs = self.tile(es, "HTs", [P, KC, UT], BF16)
            WR = Ring(es, self.nc, "W", [P, nproj, KC, P], BF16, 2)
            QS = self.tile(es, "QS", [P, UT])
            SG = self.tile(es, "SG", [P, UT])
            GG = self.tile(es, "GG", [P, UT])
            BB = self.tile(es, "BB", [P, UT])
            EE = self.tile(es, "EE", [P, UT])
            SMK = self.tile(es, "SMK", [P, UT])
            QT = self.tile(es, "QT", [P, UT], BF16)
            KT = self.tile(es, "KT", [P, UT], BF16)
            VT = self.tile(es, "VT", [P, 16, P], BF16)
            KTK = self.tile(es, "KTK", [P, 16, P], BF16)
            OT = self.tile(es, "OT", [P, UT])
            SML = self.tile(es, "SML", [P, 4, NCHK])
            SMS = [self.tile(es, "SMS%d" % i, [P, CH], BF16) for i in range(2)]
            STL = [self.tile(es, "STL%d" % i, [P, P], BF16) for i in range(2)]
            U1 = self.tile(es, "U1", [P, P])
            if d == 0:
                HG = self.tile(es, "HG", [P, UT])
                AT = self.tile(es, "AT", [P, UT], BF16)
            ps = self.psum(es, "ps", [P, 6, 512], F32)
            pT = self.psum(es, "pTs", [P, 2, 1024], BF16)
            S.op('pool', lambda e: e.memset(SMK[:], 1.0), writes=['SMK'])
            S.op('pool', lambda e: e.memset(SMK[:].rearrange("p (c t) -> p c t", t=CH)[:, :, 0:1], 0.0), writes=['SMK'])
            S.dma('sp', lambda e: e.dma_start(out=HTs[:], in_=hin[:, :, tu:tu + UT]), writes=['HTs'])
            if first_unit:
                S.op('pool', lambda e: e.memset(self.U[d][:], 0.0), writes=['U'])
                S.op('pool', lambda e: e.memset(self.Est[d][:], 0.0), writes=['Est'])
            offs = [OFF_Q, OFF_FF if d == 0 else OFF_FB, OFF_I] + ([OFF_G] if d == 0 else [])
            wl = self.w_in[l]

            def load_w(j):
                wt, wn = WR.next()
                for pi, off in enumerate(offs):
                    self.wblk(wt[:, pi, :, :], self.WB_in[l, off // P + j], wn + '_%d' % pi)
                return wt, wn
            nxt = load_w(0)
            mask = self.hmf if d == 0 else self.hmb
            mres = 'hmf' if d == 0 else 'hmb'
            lc = (d * 8)
            for j in range(8):
                wt, wn = nxt
                if j + 1 < 8:
                    nxt = load_w(j + 1)
                lbc = lc + j
                def emit_proj(which):
                  for tt in range(4):
                      tsl = slice(tt * TT, (tt + 1) * TT)
                      for pi, dst in which:
                          bank = (tt * 3 + (pi if pi < 2 else 2)) % 4
                          def mm(e, pi=pi, bank=bank, tsl=tsl, wt=wt):
                              for k in range(KC):
                                  ins = e.matmul(ps[:, bank, :], wt[:, pi, k, :], HTs[:, k, tsl], start=(k == 0), stop=(k == KC - 1))
                              return ins
                          S.op('pe', mm, reads=[wn + '_%d' % pi, 'HTs'], writes=['ps%d' % bank])
                          if dst == 'q':
                              S.op('act', lambda e, bank=bank, tsl=tsl: e.activation(out=QS[:, tsl], in_=ps[:, bank, :], func=AF.Silu),
                                   reads=['ps%d' % bank], writes=['QS%d' % tt])
                          elif dst == 'f':
                              S.op('act', lambda e, bank=bank, tsl=tsl: e.activation(out=SG[:, tsl], in_=ps[:, bank, :], func=AF.Sigmoid),
                                   reads=['ps%d' % bank], writes=['SG%d' % tt])
                          else:
                              S.op('act', lambda e, bank=bank, tsl=tsl: e.activation(out=HG[:, tsl], in_=ps[:, bank, :], func=AF.Silu),
                                   reads=['ps%d' % bank], writes=['HG%d' % tt])
                emit_proj(((1, 'f'),))
                allq = ['QS%d' % i for i in range(4)]
                allsg = ['SG%d' % i for i in range(4)]
                S.op('act', lambda e, lbc=lbc: e.activation(out=GG[:], in_=SG[:], func=AF.Ln, scale=self.OML[:, l, lbc:lbc + 1],
                                                            bias=self.LB[:, l, lbc:lbc + 1]),
                     reads=allsg + ['OML', 'LB'], writes=['GG'])
                S.op('dve', lambda e, lbc=lbc: e.tensor_scalar(out=SG[:], in0=SG[:], scalar1=-1.0, scalar2=self.LBM1[:, l, lbc:lbc + 1],
                                                               op0=ALU.add, op1=ALU.mult),
                     reads=allsg + ['LBM1', 'GG'], writes=allsg)
                S.op('dve', lambda e: e.tensor_tensor_scan(out=BB[:], data0=SMK[:], data1=GG[:], initial=0.0, op0=ALU.mult, op1=ALU.add),
                     reads=['SMK', 'GG'], writes=['BB'])
                b3 = BB[:].rearrange("p (c t) -> p c t", t=CH)
                if d == 1:
                    S.op('dve', lambda e: e.tensor_tensor(out=EE[:].rearrange("p (c t) -> p c t", t=CH), in0=b3[:, :, CH - 1:CH].broadcast_to([P, NCHK, CH]),
                                                          in1=b3, op=ALU.subtract), reads=['BB'], writes=['EE'])
                    S.op('dve', lambda e: e.tensor_tensor(out=BB[:], in0=EE[:], in1=GG[:], op=ALU.add), reads=['EE', 'GG'], writes=['BB'])
                    iend, imid = 0, CH // 2
                else:
                    iend, imid = CH - 1, CH // 2 - 1
                ev, mv, esh, cex = SML[:, 0, :], SML[:, 1, :], SML[:, 2, :], SML[:, 3, :]
                S.op('dve', lambda e: e.tensor_copy(out=mv, in_=b3[:, :, imid]), reads=['BB'], writes=['mv'])
                S.op('dve', lambda e: e.tensor_tensor(out=ev, in0=b3[:, :, iend], in1=b3[:, :, imid], op=ALU.subtract), reads=['BB'], writes=['ev'])
                if d == 0:
                    S.op('dve', lambda e: e.tensor_copy(out=SML[:, 2, 1:NCHK], in_=SML[:, 0, 0:NCHK - 1]), reads=['ev'], writes=['esh'])
                    S.op('dve', lambda e, j=j: e.tensor_copy(out=SML[:, 2, 0:1], in_=self.Est[d][:, j:j + 1]), reads=['Est'], writes=['esh0'])
                    S.op('dve', lambda e, j=j: e.tensor_copy(out=self.Est[d][:, j:j + 1], in_=SML[:, 0, NCHK - 1:NCHK]), reads=['ev', 'esh0'], writes=['Est'])
                else:
                    S.op('dve', lambda e: e.tensor_copy(out=SML[:, 2, 0:NCHK - 1], in_=SML[:, 0, 1:NCHK]), reads=['ev'], writes=['esh'])
                    S.op('dve', lambda e, j=j: e.tensor_copy(out=SML[:, 2, NCHK - 1:NCHK], in_=self.Est[d][:, j:j + 1]), reads=['Est'], writes=['esh0'])
                    S.op('dve', lambda e, j=j: e.tensor_copy(out=self.Est[d][:, j:j + 1], in_=SML[:, 0, 0:1]), reads=['ev', 'esh0'], writes=['Est'])
                S.op('dve', lambda e: e.tensor_tensor(out=cex, in0=esh, in1=mv, op=ALU.add), reads=['esh', 'esh0', 'mv'], writes=['cex'])
                S.op('act', lambda e: e.activation(out=cex, in_=cex, func=AF.Exp), reads=['cex'], writes=['cex'])
                S.op('dve', lambda e: e.tensor_tensor(out=b3, in0=b3, in1=SML[:, 1, :].unsqueeze(2).broadcast_to([P, NCHK, CH]), op=ALU.subtract),
                     reads=['BB', 'mv'], writes=['BB'])
                S.op('act', lambda e: e.activation(out=EE[:], in_=BB[:], func=AF.Exp), reads=['BB'], writes=['EE'])
                emit_proj(((0, 'q'),) + (((3, 'g'),) if d == 0 else ()))
                for g4 in range(4):
                    bank = 4 + (g4 % 2)
                    def mmv(e, g4=g4, bank=bank, wt=wt):
                        for bl in range(4):
                            blk = g4 * 4 + bl
                            for k in range(KC):
                                ins = e.matmul(ps[:, bank, bl * P:(bl + 1) * P], HTs[:, k, blk * P:(blk + 1) * P], wt[:, 2, k, :],
                                               start=(k == 0), stop=(k == KC - 1))
                        return ins
                    S.op('pe', mmv, reads=[wn + '_2', 'HTs'], writes=['ps%d' % bank])
                    S.op('dve', lambda e, g4=g4, bank=bank: e.tensor_copy(out=VT[:, g4 * 4:(g4 + 1) * 4, :],
                                                                            in_=ps[:, bank, :].rearrange("p (b v) -> p b v", v=P)),
                         reads=['ps%d' % bank], writes=['VT%d' % g4])
                S.op('dve', lambda e: e.tensor_tensor(out=QT[:], in0=QS[:], in1=EE[:], op=ALU.mult), reads=allq + ['EE'], writes=['QT'])
                S.op('act', lambda e: e.activation(out=EE[:], in_=BB[:], func=AF.Exp, scale=-1.0), reads=['BB', 'QT'], writes=['EE'])
                S.op('dve', lambda e: e.tensor_tensor(out=KT[:], in0=SG[:], in1=EE[:], op=ALU.mult), reads=allsg + ['EE'], writes=['KT'])
                for hb in range(2):
                    def trk(e, hb=hb):
                        for c in range(8):
                            blk = hb * 8 + c
                            ins = e.transpose(out=pT[:, hb, c * P:(c + 1) * P], in_=KT[:, blk * P:(blk + 1) * P], identity=self.ident[:])
                        return ins
                    S.op('pe', trk, reads=['KT', 'ident'], writes=['pTs%d' % hb])
                    S.op('act', lambda e, hb=hb: e.activation(out=KTK[:, hb * 8:(hb + 1) * 8, :], in_=pT[:, hb, :].rearrange("p (c t) -> p c t", t=P),
                                                              func=AF.Copy), reads=['pTs%d' % hb], writes=['KTK%d' % hb])
                order = list(range(NCHK)) if d == 0 else list(reversed(range(NCHK)))
                Uj = self.U[d][:, j, :]
                def chunk_idx(n_i):
                    i = order[n_i]
                    return i, slice(i * CH, (i + 1) * CH), i // 2, slice((i % 2) * CH, (i % 2 + 1) * CH), n_i % 2

                def emit_front(n_i):
                    i, csl, blk, pr, sb = chunk_idx(n_i)
                    S.op('pe', lambda e, sb=sb, csl=csl: e.matmul(ps[0:CH, sb, 0:CH], KT[:, csl], QT[:, csl], start=True, stop=True),
                         reads=['KT', 'QT'], writes=['ps%d' % sb])
                    S.op('pe', lambda e, sb=sb, blk=blk, pr=pr: e.matmul(ps[:, 2 + sb, 0:P], KTK[pr, blk, :], VT[pr, blk, :], start=True, stop=True),
                         reads=['KTK%d' % (blk // 8), 'VT%d' % (blk // 4)], writes=['ps%d' % (2 + sb)])
                emit_front(0)
                for n_i in range(NCHK):
                    i, csl, blk, pr, sb = chunk_idx(n_i)
                    grp = i // 8
                    ob = 4 + (grp % 2)
                    oc = slice((i % 8) * CH, (i % 8 + 1) * CH)
                    sms, smn = SMS[sb], 'SMS%d' % sb
                    stl, stn = STL[sb], 'STL%d' % sb
                    S.op('dve', lambda e, sb=sb, pr=pr, sms=sms: e.tensor_tensor(out=sms[pr, :], in0=ps[0:CH, sb, 0:CH], in1=mask[:], op=ALU.mult),
                         reads=['ps%d' % sb, mres], writes=[smn])
                    Ur, Urn = (Uj, 'U') if n_i % 2 == 0 else (U1[:], 'U1')
                    Uw, Uwn = (U1[:], 'U1') if n_i % 2 == 0 else (Uj, 'U')
                    S.op('act', lambda e, i=i, stl=stl, Ur=Ur: e.activation(out=stl[:], in_=Ur, func=AF.Copy, scale=SML[:, 3, i:i + 1]),
                         reads=[Urn, 'cex'], writes=[stn])
                    S.op('dve', lambda e, i=i, sb=sb, Ur=Ur, Uw=Uw: e.scalar_tensor_tensor(out=Uw, in0=Ur, scalar=SML[:, 3, i:i + 1], in1=ps[:, 2 + sb, 0:P],
                                                                                       op0=ALU.mult, op1=ALU.add),
                         reads=[Urn, 'cex', 'ps%d' % (2 + sb)], writes=[Uwn])
                    if n_i + 1 < NCHK:
                        emit_front(n_i + 1)

                    def mmo(e, ob=ob, oc=oc, csl=csl, blk=blk, pr=pr, stl=stl, sms=sms):
                        e.matmul(ps[:, ob, oc], stl[:], QT[:, csl], start=True, stop=False)
                        return e.matmul(ps[:, ob, oc], VT[pr, blk, :], sms[pr, :], start=False, stop=True)
                    S.op('pe', mmo, reads=[stn, smn, 'QT', 'VT%d' % (blk // 4)], writes=['ps%d' % ob])
                    if n_i % 8 == 7:
                        S.op('act', lambda e, ob=ob, grp=grp: e.activation(out=OT[:, grp * 512:(grp + 1) * 512], in_=ps[:, ob, :], func=AF.Copy),
                             reads=['ps%d' % ob], writes=['OT%d' % grp])
                if d == 1:
                    S.dma('sp', lambda e, j=j: e.dma_start(out=self.OB[j, :, u * UT:(u + 1) * UT], in_=OT[:]),
                          reads=['OT%d' % g_ for g_ in range(4)])
                else:
                    allo = ['OT%d' % g_ for g_ in range(4)]
                    S.dma('sp', lambda e, j=j: e.dma_start(out=EE[:], in_=self.OB[j, :, u * UT:(u + 1) * UT]), writes=['EE'])
                    S.op('dve', lambda e: e.tensor_tensor(out=OT[:], in0=OT[:], in1=EE[:], op=ALU.add), reads=allo + ['EE'], writes=allo)
                    S.op('act', lambda e: e.activation(out=GG[:], in_=OT[:], func=AF.Square), reads=allo, writes=['GG'])
                    for tt in range(4):
                        tsl = slice(tt * TT, (tt + 1) * TT)
                        bank = tt % 2
                        S.op('pe', lambda e, bank=bank, tsl=tsl: e.matmul(ps[:, bank, :], self.ones_f[:], GG[:, tsl], start=True, stop=True),
                             reads=['ones_f', 'GG'], writes=['ps%d' % bank])
                        S.op('act', lambda e, bank=bank, tsl=tsl: e.activation(out=EE[:, tsl], in_=ps[:, bank, :], func=AF.Sqrt, scale=1.0 / P,
                                                                               bias=self.eps_rms[:, 0:1]),
                             reads=['ps%d' % bank, 'eps_rms'], writes=['EE'])
                    S.op('dve', lambda e: e.reciprocal(out=EE[:], in_=EE[:]), reads=['EE'], writes=['EE'])
                    S.op('dve', lambda e: e.scalar_tensor_tensor(out=OT[:], in0=OT[:], scalar=self.NG[:, l:l + 1], in1=EE[:], op0=ALU.mult, op1=ALU.mult),
                         reads=allo + ['EE', 'NG'], writes=allo)
                    S.op('dve', lambda e: e.tensor_tensor(out=AT[:], in0=OT[:], in1=HG[:], op=ALU.mult),
                         reads=allo + ['HG%d' % i for i in range(4)], writes=['AT'])
                    S.dma('sp', lambda e, j=j: e.dma_start(out=self.AB[:, j, :], in_=AT[:]), reads=['AT'])
            S.flush("sweep")

    def phase_attn(self, l, hin, t0, nu, u):
        S = self.S
        tu = t0 + u * UT
        has_prev = u > 0
        has_next = u < nu - 1
        NK = UT + 2 * P
        pos0 = u * UT - P
        with contextlib.ExitStack() as es:
            HTs = self.tile(es, "HTa", [P, KC, NK], BF16)
            COS = self.tile(es, "COS", [P, NK])
            SIN = self.tile(es, "SIN", [P, NK])
            WQ = Ring(es, self.nc, "WQ", [P, 4, KC, P], BF16, 1)
            WKV = self.tile(es, "WKV", [P, 4, KC, P], BF16)
            KR = self.tile(es, "KR", [P, 2, NK], BF16)
            VTt = self.tile(es, "VTt", [P, 18, 2 * P], BF16)
            QR = self.tile(es, "QR", [P, 4, UT], BF16)
            T1 = [self.tile(es, "T1%d" % i, [P, TT]) for i in range(2)]
            T2 = [self.tile(es, "T2%d" % i, [P, TT]) for i in range(2)]
            PT = [self.tile(es, "PT%d" % i, [P, 3, 512], BF16) for i in range(2)]
            RD = self.tile(es, "RD", [P, 512])
            BT = self.tile(es, "BT", [P, 4, UT], BF16)
            ps = self.psum(es, "psa", [P, 8, 512], F32)
            lo = 0 if has_prev else P
            hi = NK if has_next else NK - P
            S.dma('sp', lambda e: e.dma_start(out=HTs[:, :, lo:hi], in_=hin[:, :, tu - P + lo:tu - P + hi]), writes=['HTa'])
            S.dma('sp', lambda e: e.dma_start(out=COS[:, lo:hi], in_=self.c_cos[:, pos0 + lo:pos0 + hi]), writes=['COS'])
            S.dma('sp', lambda e: e.dma_start(out=SIN[:, lo:hi], in_=self.c_sin[:, pos0 + lo:pos0 + hi]), writes=['SIN'])
            wl = self.w_in[l]
            for i4 in range(4):
                self.wblk(WKV[:, i4, :, :], self.WB_in[l, OFF_AK // P + i4], 'WKV%d' % i4)
            rope_i = [0]

            def rope(bank, ncol, cs, out_ap, outres):
                k = rope_i[0] % 2
                rope_i[0] += 1
                t1, t2 = T1[k], T2[k]
                S.op('dve', lambda e: e.tensor_tensor(out=t1[:, 0:ncol], in0=ps[:, bank, 0:ncol], in1=COS[:, cs], op=ALU.mult),
                     reads=['ps%d' % bank, 'COS'], writes=['T1%d' % k])
                S.op('dve', lambda e: e.tensor_tensor(out=t2[0:64, 0:ncol], in0=ps[64:128, bank, 0:ncol], in1=SIN[64:128, cs], op=ALU.mult),
                     reads=['ps%d' % bank, 'SIN'], writes=['T2a%d' % k])
                S.op('dve', lambda e: e.tensor_tensor(out=t2[64:128, 0:ncol], in0=ps[0:64, bank, 0:ncol], in1=SIN[0:64, cs], op=ALU.mult),
                     reads=['ps%d' % bank, 'SIN'], writes=['T2b%d' % k])
                S.op('dve', lambda e: e.tensor_tensor(out=out_ap, in0=t1[:, 0:ncol], in1=t2[:, 0:ncol], op=ALU.add),
                     reads=['T1%d' % k, 'T2a%d' % k, 'T2b%d' % k], writes=[outres])
            segs = []
            if has_prev:
                segs.append((0, P))
            for tt in range(4):
                segs.append((P + tt * TT, TT))
            if has_next:
                segs.append((P + UT, P))
            bi = 0
            for kvh in range(2):
                for (c0, nc_) in segs:
                    bank = bi % 2
                    bi += 1
                    def mmk(e, kvh=kvh, c0=c0, nc_=nc_, bank=bank):
                        for k in range(KC):
                            ins = e.matmul(ps[:, bank, 0:nc_], WKV[:, kvh, k, :], HTs[:, k, c0:c0 + nc_], start=(k == 0), stop=(k == KC - 1))
                        return ins
                    S.op('pe', mmk, reads=['WKV%d' % kvh, 'HTa'], writes=['ps%d' % bank])
                    rope(bank, nc_, slice(c0, c0 + nc_), KR[:, kvh, c0:c0 + nc_], 'KR')
            blks = list(range(0 if has_prev else 1, 18 if has_next else 17))
            for gi in range(0, len(blks), 2):
                grp = blks[gi:gi + 2]
                bank = 2 + (gi // 2) % 2
                def mmv(e, grp=grp, bank=bank):
                    for bl, blk in enumerate(grp):
                        for k in range(KC):
                            ins = e.matmul(ps[:, bank, bl * 2 * P:(bl + 1) * 2 * P], HTs[:, k, blk * P:(blk + 1) * P], WKV[:, 2:4, k, :],
                                           start=(k == 0), stop=(k == KC - 1))
                    return ins
                S.op('pe', mmv, reads=['WKV2', 'WKV3', 'HTa'], writes=['ps%d' % bank])
                S.op('act', lambda e, grp=grp, bank=bank: e.activation(out=VTt[:, grp[0]:grp[0] + len(grp), :],
                                                                       in_=ps[:, bank, 0:len(grp) * 2 * P].rearrange("p (b v) -> p b v", v=2 * P), func=AF.Copy),
                     reads=['ps%d' % bank], writes=['VTt'])
            nxt = WQ.next()
            for i4 in range(4):
                self.wblk(nxt[0][:, i4, :, :], self.WB_in[l, OFF_AQ // P + i4], nxt[1] + '_%d' % i4)
            for kvh in range(2):
                wq, wqn = nxt
                for g in range(4):
                    for tt in range(4):
                        bank = bi % 2
                        bi += 1
                        def mmq(e, g=g, tt=tt, bank=bank, wq=wq):
                            for k in range(KC):
                                ins = e.matmul(ps[:, bank, :], wq[:, g, k, :], HTs[:, k, P + tt * TT:P + (tt + 1) * TT],
                                               start=(k == 0), stop=(k == KC - 1))
                            return ins
                        S.op('pe', mmq, reads=[wqn + '_%d' % g, 'HTa'], writes=['ps%d' % bank])
                        rope(bank, TT, slice(P + tt * TT, P + (tt + 1) * TT), QR[:, g, tt * TT:(tt + 1) * TT], 'QR')
                if kvh == 0:
                    nxt = WQ.next()
                    for i4 in range(4):
                        self.wblk(nxt[0][:, i4, :, :], self.WB_in[l, OFF_AQ // P + 4 + i4], nxt[1] + '_%d' % i4)
                for n in range(16):
                    kbs = []
                    if n > 0 or has_prev:
                        kbs.append((n, self.amp, 'amp'))
                    kbs.append((n + 1, None, None))
                    if n < 15 or has_next:
                        kbs.append((n + 2, self.amn, 'amn'))
                    pb = (n % 2) * 3
                    pt, ptn = PT[n % 2], 'PT%d' % (n % 2)
                    qap = QR[:, :, n * P:(n + 1) * P]
                    for ki, (kb, am, amres) in enumerate(kbs):
                        def mms(e, ki=ki, kb=kb, am=am, pb=pb, kvh=kvh, qap=qap):
                            ins = e.matmul(ps[:, pb + ki, :], KR[:, kvh, kb * P:(kb + 1) * P], qap, start=True, stop=(am is None))
                            if am is not None:
                                ins = e.matmul(ps[:, pb + ki, :], self.ident[:], am[:].unsqueeze(1).broadcast_to([P, 4, P]), start=False, stop=True)
                            return ins
                        S.op('pe', mms, reads=['KR', 'QR', 'ident'] + ([amres] if am is not None else []), writes=['ps%d' % (pb + ki)])
                        S.op('act', lambda e, ki=ki, pb=pb, pt=pt: e.activation(out=pt[:, ki, :], in_=ps[:, pb + ki, :], func=AF.Exp, scale=QSCALE),
                             reads=['ps%d' % (pb + ki)], writes=[ptn + '_%d' % ki])
                    nk = len(kbs)

                    def mmd(e, nk=nk, pt=pt):
                        for ki in range(nk):
                            ins = e.matmul(ps[:, 6, :], self.ones_bf[:], pt[:, ki, :], start=(ki == 0), stop=(ki == nk - 1))
                        return ins
                    S.op('pe', mmd, reads=['ones_bf'] + [ptn + '_%d' % ki for ki in range(nk)], writes=['ps6'])

                    def mmpv(e, kbs=kbs, pt=pt, kvh=kvh):
                        for ki, (kb, _, _) in enumerate(kbs):
                            ins = e.matmul(ps[:, 7, :], VTt[:, kb, kvh * P:(kvh + 1) * P], pt[:, ki, :], start=(ki == 0), stop=(ki == len(kbs) - 1))
                        return ins
                    S.op('pe', mmpv, reads=['VTt'] + [ptn + '_%d' % ki for ki in range(nk)], writes=['ps7'])
                    es_ap = self.ES[:, l * 8 + kvh * 4:l * 8 + kvh * 4 + 4].unsqueeze(2).broadcast_to([P, 4, P])
                    S.op('dve', lambda e, es_ap=es_ap: e.tensor_tensor(out=RD[:].rearrange("p (g q) -> p g q", q=P), in0=ps[:, 6, :].rearrange("p (g q) -> p g q", q=P),
                                                                       in1=es_ap, op=ALU.add), reads=['ps6', 'ES'], writes=['RD'])
                    S.op('dve', lambda e: e.reciprocal(out=RD[:], in_=RD[:]), reads=['RD'], writes=['RD'])
                    S.op('dve', lambda e, n=n: e.tensor_tensor(out=BT[:, :, n * P:(n + 1) * P], in0=ps[:, 7, :].rearrange("p (g q) -> p g q", q=P),
                                                               in1=RD[:].rearrange("p (g q) -> p g q", q=P), op=ALU.mult),
                         reads=['ps7', 'RD'], writes=['BT'])
                S.dma('sp', lambda e, kvh=kvh: e.dma_start(out=self.AB[:, 8 + kvh * 4:8 + kvh * 4 + 4, :], in_=BT[:]), reads=['BT'])
            S.flush("attn")

    def phase_tail(self, l, hin, hout, t0, u, last):
        S = self.S
        tu = t0 + u * UT
        with contextlib.ExitStack() as es:
            HTt = self.tile(es, "HTt", [P, KC, TT], BF16)
            ABt = self.tile(es, "ABt", [P, KC, TT], BF16)
            MT = self.tile(es, "MT", [P, KC, TT], BF16)
            X = self.tile(es, "Xt", [P, 4, D])
            RES = Ring(es, self.nc, "RES", [P, 1024], F32, 2)
            ACT_T = self.tile(es, "ACTT", [P, FC, TT], BF16)
            G = self.tile(es, "Gt", [P, D])
            Bt = self.tile(es, "Btt", [P, D])
            WG = Ring(es, self.nc, "WG", [P, 2, KC, P], BF16, 2)
            WAB = Ring(es, self.nc, "WAB", [P, 2, 8, P], BF16, 2)
            WK = Ring(es, self.nc, "WK", [P, 1024], BF16, 4)
            SGA = [self.tile(es, "SGA%d" % i, [P, TT]) for i in range(2)]
            SGB = [self.tile(es, "SGB%d" % i, [P, TT]) for i in range(2)]
            XB = [self.tile(es, "XBt%d" % i, [P, D], BF16) for i in range(2)]
            SC = [self.tile(es, "SCt%d" % i, [P, 32]) for i in range(2)]
            ps = self.psum(es, "pst", [P, 8, 512], F32)
            wl = self.w_in[l]
            allmt = ['MT%d' % c for c in range(KC)]

            def stage_merge(tt):
                tk = tu + tt * TT
                S.dma('sp', lambda e, tk=tk: e.dma_start(out=HTt[:], in_=hin[:, :, tk:tk + TT]), writes=['HTt'])
                S.dma('sp', lambda e, tt=tt: e.dma_start(out=ABt[:], in_=self.AB[:, :, tt * TT:(tt + 1) * TT]), writes=['ABt'])

                def load_merge(c):
                    wg, wgn = WG.next()
                    self.wblk(wg[:, 0, :, :], self.WB_in[l, OFF_GA // P + c], wgn + '_0')
                    self.wblk(wg[:, 1, :, :], self.WB_in[l, OFF_GB // P + c], wgn + '_1')
                    wab, wabn = WAB.next()
                    self.wblk(wab[:, 0, :, :], self.WB_a[l, c], wabn + '_0')
                    self.wblk(wab[:, 1, :, :], self.WB_b[l, c], wabn + '_1')
                    return wg, wgn, wab, wabn
                nxt = load_merge(0)
                for c in range(KC):
                    wg, wgn, wab, wabn = nxt
                    if c + 1 < KC:
                        nxt = load_merge(c + 1)
                    pb = (c % 2) * 4
                    for gi in range(2):
                        def mmg(e, gi=gi, pb=pb, wg=wg):
                            for k in range(KC):
                                ins = e.matmul(ps[:, pb + gi, :], wg[:, gi, k, :], HTt[:, k, :], start=(k == 0), stop=(k == KC - 1))
                            return ins
                        S.op('pe', mmg, reads=[wgn + '_%d' % gi, 'HTt'], writes=['ps%d' % (pb + gi)])

                        def mmb(e, gi=gi, pb=pb, wab=wab):
                            for k in range(8):
                                ins = e.matmul(ps[:, pb + 2 + gi, :], wab[:, gi, k, :], ABt[:, gi * 8 + k, :], start=(k == 0), stop=(k == 7))
                            return ins
                        S.op('pe', mmb, reads=[wabn + '_%d' % gi, 'ABt'], writes=['ps%d' % (pb + 2 + gi)])
                    k2 = c % 2
                    sga, sgb = SGA[k2], SGB[k2]
                    S.op('act', lambda e, pb=pb, sga=sga: e.activation(out=sga[:], in_=ps[:, pb, :], func=AF.Sigmoid),
                         reads=['ps%d' % pb], writes=['SGA%d' % k2])
                    S.op('act', lambda e, pb=pb, sgb=sgb: e.activation(out=sgb[:], in_=ps[:, pb + 1, :], func=AF.Sigmoid),
                         reads=['ps%d' % (pb + 1)], writes=['SGB%d' % k2])
                    S.op('dve', lambda e, pb=pb, sga=sga: e.tensor_tensor(out=sga[:], in0=sga[:], in1=ps[:, pb + 2, :], op=ALU.mult),
                         reads=['SGA%d' % k2, 'ps%d' % (pb + 2)], writes=['SGA%d' % k2])
                    S.op('dve', lambda e, pb=pb, sgb=sgb: e.tensor_tensor(out=sgb[:], in0=sgb[:], in1=ps[:, pb + 3, :], op=ALU.mult),
                         reads=['SGB%d' % k2, 'ps%d' % (pb + 3)], writes=['SGB%d' % k2])
                    S.op('dve', lambda e, c=c, sga=sga, sgb=sgb: e.tensor_tensor(out=MT[:, c, :], in0=sga[:], in1=sgb[:], op=ALU.add),
                         reads=['SGA%d' % k2, 'SGB%d' % k2], writes=['MT%d' % c])


            stage_merge(0)
            for tt in range(4):
                tk = tu + tt * TT
                self.load_gb(G, Bt, self.ln1_g[l], self.ln1_b[l], 'GBt')
                def tm_proj(wsrc, nk, lhs_tile, lhs_res, res_from_dram):
                    for half in range(2):
                        hs = slice(half * 1024, (half + 1) * 1024)

                        def load_k(k):
                            wk, wkn = WK.next()
                            S.dma('pool', lambda e, k=k, wk=wk, hs=hs: e.dma_start(out=wk[:], in_=wsrc[k * P:(k + 1) * P, hs]), writes=[wkn])
                            return wk, wkn
                        q = [load_k(0), load_k(1), load_k(2)]
                        for k in range(nk):
                            wk, wkn = q.pop(0)
                            if k + 3 < nk:
                                q.append(load_k(k + 3))

                            def mmt(e, k=k, wk=wk):
                                for blk in range(4):
                                    for nb in range(2):
                                        ins = e.matmul(ps[:, blk * 2 + nb, :], lhs_tile[:, k, blk * P:(blk + 1) * P], wk[:, nb * 512:(nb + 1) * 512],
                                                       start=(k == 0), stop=(k == nk - 1))
                                return ins
                            S.op('pe', mmt, reads=[wkn] + lhs_res, writes=['ps%d' % b for b in range(8)])
                        for blk in range(4):
                            if res_from_dram:
                                rt, rn = RES.next()
                                S.dma('sp', lambda e, blk=blk, rt=rt, hs=hs, tk=tk: e.dma_start(out=rt[:], in_=self.H[tk + blk * P:tk + (blk + 1) * P, hs]), writes=[rn])
                                S.op('dve', lambda e, blk=blk, rt=rt, hs=hs: e.scalar_tensor_tensor(out=X[:, blk, hs], in0=rt[:], scalar=ALPHA,
                                                                                            in1=ps[:, blk * 2:blk * 2 + 2, :].rearrange("p a b -> p (a b)"),
                                                                                            op0=ALU.mult, op1=ALU.add),
                                     reads=[rn, 'ps%d' % (blk * 2), 'ps%d' % (blk * 2 + 1)], writes=['X%d_%d' % (blk, half)])
                            else:
                                S.op('dve', lambda e, blk=blk, hs=hs: e.scalar_tensor_tensor(out=X[:, blk, hs], in0=X[:, blk, hs], scalar=ALPHA,
                                                                                     in1=ps[:, blk * 2:blk * 2 + 2, :].rearrange("p a b -> p (a b)"),
                                                                                     op0=ALU.mult, op1=ALU.add),
                                     reads=['X%d_%d' % (blk, half), 'ps%d' % (blk * 2), 'ps%d' % (blk * 2 + 1)], writes=['X%d_%d' % (blk, half)])
                tm_proj(self.WB_o[l], KC, MT, allmt, True)
                for blk in range(4):
                    k2 = blk % 2
                    xr = ['X%d_0' % blk, 'X%d_1' % blk]
                    self.ln_block(X[:, blk, :], xr, G, Bt, 'GBt', SC[k2], 'SCt%d' % k2)
                    self.tail_transpose(X[:, blk, :], xr, XB[k2], 'XBt%d' % k2, ps, MT, blk, 'H1T', alias='MT%d')
                h1t = ['H1Tb%d_%d' % (b, hb) for b in range(4) for hb in range(2)]
                self.load_gb(G, Bt, self.ln2_g[l], self.ln2_b[l], 'GBt')
                wf = self.w_ffn_in[l]

                def load_ffn(c):
                    wg, wgn = WG.next()
                    self.wblk(wg[:, 0, :, :], self.WB_fi[l, c], wgn + '_0')
                    self.wblk(wg[:, 1, :, :], self.WB_fi[l, FC + c], wgn + '_1')
                    return wg, wgn
                nxt = load_ffn(0)
                for c in range(FC):
                    wg, wgn = nxt
                    if c + 1 < FC:
                        nxt = load_ffn(c + 1)
                    pb = (c % 4) * 2
                    for gi in range(2):
                        def mmf(e, gi=gi, pb=pb, wg=wg):
                            for k in range(KC):
                                ins = e.matmul(ps[:, pb + gi, :], wg[:, gi, k, :], MT[:, k, :], start=(k == 0), stop=(k == KC - 1))
                            return ins
                        S.op('pe', mmf, reads=[wgn + '_%d' % gi] + h1t + allmt, writes=['ps%d' % (pb + gi)])
                    k2 = c % 2
                    sga = SGA[k2]
                    S.op('act', lambda e, pb=pb, sga=sga: e.activation(out=sga[:], in_=ps[:, pb, :], func=AF.Silu),
                         reads=['ps%d' % pb], writes=['SGA%d' % k2])
                    S.op('dve', lambda e, pb=pb, sga=sga, c=c: e.tensor_tensor(out=ACT_T[:, c, :], in0=sga[:], in1=ps[:, pb + 1, :], op=ALU.mult),
                         reads=['SGA%d' % k2, 'ps%d' % (pb + 1)], writes=['ACT%d' % c])
                tm_proj(self.WB_fo[l], FC, ACT_T, ['ACT%d' % c for c in range(FC)], False)
                for blk in range(4):
                    k2 = blk % 2
                    xr = ['X%d_0' % blk, 'X%d_1' % blk]
                    self.ln_block(X[:, blk, :], xr, G, Bt, 'GBt', SC[k2], 'SCt%d' % k2)
                if tt + 1 < 4:
                    stage_merge(tt + 1)
                if not last:
                    for blk in range(4):
                        k2 = blk % 2
                        xr = ['X%d_0' % blk, 'X%d_1' % blk]
                        self.tail_transpose(X[:, blk, :], xr, XB[k2], 'XBt%d' % k2, ps, ACT_T, blk, 'H2T', alias='ACT%d')
                dst = self.y if last else self.H
                S.dma('sp', lambda e, tk=tk: e.dma_start(out=dst[tk:tk + TT, :].rearrange("(b p) d -> p b d", p=P), in_=X[:]),
                      reads=['X%d_%d' % (b, h_) for b in range(4) for h_ in range(2)])
                if not last:
                    S.dma('sp', lambda e, tk=tk: e.dma_start(out=hout[:, :, tk:tk + TT], in_=ACT_T[:, 0:KC, :]),
                          reads=['H2Tb%d_%d' % (b, hb) for b in range(4) for hb in range(2)] + ['ACT%d' % c for c in range(KC)])
            S.flush("tail")

    def tail_transpose(self, xap, xres, xb, xbres, ps, hto, blk, htres, alias=None):
        S = self.S
        S.op('act', lambda e: e.activation(out=xb[:], in_=xap, func=AF.Copy), reads=xres, writes=[xbres])
        for hb in range(2):
            pv = ps[:, hb, :].bitcast(BF16)

            def tr(e, hb=hb, pv=pv):
                for c in range(8):
                    cc = hb * 8 + c
                    ins = e.transpose(out=pv[:, c * P:(c + 1) * P], in_=xb[:, cc * P:(cc + 1) * P], identity=self.ident[:])
                return ins
            S.op('pe', tr, reads=[xbres, 'ident'], writes=['ps%d' % hb])
            if hb == 0:
                S.op('dve', lambda e, hb=hb, pv=pv: e.tensor_copy(out=hto[:, hb * 8:(hb + 1) * 8, blk * P:(blk + 1) * P],
                                                                   in_=pv.rearrange("p (c t) -> p c t", t=P)),
                     reads=['ps%d' % hb], writes=[htres + 'b%d_%d' % (blk, hb)] + ([alias % c for c in range(hb * 8, hb * 8 + 8)] if alias else []))
            else:
                S.op('act', lambda e, hb=hb, pv=pv: e.activation(out=hto[:, hb * 8:(hb + 1) * 8, blk * P:(blk + 1) * P],
                                                                  in_=pv.rearrange("p (c t) -> p c t", t=P), func=AF.Copy),
                     reads=['ps%d' % hb], writes=[htres + 'b%d_%d' % (blk, hb)] + ([alias % c for c in range(hb * 8, hb * 8 + 8)] if alias else []))


def const_tables():
    inv = 1.0 / (10000.0 ** (np.arange(0, 128, 2, dtype=np.float32) / 128.0))
    ang = np.arange(LMAX, dtype=np.float32)[None, :] * inv[:, None].astype(np.float32)
    ang = ang.astype(np.float32)
    cos, sin = np.cos(ang).astype(np.float32), np.sin(ang).astype(np.float32)
    c_cos = np.concatenate([cos, cos], 0)
    c_sin = np.concatenate([sin, -sin], 0)
    s = np.arange(CH)[:, None]
    t = np.arange(CH)[None, :]
    hmf = (s <= t).astype(np.float32)
    hmb = (s >= t).astype(np.float32)
    j = np.arange(P)[:, None]
    i = np.arange(P)[None, :]
    amp = np.where(j >= i, 0.0, NEG).astype(ml_dtypes.bfloat16)
    amn = np.where(j <= i, 0.0, NEG).astype(ml_dtypes.bfloat16)
    return dict(c_cos=np.ascontiguousarray(c_cos), c_sin=np.ascontiguousarray(c_sin), c_ident=np.eye(P).astype(ml_dtypes.bfloat16),
                c_hmf=hmf, c_hmb=hmb, c_amp=amp, c_amn=amn)


_WNAMES = ["ln_in_g", "ln_in_b", "w_in", "lb_logits", "hg_norm_g", "attn_sink", "w_branch_a", "w_branch_b", "w_out",
           "ln1_g", "ln1_b", "w_ffn_in", "w_ffn_out", "ln2_g", "ln2_b"]


def run_cores(xs, weights, seq_units, depth):
    b = Builder(seq_units, depth)
    nc = b.build()
    consts = const_tables()
    in_maps = []
    for x in xs:
        m = {"x": np.ascontiguousarray(x, dtype=np.float32)}
        for k in _WNAMES:
            m[k] = np.ascontiguousarray(np.asarray(weights[k], dtype=np.float32))
        m.update(consts)
        in_maps.append(m)
    res = run_bass_kernel_spmd(nc, in_maps, core_ids=list(range(len(xs))))
    return [r["y"] for r in res.results]


def kernel(x_prompt, x_sample, **weights):
    x_prompt = np.asarray(x_prompt, dtype=np.float32)
    x_sample = np.asarray(x_sample, dtype=np.float32)
    weights = {k: np.asarray(v) for k, v in weights.items()}
    n = 8
    B, Ls, _ = x_prompt.shape
    per = B // n
    xs = []
    for c in range(n):
        xs.append(np.concatenate([x_prompt[c * per:(c + 1) * per].reshape(per * Ls, D), x_sample[0]], axis=0))
    nsu = x_sample.shape[1] // UT
    ys = run_cores(xs, weights, [1] * per + [nsu], DEPTH)
    y_prompt = np.empty_like(x_prompt)
    y_sample = np.empty_like(x_sample)
    seg = x_sample.shape[1] // n
    for c in range(n):
        y_prompt[c * per:(c + 1) * per] = ys[c][:per * Ls].reshape(per, Ls, D)
        y_sample[0, c * seg:(c + 1) * seg] = ys[c][per * Ls + c * seg:per * Ls + (c + 1) * seg]
    return (y_prompt, y_sample)
```
